# Optimizing a Trainium2 kernel written in Bass

```python
import math, functools
import jax, jax.numpy as jnp
from jax import lax
import numpy as np

D_MODEL = 2048
BATCH = 4
SEQ = 2048
DEPTH = 2
DEC_BATCH = 8
DEC_SEQ = 4
PAST_LEN = 16384
PAGE_SIZE = 128

D_MIX = D_MODEL
RWKV_WIDTH = D_MIX // 4
RWKV_HEAD = 64
RWKV_HEADS = RWKV_WIDTH // RWKV_HEAD
DECAY_LORA = 64
ICLR_LORA = 64
GATE_LORA = 128
RWKV_PROJ = 3 * RWKV_WIDTH + DECAY_LORA + ICLR_LORA + GATE_LORA
RWKV_SPLITS = (RWKV_WIDTH, 2 * RWKV_WIDTH, 3 * RWKV_WIDTH, 3 * RWKV_WIDTH + DECAY_LORA, 3 * RWKV_WIDTH + DECAY_LORA + ICLR_LORA)
GN_EPS = 64e-5
LRU_WIDTH = D_MIX // 4
LRU_BLOCKS = 8
LRU_BLOCK = LRU_WIDTH // LRU_BLOCKS
CONV_W = 4
RG_C = 8.0
ATTN_WIDTH = D_MIX - RWKV_WIDTH - LRU_WIDTH
ATTN_HEAD = 128
ATTN_HEADS = ATTN_WIDTH // ATTN_HEAD
DIL_PAIRS = ((128, 1), (512, 4), (2048, 16))
ATTN_WINDOW = 2048
N_BUCKETS = 32
REL_MAX_DIST = ATTN_WINDOW
N_IN = RWKV_PROJ + 2 * LRU_WIDTH + 3 * ATTN_WIDTH
IN_SPLITS = (RWKV_PROJ, RWKV_PROJ + LRU_WIDTH, RWKV_PROJ + 2 * LRU_WIDTH, RWKV_PROJ + 2 * LRU_WIDTH + ATTN_WIDTH, RWKV_PROJ + 2 * LRU_WIDTH + 2 * ATTN_WIDTH)
D_FF = 4 * D_MODEL
RMS_EPS = 1e-6
NEG = -1e30

kernel_name = 'hybrid_rwkv7_rglru_dilated_attn_step'


def rms_norm(x, g):
    xf = x.astype(jnp.float32)
    y = xf * lax.rsqrt(jnp.mean(xf * xf, axis=-1, keepdims=True) + RMS_EPS)
    return (y * g.astype(jnp.float32)).astype(x.dtype)


def t5_bucket(dist):
    exact = N_BUCKETS // 2
    d = jnp.maximum(dist, 1).astype(jnp.float32)
    large = exact + (jnp.log(d / exact) / math.log(REL_MAX_DIST / exact) * (N_BUCKETS - exact)).astype(jnp.int32)
    return jnp.where(dist < exact, dist, jnp.minimum(large, N_BUCKETS - 1))


def rwkv7_mix(p, shift_prev, s0, mu, w0, w_up, a0, a_up, g_up, k_k, k_a, r_k, lnx_g, lnx_b):
    f32 = jnp.float32
    B, T, _ = p.shape
    shifted = jnp.concatenate([shift_prev[:, None].astype(p.dtype), p[:, :-1]], axis=1)
    m = p + (shifted - p) * mu
    r, k, v, wd, ad, gd = jnp.split(m, RWKV_SPLITS, axis=-1)
    w_log = -jax.nn.softplus(-(w0 + jnp.tanh(wd) @ w_up)) - 0.5
    decay = jnp.exp(-jnp.exp(w_log.astype(f32)))
    a = jax.nn.sigmoid(a0 + ad @ a_up)
    g = jax.nn.sigmoid(gd) @ g_up

    def heads(t):
        return t.reshape(B, T, RWKV_HEADS, RWKV_HEAD).astype(f32)

    kk = heads(k * k_k)
    kk = kk / jnp.maximum(jnp.sqrt(jnp.sum(kk * kk, axis=-1, keepdims=True)), 1e-12)
    k = k * (1.0 + (a - 1.0) * k_a)
    rh, wh, kh, vh, ah = heads(r), heads(decay), heads(k), heads(v), heads(a)

    def step(S, inp):
        r_t, w_t, k_t, v_t, kk_t, a_t = inp
        sa = jnp.einsum('bhvk,bhk->bhv', S, -kk_t)
        S = S * w_t[:, :, None, :] + sa[..., None] * (kk_t * a_t)[:, :, None, :] + v_t[..., None] * k_t[:, :, None, :]
        return S, jnp.einsum('bhvk,bhk->bhv', S, r_t)

    seq = tuple(jnp.moveaxis(t, 1, 0) for t in (rh, wh, kh, vh, kk, ah))
    s_fin, y = lax.scan(step, s0.astype(f32), seq)
    y = jnp.moveaxis(y, 0, 1)
    mean = jnp.mean(y, axis=-1, keepdims=True)
    var = jnp.mean(jnp.square(y - mean), axis=-1, keepdims=True)
    y = ((y - mean) * lax.rsqrt(var + GN_EPS)).reshape(B, T, RWKV_WIDTH) * lnx_g + lnx_b
    bonus = jnp.sum(rh * kh * r_k.astype(f32), axis=-1, keepdims=True) * vh
    y = (y + bonus.reshape(B, T, RWKV_WIDTH)) * g
    return y.astype(p.dtype), p[:, -1], s_fin


def rglru_mix(xb, gb, conv_prev, h0, conv_w, conv_b, wa, ba, wx, bx, lam):
    f32 = jnp.float32
    B, T, C = xb.shape
    xp = jnp.concatenate([conv_prev.astype(xb.dtype), xb], axis=1)
    xc = conv_b + sum(xp[:, i:i + T] * conv_w[i] for i in range(CONV_W))
    xh = xc.reshape(B, T, LRU_BLOCKS, LRU_BLOCK)
    gate_r = jax.nn.sigmoid(jnp.einsum('btnc,ncd->btnd', xh, wa).reshape(B, T, C) + ba)
    gate_i = jax.nn.sigmoid(jnp.einsum('btnc,ncd->btnd', xh, wx).reshape(B, T, C) + bx)
    log_a = (-RG_C * gate_r * jax.nn.softplus(-lam)).astype(f32)
    a = jnp.exp(log_a)
    u = jnp.sqrt(-jnp.expm1(2.0 * log_a)) * (gate_i * xc).astype(f32)

    def combine(left, right):
        a_l, u_l = left
        a_r, u_r = right
        return a_l * a_r, a_r * u_l + u_r

    a_cum, u_cum = lax.associative_scan(combine, (a, u), axis=1)
    h = a_cum * h0[:, None].astype(f32) + u_cum
    y = h * jax.nn.gelu(gb.astype(f32))
    return y.astype(xb.dtype), h[:, -1], xp[:, -(CONV_W - 1):]


def _banded_dilated(q, k, v, rel_bias, win, dil):
    B, S, H, Dh = q.shape
    span = win // dil
    L = S // dil
    nblk = -(-L // span)
    Lp = nblk * span

    def to_blocks(t):
        t = t.reshape(B, L, dil, H, Dh).transpose(0, 2, 1, 3, 4)
        t = jnp.pad(t, ((0, 0), (0, 0), (0, Lp - L), (0, 0), (0, 0)))
        return t.reshape(B, dil, nblk, span, H, Dh)

    def with_prev(t):
        prev = jnp.pad(t, ((0, 0), (0, 0), (1, 0), (0, 0), (0, 0), (0, 0)))[:, :, :-1]
        return jnp.concatenate([prev, t], axis=3)

    qb = to_blocks(q)
    kb = with_prev(to_blocks(k))
    vb = with_prev(to_blocks(v))
    qi = jnp.arange(span)[:, None]
    kj = jnp.arange(2 * span)[None, :]
    delta = qi + span - kj
    key_sub = jnp.arange(nblk)[:, None, None] * span + kj[None] - span
    valid = (delta >= 0) & (delta <= span) & (key_sub >= 0)
    bias = rel_bias[t5_bucket(jnp.clip(delta, 0, span) * dil)].astype(jnp.float32)
    s = jnp.einsum('bcnqhd,bcnkhd->bcnhqk', qb, kb) * (Dh ** -0.5) + jnp.moveaxis(bias, -1, 0)
    s = jnp.where(valid[:, None], s, NEG)
    m = jnp.max(s, axis=-1, keepdims=True)
    pr = jnp.exp(s - m)
    den = jnp.sum(pr, axis=-1, keepdims=True)
    o = jnp.einsum('bcnhqk,bcnkhd->bcnqhd', pr / den, vb)
    lse = jnp.moveaxis((m + jnp.log(den))[..., 0], 3, 4)

    def from_blocks(t):
        t = t.reshape((B, dil, Lp) + t.shape[4:])[:, :, :L]
        t = jnp.swapaxes(t, 1, 2)
        return t.reshape((B, S) + t.shape[3:])

    return from_blocks(o), from_blocks(lse)


def merge_groups(outs, lses):
    wts = jax.nn.softmax(jnp.stack(lses, axis=0), axis=0)
    return jnp.einsum('gbth,gbthd->bthd', wts, jnp.stack(outs, axis=0))


def attn_prompt(q, k, v, rel_bias):
    dt = q.dtype
    q, k, v = (t.astype(jnp.float32) for t in (q, k, v))
    outs, lses = [], []
    for win, dil in DIL_PAIRS:
        o, lse = _banded_dilated(q, k, v, rel_bias, win, dil)
        outs.append(o)
        lses.append(lse)
    return merge_groups(outs, lses).astype(dt)


def attn_sample(q, k, v, k_buf, v_buf, rel_bias):
    dt = q.dtype
    q, k, v, k_buf, v_buf = (t.astype(jnp.float32) for t in (q, k, v, k_buf, v_buf))
    T, Dh = q.shape[1], q.shape[-1]
    Wb = k_buf.shape[1]
    k_all = jnp.concatenate([k_buf, k], axis=1)
    v_all = jnp.concatenate([v_buf, v], axis=1)
    outs, lses = [], []
    for win, dil in DIL_PAIRS:
        span = win // dil
        steps = jnp.arange(span + 1)
        idx = Wb + jnp.arange(T)[:, None] - dil * steps[None, :]
        valid = idx >= 0
        idx = jnp.maximum(idx, 0)
        kg = k_all[:, idx]
        vg = v_all[:, idx]
        bias = rel_bias[t5_bucket(dil * steps)].astype(jnp.float32).T
        s = jnp.einsum('bthd,btshd->bths', q, kg) * (Dh ** -0.5) + bias
        s = jnp.where(valid[:, None, :], s, NEG)
        m = jnp.max(s, axis=-1, keepdims=True)
        pr = jnp.exp(s - m)
        den = jnp.sum(pr, axis=-1, keepdims=True)
        outs.append(jnp.einsum('bths,btshd->bthd', pr / den, vg))
        lses.append((m + jnp.log(den))[..., 0])
    return merge_groups(outs, lses).astype(dt)


def decoder_layer(x, l, prm, shift_prev, wkv0, conv_prev, h0, attend):
    B, T, _ = x.shape
    h = rms_norm(x, prm['norm_mix_pre'][l])
    proj = h @ prm['w_in'][l]
    p_rwkv, lru_x, lru_g, q, k, v = jnp.split(proj, IN_SPLITS, axis=-1)
    y_a, shift_new, wkv_new = rwkv7_mix(p_rwkv, shift_prev, wkv0, prm['rwkv_mu'][l], prm['rwkv_w0'][l], prm['rwkv_w_up'][l], prm['rwkv_a0'][l], prm['rwkv_a_up'][l], prm['rwkv_g_up'][l], prm['rwkv_k_k'][l], prm['rwkv_k_a'][l], prm['rwkv_r_k'][l], prm['rwkv_lnx_g'][l], prm['rwkv_lnx_b'][l])
    y_b, h_new, conv_new = rglru_mix(lru_x, lru_g, conv_prev, h0, prm['lru_conv_w'][l], prm['lru_conv_b'][l], prm['lru_wa'][l], prm['lru_ba'][l], prm['lru_wx'][l], prm['lru_bx'][l], prm['lru_lambda'][l])
    q, k, v = (t.reshape(B, T, ATTN_HEADS, ATTN_HEAD) for t in (q, k, v))
    y_c = attend(q, k, v).reshape(B, T, ATTN_WIDTH)
    mixed = jnp.concatenate([y_a, y_b, y_c], axis=-1) @ prm['w_out'][l]
    x = x + rms_norm(mixed, prm['norm_mix_post'][l])
    hf = rms_norm(x, prm['norm_ffn_pre'][l])
    f = jnp.square(jax.nn.relu(hf @ prm['ffn_w1'][l])) @ prm['ffn_w2'][l]
    x = x + rms_norm(f, prm['norm_ffn_post'][l])
    return x, (shift_new, wkv_new, conv_new, h_new, k, v)


def setup_inputs(seed: int = 0) -> dict:
    key = jax.random.key(seed)
    ks = list(jax.random.split(key, 48))

    def nrm(shape, scale=1.0):
        return scale * jax.random.normal(ks.pop(), shape, jnp.float32)

    def unif(shape, lo, hi):
        return jax.random.uniform(ks.pop(), shape, jnp.float32, lo, hi)

    cache_win = min(ATTN_WINDOW, PAST_LEN)
    a_init = unif((DEPTH, LRU_WIDTH), 0.9, 0.999) ** (1.0 / RG_C)
    return {
        'x_prompt': nrm((BATCH, SEQ, D_MODEL)),
        'x_sample': nrm((DEC_BATCH, DEC_SEQ, D_MODEL)),
        'state_rwkv_wkv': nrm((DEPTH, DEC_BATCH, RWKV_HEADS, RWKV_HEAD, RWKV_HEAD), 0.5),
        'state_rwkv_shift': nrm((DEPTH, DEC_BATCH, RWKV_PROJ)),
        'state_lru_h': nrm((DEPTH, DEC_BATCH, LRU_WIDTH), 0.5),
        'state_lru_conv': nrm((DEPTH, DEC_BATCH, CONV_W - 1, LRU_WIDTH)),
        'cache_attn_k': nrm((DEPTH, DEC_BATCH, cache_win, ATTN_HEADS, ATTN_HEAD)),
        'cache_attn_v': nrm((DEPTH, DEC_BATCH, cache_win, ATTN_HEADS, ATTN_HEAD)),
        'rel_bias': nrm((N_BUCKETS, ATTN_HEADS), 0.5),
        'norm_mix_pre': 1.0 + nrm((DEPTH, D_MODEL), 0.05),
        'norm_mix_post': 1.0 + nrm((DEPTH, D_MODEL), 0.05),
        'norm_ffn_pre': 1.0 + nrm((DEPTH, D_MODEL), 0.05),
        'norm_ffn_post': 1.0 + nrm((DEPTH, D_MODEL), 0.05),
        'w_in': nrm((DEPTH, D_MODEL, N_IN), D_MODEL ** -0.5),
        'w_out': nrm((DEPTH, D_MIX, D_MODEL), D_MIX ** -0.5),
        'rwkv_mu': unif((DEPTH, RWKV_PROJ), 0.0, 1.0),
        'rwkv_w0': unif((DEPTH, RWKV_WIDTH), -6.0, 0.0),
        'rwkv_w_up': nrm((DEPTH, DECAY_LORA, RWKV_WIDTH), 0.1),
        'rwkv_a0': nrm((DEPTH, RWKV_WIDTH), 0.5),
        'rwkv_a_up': nrm((DEPTH, ICLR_LORA, RWKV_WIDTH), 0.1),
        'rwkv_g_up': nrm((DEPTH, GATE_LORA, RWKV_WIDTH), GATE_LORA ** -0.5),
        'rwkv_k_k': 0.85 + nrm((DEPTH, RWKV_WIDTH), 0.05),
        'rwkv_k_a': 1.0 + nrm((DEPTH, RWKV_WIDTH), 0.05),
        'rwkv_r_k': nrm((DEPTH, RWKV_HEADS, RWKV_HEAD), 0.1),
        'rwkv_lnx_g': 1.0 + nrm((DEPTH, RWKV_WIDTH), 0.05),
        'rwkv_lnx_b': nrm((DEPTH, RWKV_WIDTH), 0.02),
        'lru_conv_w': nrm((DEPTH, CONV_W, LRU_WIDTH), CONV_W ** -0.5),
        'lru_conv_b': nrm((DEPTH, LRU_WIDTH), 0.02),
        'lru_wa': nrm((DEPTH, LRU_BLOCKS, LRU_BLOCK, LRU_BLOCK), LRU_BLOCK ** -0.5),
        'lru_ba': nrm((DEPTH, LRU_WIDTH), 0.02),
        'lru_wx': nrm((DEPTH, LRU_BLOCKS, LRU_BLOCK, LRU_BLOCK), LRU_BLOCK ** -0.5),
        'lru_bx': nrm((DEPTH, LRU_WIDTH), 0.02),
        'lru_lambda': jnp.log(a_init / (1.0 - a_init)),
        'ffn_w1': nrm((DEPTH, D_MODEL, D_FF), D_MODEL ** -0.5),
        'ffn_w2': nrm((DEPTH, D_FF, D_MODEL), D_FF ** -0.5),
    }


def reference(x_prompt, x_sample, state_rwkv_wkv, state_rwkv_shift, state_lru_h, state_lru_conv, cache_attn_k, cache_attn_v, rel_bias, norm_mix_pre, norm_mix_post, norm_ffn_pre, norm_ffn_post, w_in, w_out, rwkv_mu, rwkv_w0, rwkv_w_up, rwkv_a0, rwkv_a_up, rwkv_g_up, rwkv_k_k, rwkv_k_a, rwkv_r_k, rwkv_lnx_g, rwkv_lnx_b, lru_conv_w, lru_conv_b, lru_wa, lru_ba, lru_wx, lru_bx, lru_lambda, ffn_w1, ffn_w2):
    prm = {'norm_mix_pre': norm_mix_pre, 'norm_mix_post': norm_mix_post, 'norm_ffn_pre': norm_ffn_pre, 'norm_ffn_post': norm_ffn_post, 'w_in': w_in, 'w_out': w_out, 'rwkv_mu': rwkv_mu, 'rwkv_w0': rwkv_w0, 'rwkv_w_up': rwkv_w_up, 'rwkv_a0': rwkv_a0, 'rwkv_a_up': rwkv_a_up, 'rwkv_g_up': rwkv_g_up, 'rwkv_k_k': rwkv_k_k, 'rwkv_k_a': rwkv_k_a, 'rwkv_r_k': rwkv_r_k, 'rwkv_lnx_g': rwkv_lnx_g, 'rwkv_lnx_b': rwkv_lnx_b, 'lru_conv_w': lru_conv_w, 'lru_conv_b': lru_conv_b, 'lru_wa': lru_wa, 'lru_ba': lru_ba, 'lru_wx': lru_wx, 'lru_bx': lru_bx, 'lru_lambda': lru_lambda, 'ffn_w1': ffn_w1, 'ffn_w2': ffn_w2}
    B = x_prompt.shape[0]
    keep = min(ATTN_WINDOW, x_prompt.shape[1])
    attend_prompt = functools.partial(attn_prompt, rel_bias=rel_bias)
    xp, xs = x_prompt, x_sample
    new_p, new_s = [], []
    for l in range(DEPTH):
        xp, st_p = decoder_layer(xp, l, prm, jnp.zeros((B, RWKV_PROJ), xp.dtype), jnp.zeros((B, RWKV_HEADS, RWKV_HEAD, RWKV_HEAD), jnp.float32), jnp.zeros((B, CONV_W - 1, LRU_WIDTH), xp.dtype), jnp.zeros((B, LRU_WIDTH), jnp.float32), attend_prompt)
        attend_sample = functools.partial(attn_sample, k_buf=cache_attn_k[l], v_buf=cache_attn_v[l], rel_bias=rel_bias)
        xs, st_s = decoder_layer(xs, l, prm, state_rwkv_shift[l], state_rwkv_wkv[l], state_lru_conv[l], state_lru_h[l], attend_sample)
        new_p.append(st_p)
        new_s.append(st_s)
    shift_p = jnp.stack([s[0] for s in new_p])
    shift_s = jnp.stack([s[0] for s in new_s])
    wkv_p = jnp.stack([s[1] for s in new_p])
    wkv_s = jnp.stack([s[1] for s in new_s])
    conv_p = jnp.stack([s[2] for s in new_p])
    conv_s = jnp.stack([s[2] for s in new_s])
    h_p = jnp.stack([s[3] for s in new_p])
    h_s = jnp.stack([s[3] for s in new_s])
    k_p = jnp.stack([s[4][:, -keep:] for s in new_p])
    k_s = jnp.stack([s[4] for s in new_s])
    v_p = jnp.stack([s[5][:, -keep:] for s in new_p])
    v_s = jnp.stack([s[5] for s in new_s])
    return (xp, xs, wkv_p, wkv_s, shift_p, shift_s, h_p, h_s, conv_p, conv_s, k_p, k_s, v_p, v_s)
```

```python
import contextlib
import numpy as np
import concourse.bass as bass
import concourse.mybir as mybir
from concourse.bass_utils import run_bass_kernel_spmd

F32 = mybir.dt.float32
BF16 = mybir.dt.bfloat16
F32R = mybir.dt.float32r
AF = mybir.ActivationFunctionType
ALU = mybir.AluOpType
AX = mybir.AxisListType

D = 2048
SEQ = 2048
DEPTH = 2
NS = 4
TT = SEQ + NS
RW = 512
RH = 8
RN = 64
RPROJ = 1792
LW = 512
AW = 1024
AH = 8
AD = 128
N_IN = 5888
DFF = 8192
NEG = -1e30
C_P, C_LX, C_LG, C_Q, C_K, C_V = 0, 1792, 2304, 2816, 3840, 4864
NFM = 4864


class Buf:
    __slots__ = ("name", "w", "r", "x")

    def __init__(self, name="", x=False):
        self.name = name
        self.w = None
        self.r = {}
        self.x = x


def PBuf():
    return Buf("psum", True)


class Tok:
    __slots__ = ("eng", "sem", "val")

    def __init__(self, eng, sem, val):
        self.eng, self.sem, self.val = eng, sem, val


class Eng:
    def __init__(self, name, eng, sems):
        self.name, self.eng, self.sems = name, eng, sems
        self.si = 0
        self.count = 0
        self.waited = {}


class Sched:
    EPOCH = 30000

    def __init__(self, nc, es):
        self.nc = nc
        self.engs = {}
        for name in ("tensor", "vector", "scalar", "gpsimd", "sync"):
            sems = [es.enter_context(nc.semaphore(f"p_{name}_{i}")) for i in range(3)]
            self.engs[name] = Eng(name, getattr(nc, name), sems)
        self.dsem = {}
        self.dval = {}
        self.dnext = {}
        for q, cnt in (("sync", 24), ("gpsimd", 12), ("scalar", 4)):
            self.dsem[q] = [es.enter_context(nc.semaphore(f"dq_{q}_{i}")) for i in range(cnt)]
            self.dval[q] = [0] * cnt
            self.dnext[q] = 0
        self.dma_toks = []
        self.uid = 0

    def _wait(self, E, tok):
        key = id(tok.sem)
        if E.waited.get(key, 0) >= tok.val:
            return
        E.eng.wait_ge(tok.sem, tok.val)
        E.waited[key] = tok.val

    def _deps(self, E, reads, writes):
        for b in reads:
            if b.w is not None:
                self._dep1(E, b.w)
            if b.x:
                for t in b.r.values():
                    if t.eng is not E:
                        self._dep1(E, t)
        for b in writes:
            if b.w is not None:
                self._dep1(E, b.w)
            for t in b.r.values():
                self._dep1(E, t)

    def _dep1(self, E, tok):
        if tok.eng is E and E.name == "tensor":
            return
        self._wait(E, tok)

    def _mark(self, tok, reads, writes):
        for b in writes:
            b.w = tok
            b.r = {}
        for b in reads:
            if tok.eng is None:
                self.uid += 1
                b.r[("d", self.uid)] = tok
            else:
                b.r[tok.eng.name] = tok

    def op(self, engname, fn, reads=(), writes=(), inc=True):
        E = self.engs[engname]
        self._deps(E, reads, writes)
        ins = fn(E.eng)
        if inc:
            if E.count >= self.EPOCH:
                E.si += 1
                E.count = 0
            E.count += 1
            ins.then_inc(E.sems[E.si], 1)
            tok = Tok(E, E.sems[E.si], E.count)
        else:
            tok = Tok(E, E.sems[E.si], E.count + 1)
        self._mark(tok, reads, writes)
        return tok

    def dma(self, qname, out, in_, reads=(), writes=(), **kw):
        E = self.engs[qname]
        self._deps(E, reads, writes)
        i = self.dnext[qname]
        self.dnext[qname] = (i + 1) % len(self.dsem[qname])
        sem = self.dsem[qname][i]
        dv = self.dval[qname]
        if dv[i] > 0:
            self._wait(E, Tok(None, sem, dv[i]))
        dv[i] += 16
        E.eng.dma_start(out=out, in_=in_, **kw).then_inc(sem, 16)
        tok = Tok(None, sem, dv[i])
        self._mark(tok, reads, writes)
        return tok

    def barrier(self):
        names = list(self.engs)
        for a in names:
            A = self.engs[a]
            for b in names:
                if a == b:
                    continue
                B = self.engs[b]
                if B.count > 0:
                    self._wait(A, Tok(B, B.sems[B.si], B.count))
            for q in self.dsem:
                for i, sem in enumerate(self.dsem[q]):
                    if self.dval[q][i] > 0:
                        self._wait(A, Tok(None, sem, self.dval[q][i]))

    def finish(self):
        self.barrier()


def dram_bcast(ap2d_row, nparts, n):
    return bass.AP(tensor=ap2d_row.tensor, offset=ap2d_row.offset, ap=[[0, nparts], [1, n]])


def tok_tiles():
    tl = [(i * 128, 128) for i in range(SEQ // 128)]
    tl.append((SEQ, NS))
    return tl


def tok_blocks():
    bl = [(i * 512, 512) for i in range(SEQ // 512)]
    bl.append((SEQ, NS))
    return bl


class Prog:
    def __init__(self, stages):
        self.stages = stages
        nc = self.nc = bass.Bass("TRN2", target_bir_lowering=False)
        self.es = contextlib.ExitStack()
        self.S = Sched(nc, self.es)
        self.I = {}
        self.O = {}
        self.X = {}
        self.uid = 0

    def sbt(self, name, shape, dt):
        self.uid += 1
        return self.nc.sbuf_tensor(f"{name}_u{self.uid}", shape, dt)

    def pst(self, name, shape, dt):
        self.uid += 1
        return self.nc.psum_tensor(f"{name}_u{self.uid}", shape, dt)

    def din(self, name, shape, dt=F32):
        self.I[name] = self.nc.dram_tensor(name, list(shape), dt, kind="ExternalInput").ap()
        return self.I[name]

    def dout(self, name, shape, dt=F32):
        self.O[name] = self.nc.dram_tensor(name, list(shape), dt, kind="ExternalOutput").ap()
        return self.O[name]

    def dscr(self, name, shape, dt=F32):
        self.X[name] = self.nc.dram_tensor(name, list(shape), dt, kind="Internal").ap()
        return self.X[name]

    def declare(self):
        din, dout, dscr = self.din, self.dout, self.dscr
        din("xp", [SEQ, D]); din("xs", [NS, D])
        din("st_wkv", [DEPTH, RH, RN, RN]); din("st_shift", [DEPTH, RPROJ])
        din("st_lruh", [DEPTH, LW]); din("st_conv", [DEPTH, 3, LW])
        din("cache_k", [DEPTH, SEQ, AW]); din("cache_v", [DEPTH, SEQ, AW])
        din("rel_bias", [32, AH])
        for n in ("norm_mix_pre", "norm_mix_post", "norm_ffn_pre", "norm_ffn_post"):
            din(n, [DEPTH, D])
        din("w_in", [DEPTH, D, N_IN]); din("w_out", [DEPTH, D, D])
        din("rwkv_mu", [DEPTH, RPROJ]); din("rwkv_w0", [DEPTH, RW]); din("rwkv_w_up", [DEPTH, 64, RW])
        din("rwkv_a0", [DEPTH, RW]); din("rwkv_a_up", [DEPTH, 64, RW]); din("rwkv_g_up", [DEPTH, 128, RW])
        din("rwkv_k_k", [DEPTH, RW]); din("rwkv_k_a", [DEPTH, RW]); din("rwkv_r_k", [DEPTH, RW])
        din("rwkv_lnx_g", [DEPTH, RW]); din("rwkv_lnx_b", [DEPTH, RW])
        din("lru_conv_w", [DEPTH, 4, LW]); din("lru_conv_b", [DEPTH, LW])
        din("lru_wa", [DEPTH, 8, 64, 64]); din("lru_ba", [DEPTH, LW])
        din("lru_wx", [DEPTH, 8, 64, 64]); din("lru_bx", [DEPTH, LW]); din("lru_lambda", [DEPTH, LW])
        din("ffn_w1", [DEPTH, D, DFF]); din("ffn_w2", [DEPTH, DFF, D])
        din("c_ident", [128, 128]); din("c_tri", [128, 128]); din("c_stri", [128, 128]); din("c_ltri", [128, 128])
        din("c_onehot", [32, 387]); din("c_d01", [4, 4]); din("c_dneg", [4, 4]); din("c_colmask", [128, 4, 4]); din("c_antiident", [128, 128])
        dout("yp", [SEQ, D]); dout("ys", [NS, D])
        dout("wkv_p", [DEPTH, RH, RN, RN]); dout("wkv_s", [DEPTH, RH, RN, RN])
        dout("shift_p", [DEPTH, RPROJ]); dout("shift_s", [DEPTH, RPROJ])
        dout("lruh_p", [DEPTH, LW]); dout("lruh_s", [DEPTH, LW])
        dout("conv_p", [DEPTH, 3, LW]); dout("conv_s", [DEPTH, 3, LW])
        dout("k_p", [DEPTH, SEQ, AW]); dout("k_s", [DEPTH, NS, AW])
        dout("v_p", [DEPTH, SEQ, AW]); dout("v_s", [DEPTH, NS, AW])
        dscr("projT", [NFM - 2048, TT])
        dscr("qT", [AW, TT], BF16)
        dscr("kT", [AW, TT], BF16)
        dscr("xres", [TT, D])
        dscr("yT", [D, TT], BF16)
        dscr("Gd", [8, 3, 383])
        dscr("x1s", [TT, D])
        dscr("hfT", [D, TT], BF16)
        dscr("Os", [3, SEQ, AW])
        dscr("Ls", [3, SEQ, 8])

    def load_consts(self):
        nc, S, es = self.nc, self.S, self.es
        self.identf = es.enter_context(self.sbt("identf", [128, 128], F32))
        self.identb = es.enter_context(self.sbt("identb", [128, 128], BF16))
        self.b_ident = Buf("ident")
        S.dma("sync", self.identf[:], self.I["c_ident"][:, :], writes=[self.b_ident])
        S.dma("gpsimd", self.identb[:], self.I["c_ident"][:, :], writes=[self.b_ident])

    def stage_norm_T(self, l, first, hT, b_hT, gname):
        nc, S = self.nc, self.S
        I = self.I
        with contextlib.ExitStack() as es:
            sb = lambda n, s, d: es.enter_context(self.sbt(n, s, d))
            xt = [sb(f"n_xt{j}", [128, D], F32) for j in range(2)]
            hb = [sb(f"n_hb{j}", [128, D], BF16) for j in range(2)]
            junk = sb("n_junk", [128, D], BF16)
            gt = sb("n_gt", [128, D], F32)
            ss = [sb(f"n_ss{j}", [128, 2], F32) for j in range(2)]
            pt = [es.enter_context(self.pst(f"n_pt{j}", [128, D], BF16)) for j in range(2)]
            b_xt = [Buf(), Buf()]; b_hb = [Buf(), Buf()]; b_junk = Buf(); b_gt = Buf()
            b_ss = [Buf(), Buf()]; b_pt = [PBuf(), PBuf()]
            S.dma("sync", gt[:], dram_bcast(I[gname][l:l + 1, :], 128, D), writes=[b_gt])
            for ti, (c0, n) in enumerate(tok_tiles()):
                j = ti % 2
                if c0 < SEQ:
                    src = (I["xp"] if first else self.X["xres"])[c0:c0 + n, :]
                else:
                    src = I["xs"][:, :] if first else self.X["xres"][c0:c0 + n, :]
                S.dma("sync", xt[j][:n, :], src, writes=[b_xt[j]])
                S.op("scalar", lambda e: e.activation(out=junk[:n, :], in_=xt[j][:n, :], func=AF.Square,
                                                      scale=float(D ** -0.5), accum_out=ss[j][:n, 0:1]),
                     reads=[b_xt[j]], writes=[b_junk, b_ss[j]])
                S.op("vector", lambda e: e.tensor_scalar(out=ss[j][:n, 1:2], in0=ss[j][:n, 0:1], scalar1=1e-6,
                                                         scalar2=None, op0=ALU.add),
                     reads=[b_ss[j]], writes=[b_ss[j]])
                S.op("scalar", lambda e: e.activation(out=ss[j][:n, 0:1], in_=ss[j][:n, 1:2], func=AF.Sqrt),
                     reads=[b_ss[j]], writes=[b_ss[j]])
                S.op("vector", lambda e: e.reciprocal(out=ss[j][:n, 1:2], in_=ss[j][:n, 0:1]),
                     reads=[b_ss[j]], writes=[b_ss[j]])
                S.op("vector", lambda e: e.scalar_tensor_tensor(out=hb[j][:n, :], in0=xt[j][:n, :],
                                                                scalar=ss[j][:n, 1:2], in1=gt[:n, :],
                                                                op0=ALU.mult, op1=ALU.mult),
                     reads=[b_xt[j], b_ss[j], b_gt], writes=[b_hb[j]])
                for c in range(16):
                    S.op("tensor", lambda e: e.transpose(out=pt[j][:, c * 128:c * 128 + n],
                                                         in_=hb[j][:n, c * 128:(c + 1) * 128],
                                                         identity=self.identb[:n, :n]),
                         reads=[b_hb[j], self.b_ident], writes=[b_pt[j]], inc=(c == 15))
                src3 = pt[j][:, :].rearrange("p (c t) -> p c t", c=16)[:, :, 0:n]
                eng = "scalar" if ti % 2 == 0 else "vector"
                if eng == "scalar":
                    S.op("scalar", lambda e: e.copy(out=hT[:, :, c0:c0 + n], in_=src3),
                         reads=[b_pt[j]], writes=[b_hT])
                else:
                    S.op("vector", lambda e: e.tensor_copy(out=hT[:, :, c0:c0 + n], in_=src3),
                         reads=[b_pt[j]], writes=[b_hT])
            S.barrier()

    def stage_win(self, l, hT, b_hT):
        nc, S = self.nc, self.S
        I, O, X = self.I, self.O, self.X
        w_in = I["w_in"]
        blocks = []
        for c0 in range(0, 3584, 512):
            blocks.append((c0, 512, True, False))
        blocks.append((3584, 256, True, False))
        blocks.append((3840, 512, True, True))
        blocks.append((4352, 512, True, True))
        blocks.append((4864, 512, False, True))
        blocks.append((5376, 512, False, True))
        with contextlib.ExitStack() as es:
            sb = lambda n, s, d: es.enter_context(self.sbt(n, s, d))
            wb = [sb(f"w_wb{j}", [128, 16, 512], BF16) for j in range(2)]
            b_wb = [Buf(), Buf()]
            NST = 4
            st = [sb(f"w_st{j}", [128, 512], F32) for j in range(NST)]
            stb = [sb(f"w_stb{j}", [128, 512], BF16) for j in range(NST)]
            b_st = [Buf() for _ in range(NST)]
            b_stb = [Buf() for _ in range(NST)]
            NPS = 6
            ps = [es.enter_context(self.pst(f"w_ps{j}", [128, 512], F32)) for j in range(NPS)]
            b_ps = [PBuf() for _ in range(NPS)]
            cnt = {"ps": 0, "st": 0, "ev": 0}

            def load_block(bi):
                c0, ncols, fm, tm = blocks[bi]
                j = bi % 2
                src = w_in[l, :, c0:c0 + ncols].rearrange("(c p) n -> p c n", p=128)
                S.dma("gpsimd", wb[j][:, 0:8, 0:ncols], src[:, 0:8, :], writes=[b_wb[j]])
                S.dma("gpsimd", wb[j][:, 8:16, 0:ncols], src[:, 8:16, :], writes=[b_wb[j]])

            def evac(pj, n_p, n_f, dst_kind, dst_ap):
                k = cnt["st"] % NST
                cnt["st"] += 1
                use_act = cnt["ev"] % 2 == 0
                cnt["ev"] += 1
                if dst_kind == "f32":
                    tgt, bt = st[k], b_st[k]
                else:
                    tgt, bt = stb[k], b_stb[k]
                if use_act:
                    S.op("scalar", lambda e: e.copy(out=tgt[:n_p, :n_f], in_=ps[pj][:n_p, :n_f]),
                         reads=[b_ps[pj]], writes=[bt])
                else:
                    S.op("vector", lambda e: e.tensor_copy(out=tgt[:n_p, :n_f], in_=ps[pj][:n_p, :n_f]),
                         reads=[b_ps[pj]], writes=[bt])
                S.dma("sync", dst_ap, tgt[:n_p, :n_f], reads=[bt])

            load_block(0)
            for bi, (c0, ncols, fm, tm) in enumerate(blocks):
                j = bi % 2
                if bi + 1 < len(blocks):
                    load_block(bi + 1)
                if fm:
                    for ft in range(ncols // 128):
                        f0 = c0 + ft * 128
                        for (t0, nt) in tok_blocks():
                            pj = cnt["ps"] % NPS
                            cnt["ps"] += 1
                            for c in range(16):
                                S.op("tensor", lambda e: e.matmul(ps[pj][:, :nt], lhsT=wb[j][:, c, ft * 128:(ft + 1) * 128],
                                                                  rhs=hT[:, c, t0:t0 + nt], start=(c == 0), stop=(c == 15)),
                                     reads=[b_wb[j], b_hT], writes=[b_ps[pj]], inc=(c == 15))
                            if f0 < C_Q:
                                evac(pj, 128, nt, "f32", X["projT"][f0:f0 + 128, t0:t0 + nt])
                            elif f0 < C_K:
                                evac(pj, 128, nt, "bf16", X["qT"][f0 - C_Q:f0 - C_Q + 128, t0:t0 + nt])
                            else:
                                evac(pj, 128, nt, "bf16", X["kT"][f0 - C_K:f0 - C_K + 128, t0:t0 + nt])
                if tm:
                    for (t0, nt) in tok_tiles():
                        pj = cnt["ps"] % NPS
                        cnt["ps"] += 1
                        for c in range(16):
                            S.op("tensor", lambda e: e.matmul(ps[pj][:nt, :ncols], lhsT=hT[:, c, t0:t0 + nt],
                                                              rhs=wb[j][:, c, 0:ncols], start=(c == 0), stop=(c == 15)),
                                 reads=[b_wb[j], b_hT], writes=[b_ps[pj]], inc=(c == 15))
                        if c0 < C_V:
                            dp, dsm, off = O["k_p"], O["k_s"], c0 - C_K
                        else:
                            dp, dsm, off = O["v_p"], O["v_s"], c0 - C_V
                        if t0 < SEQ:
                            dst = dp[l, t0:t0 + nt, off:off + ncols]
                        else:
                            dst = dsm[l, 0:nt, off:off + ncols]
                        evac(pj, nt, ncols, "f32", dst)
            S.barrier()

    def fm_vec(self, ap1d, ntiles):
        return bass.AP(tensor=ap1d.tensor, offset=ap1d.offset, ap=[[1, 128], [128, ntiles]])

    def stage_lru(self, l):
        nc, S = self.nc, self.S
        I, O, X = self.I, self.O, self.X
        with contextlib.ExitStack() as es:
            sb = lambda n, s, d: es.enter_context(self.sbt(n, s, d))
            T = SEQ
            cw = sb("l_cw", [128, 4, 4], F32)
            cb = sb("l_cb", [128, 4], F32)
            ba = sb("l_ba", [128, 4], F32)
            bx = sb("l_bx", [128, 4], F32)
            lam = sb("l_lam", [128, 4], F32)
            sp = sb("l_sp", [128, 6, 4], F32)
            h0 = sb("l_h0", [128, 4], F32)
            wbd = sb("l_wbd", [128, 2, 4, 128], BF16)
            xpad = sb("l_xpad", [128, T + 3], F32)
            gb = sb("l_gb", [128, T], F32)
            xc = sb("l_xc", [128, T], F32)
            gr = sb("l_gr", [128, T], F32)
            gi = sb("l_gi", [128, T], F32)
            tmp = sb("l_tmp", [128, T], F32)
            hh = sb("l_hh", [128, T], F32)
            xcb = sb("l_xcb", [128, T], BF16)
            yb = sb("l_yb", [128, T], BF16)
            ps = [es.enter_context(self.pst(f"l_ps{j}", [128, 512], F32)) for j in range(2)]
            b_par = Buf(); b_wbd = Buf(); b_xpad = Buf(); b_gb = Buf(); b_xc = Buf(); b_gr = Buf(); b_gi = Buf()
            b_tmp = Buf(); b_hh = Buf(); b_xcb = Buf(); b_yb = Buf(); b_ps = [PBuf(), PBuf()]; b_h0 = Buf()
            nck = dict(allow_slow_non_contiguous=True)
            for tap in range(4):
                S.dma("sync", cw[:, tap, :], self.fm_vec(I["lru_conv_w"][l, tap, :], 4), writes=[b_par], **nck)
            S.dma("sync", cb[:, :], self.fm_vec(I["lru_conv_b"][l, :], 4), writes=[b_par], **nck)
            S.dma("sync", ba[:, :], self.fm_vec(I["lru_ba"][l, :], 4), writes=[b_par], **nck)
            S.dma("sync", bx[:, :], self.fm_vec(I["lru_bx"][l, :], 4), writes=[b_par], **nck)
            S.dma("sync", lam[:, :], self.fm_vec(I["lru_lambda"][l, :], 4), writes=[b_par], **nck)
            S.dma("sync", h0[:, :], self.fm_vec(I["st_lruh"][l, :], 4), writes=[b_h0], **nck)
            S.op("vector", lambda e: e.memset(wbd[:], 0.0), writes=[b_wbd])
            for gi_, wn in enumerate(("lru_wa", "lru_wx")):
                for n in range(8):
                    ct, hf = n // 2, n % 2
                    S.dma("gpsimd", wbd[hf * 64:(hf + 1) * 64, gi_, ct, hf * 64:(hf + 1) * 64], I[wn][l, n, :, :],
                          writes=[b_wbd])
            e_, ln_, ser, msk, res = sp[:, 0, :], sp[:, 1, :], sp[:, 2, :], sp[:, 3, :], sp[:, 4, :]
            S.op("scalar", lambda e: e.activation(out=e_, in_=lam[:, :], func=AF.Exp, scale=-1.0), reads=[b_par], writes=[b_par])
            S.op("scalar", lambda e: e.activation(out=ln_, in_=e_, func=AF.Ln, bias=1.0), reads=[b_par], writes=[b_par])
            S.op("vector", lambda e: e.tensor_scalar(out=ser, in0=e_, scalar1=-0.25, scalar2=1.0 / 3.0, op0=ALU.mult, op1=ALU.add), reads=[b_par], writes=[b_par])
            S.op("vector", lambda e: e.tensor_tensor(out=ser, in0=ser, in1=e_, op=ALU.mult), reads=[b_par], writes=[b_par])
            S.op("vector", lambda e: e.tensor_scalar(out=ser, in0=ser, scalar1=-1.0, scalar2=0.5, op0=ALU.mult, op1=ALU.add), reads=[b_par], writes=[b_par])
            S.op("vector", lambda e: e.tensor_tensor(out=ser, in0=ser, in1=e_, op=ALU.mult), reads=[b_par], writes=[b_par])
            S.op("vector", lambda e: e.tensor_scalar(out=ser, in0=ser, scalar1=-1.0, scalar2=1.0, op0=ALU.mult, op1=ALU.add), reads=[b_par], writes=[b_par])
            S.op("vector", lambda e: e.tensor_tensor(out=ser, in0=ser, in1=e_, op=ALU.mult), reads=[b_par], writes=[b_par])
            S.op("vector", lambda e: e.tensor_single_scalar(out=msk, in_=e_, scalar=0.05, op=ALU.is_lt), reads=[b_par], writes=[b_par])
            S.op("vector", lambda e: e.tensor_tensor(out=ser, in0=ser, in1=ln_, op=ALU.subtract), reads=[b_par], writes=[b_par])
            S.op("vector", lambda e: e.tensor_tensor(out=ser, in0=ser, in1=msk, op=ALU.mult), reads=[b_par], writes=[b_par])
            S.op("vector", lambda e: e.tensor_tensor(out=res, in0=ser, in1=ln_, op=ALU.add), reads=[b_par], writes=[b_par])
            S.op("vector", lambda e: e.tensor_scalar(out=res, in0=res, scalar1=-8.0, scalar2=None, op0=ALU.mult), reads=[b_par], writes=[b_par])
            m8sp = res

            for (col0, Tn, is_s) in ((0, SEQ, False), (SEQ, NS, True)):
                for ct in range(4):
                    xa = xpad[:, 3:3 + Tn]
                    S.dma("sync", xa, X["projT"][C_LX + ct * 128:C_LX + (ct + 1) * 128, col0:col0 + Tn], writes=[b_xpad])
                    S.dma("sync", gb[:, :Tn], X["projT"][C_LG + ct * 128:C_LG + (ct + 1) * 128, col0:col0 + Tn], writes=[b_gb])
                    if is_s:
                        src = bass.AP(tensor=I["st_conv"].tensor, offset=I["st_conv"][l, 0, ct * 128:(ct + 1) * 128].offset,
                                      ap=[[1, 128], [LW, 3]])
                        S.dma("sync", xpad[:, 0:3], src, writes=[b_xpad], **nck)
                    else:
                        S.op("vector", lambda e: e.memset(xpad[:, 0:3], 0.0), writes=[b_xpad])
                    S.op("vector", lambda e: e.tensor_scalar(out=xc[:, :Tn], in0=xa, scalar1=cw[:, 3, ct:ct + 1], scalar2=cb[:, ct:ct + 1],
                                                             op0=ALU.mult, op1=ALU.add), reads=[b_xpad, b_par], writes=[b_xc])
                    for tap in range(3):
                        S.op("vector", lambda e: e.scalar_tensor_tensor(out=xc[:, :Tn], in0=xpad[:, tap:tap + Tn], scalar=cw[:, tap, ct:ct + 1],
                                                                        in1=xc[:, :Tn], op0=ALU.mult, op1=ALU.add),
                             reads=[b_xpad, b_par, b_xc], writes=[b_xc])
                    S.op("scalar", lambda e: e.copy(out=xcb[:, :Tn], in_=xc[:, :Tn]), reads=[b_xc], writes=[b_xcb])
                    k = 0
                    for t0 in range(0, Tn, 512):
                        nt = min(512, Tn - t0)
                        for gsel, (dst, bd, bias) in enumerate(((gr, b_gr, ba), (gi, b_gi, bx))):
                            pj = k % 2
                            k += 1
                            S.op("tensor", lambda e: e.matmul(ps[pj][:, :nt], lhsT=wbd[:, gsel, ct, :], rhs=xcb[:, t0:t0 + nt], start=True, stop=True),
                                 reads=[b_wbd, b_xcb], writes=[b_ps[pj]])
                            S.op("scalar", lambda e: e.activation(out=dst[:, t0:t0 + nt], in_=ps[pj][:, :nt], func=AF.Sigmoid, bias=bias[:, ct:ct + 1]),
                                 reads=[b_ps[pj], b_par], writes=[bd])
                    S.op("vector", lambda e: e.tensor_scalar(out=gr[:, :Tn], in0=gr[:, :Tn], scalar1=m8sp[:, ct:ct + 1], scalar2=None, op0=ALU.mult),
                         reads=[b_gr, b_par], writes=[b_gr])
                    S.op("scalar", lambda e: e.activation(out=tmp[:, :Tn], in_=gr[:, :Tn], func=AF.Exp, scale=2.0), reads=[b_gr], writes=[b_tmp])
                    S.op("scalar", lambda e: e.activation(out=gr[:, :Tn], in_=gr[:, :Tn], func=AF.Exp), reads=[b_gr], writes=[b_gr])
                    S.op("vector", lambda e: e.tensor_scalar(out=tmp[:, :Tn], in0=tmp[:, :Tn], scalar1=-1.0, scalar2=1.0, op0=ALU.mult, op1=ALU.add),
                         reads=[b_tmp], writes=[b_tmp])
                    S.op("scalar", lambda e: e.activation(out=tmp[:, :Tn], in_=tmp[:, :Tn], func=AF.Sqrt), reads=[b_tmp], writes=[b_tmp])
                    S.op("vector", lambda e: e.tensor_tensor(out=gi[:, :Tn], in0=gi[:, :Tn], in1=xc[:, :Tn], op=ALU.mult), reads=[b_gi, b_xc], writes=[b_gi])
                    S.op("vector", lambda e: e.tensor_tensor(out=gi[:, :Tn], in0=gi[:, :Tn], in1=tmp[:, :Tn], op=ALU.mult), reads=[b_gi, b_tmp], writes=[b_gi])
                    init = h0[:, ct:ct + 1] if is_s else 0.0
                    S.op("vector", lambda e: e.tensor_tensor_scan(out=hh[:, :Tn], data0=gr[:, :Tn], data1=gi[:, :Tn], initial=init, op0=ALU.mult, op1=ALU.add),
                         reads=[b_gr, b_gi, b_h0], writes=[b_hh])
                    S.op("gpsimd", lambda e: e.tensor_tensor(out=tmp[:, :Tn], in0=gb[:, :Tn], in1=gb[:, :Tn], op=ALU.mult), reads=[b_gb], writes=[b_tmp])
                    S.op("gpsimd", lambda e: e.tensor_scalar(out=tmp[:, :Tn], in0=tmp[:, :Tn], scalar1=0.044715, scalar2=1.0, op0=ALU.mult, op1=ALU.add),
                         reads=[b_tmp], writes=[b_tmp])
                    S.op("gpsimd", lambda e: e.tensor_tensor(out=tmp[:, :Tn], in0=tmp[:, :Tn], in1=gb[:, :Tn], op=ALU.mult), reads=[b_tmp, b_gb], writes=[b_tmp])
                    S.op("scalar", lambda e: e.activation(out=tmp[:, :Tn], in_=tmp[:, :Tn], func=AF.Sigmoid, scale=1.5957691216057308), reads=[b_tmp], writes=[b_tmp])
                    S.op("gpsimd", lambda e: e.tensor_tensor(out=tmp[:, :Tn], in0=tmp[:, :Tn], in1=gb[:, :Tn], op=ALU.mult), reads=[b_tmp, b_gb], writes=[b_tmp])
                    S.op("vector", lambda e: e.tensor_tensor(out=yb[:, :Tn], in0=tmp[:, :Tn], in1=hh[:, :Tn], op=ALU.mult), reads=[b_tmp, b_hh], writes=[b_yb])
                    S.dma("sync", X["yT"][RW + ct * 128:RW + (ct + 1) * 128, col0:col0 + Tn], yb[:, :Tn], reads=[b_yb])
                    oh = O["lruh_s"] if is_s else O["lruh_p"]
                    oc = O["conv_s"] if is_s else O["conv_p"]
                    S.dma("sync", bass.AP(tensor=oh.tensor, offset=oh[l, ct * 128:(ct + 1) * 128].offset, ap=[[1, 128], [1, 1]]),
                          hh[:, Tn - 1:Tn], reads=[b_hh], **nck)
                    S.dma("sync", bass.AP(tensor=oc.tensor, offset=oc[l, 0, ct * 128:(ct + 1) * 128].offset, ap=[[1, 128], [LW, 3]]),
                          xpad[:, Tn:Tn + 3], reads=[b_xpad], **nck)
            S.barrier()

    def vv(self, out, in0, in1, op, R, W, eng="vector"):
        return self.S.op(eng, lambda e: e.tensor_tensor(out=out, in0=in0, in1=in1, op=op), reads=R, writes=W)

    def vs(self, out, in0, s1, s2, op0, op1, R, W, eng="vector"):
        if op1 is None:
            return self.S.op(eng, lambda e: e.tensor_scalar(out=out, in0=in0, scalar1=s1, scalar2=None, op0=op0), reads=R, writes=W)
        return self.S.op(eng, lambda e: e.tensor_scalar(out=out, in0=in0, scalar1=s1, scalar2=s2, op0=op0, op1=op1), reads=R, writes=W)

    def stt(self, out, in0, scalar, in1, op0, op1, R, W):
        return self.S.op("vector", lambda e: e.scalar_tensor_tensor(out=out, in0=in0, scalar=scalar, in1=in1, op0=op0, op1=op1), reads=R, writes=W)

    def act(self, out, in_, func, R, W, **kw):
        return self.S.op("scalar", lambda e: e.activation(out=out, in_=in_, func=func, **kw), reads=R, writes=W)

    def cp(self, out, in_, R, W, eng="vector"):
        if eng == "scalar":
            return self.S.op("scalar", lambda e: e.copy(out=out, in_=in_), reads=R, writes=W)
        return self.S.op(eng, lambda e: e.tensor_copy(out=out, in_=in_), reads=R, writes=W)

    def pe_fence(self):
        self.S.op("tensor", lambda e: e.matmul(self.ps_dummy[0:1, 0:1], lhsT=self.identb[:, 0:1], rhs=self.identb[:, 0:1], start=True, stop=True),
                  reads=[self.b_ident], writes=[], inc=True)

    def mm(self, out, lhsT, rhs, R, W, start=True, stop=True, inc=None, f32r=False):
        if inc is None:
            inc = stop
        guard = inc and (lhsT.dtype == F32) and not (f32r and self.stages.get('nofence_r', True))
        if f32r:
            lhsT = lhsT.bitcast(F32R)
            rhs = rhs.bitcast(F32R)
        t = self.S.op("tensor", lambda e: e.matmul(out, lhsT=lhsT, rhs=rhs, start=start, stop=stop), reads=R, writes=W, inc=(inc and not guard))
        if guard:
            self.pe_fence()
        return t

    def tr(self, out, in_, ident, R, W, inc=True):
        guard = inc and (in_.dtype == F32)
        t = self.S.op("tensor", lambda e: e.transpose(out=out, in_=in_, identity=ident), reads=list(R) + [self.b_ident], writes=W, inc=(inc and not guard))
        if guard:
            self.pe_fence()
        return t

    def getps(self):
        i = self.ps_next
        self.ps_next = (i + 1) % len(self.ps_pool)
        return self.ps_pool[i], self.ps_bufs[i]

    @staticmethod
    def bc_last(a, k):
        return bass.AP(tensor=a.tensor, offset=a.offset, ap=[list(x) for x in a.ap] + [[0, k]])

    def stage_rwkv(self, l):
        nc, S = self.nc, self.S
        I, O, X = self.I, self.O, self.X
        vv, vs, stt, act, cp, mm, tr = self.vv, self.vs, self.stt, self.act, self.cp, self.mm, self.tr
        nck = dict(allow_slow_non_contiguous=True)
        with contextlib.ExitStack() as es:
            sb = lambda n, s, d: es.enter_context(self.sbt(n, s, d))
            allps = [es.enter_context(self.pst(f"r_ps{j}", [128, 512], F32)) for j in range(7)]
            allpb = [PBuf() for _ in range(7)]
            own_ps, own_pb = allps[0:4], allpb[0:4]
            self.ps_pool = allps[4:7]
            self.ps_bufs = allpb[4:7]
            self.ps_next = 0
            self.ps_dummy = es.enter_context(self.pst("r_psd", [128, 512], F32))
            mu = sb("r_mu", [128, 14], F32)
            lup = sb("r_lup", [128, 512], BF16)
            gup = sb("r_gup", [128, 512], BF16)
            bpar = sb("r_bpar", [128, 7, 512], F32)
            tri = sb("r_tri", [128, 128], F32)
            stri = sb("r_stri", [128, 128], F32)
            ltri = sb("r_ltri", [128, 128], F32)
            ones = sb("r_ones", [128, 1], F32)
            ST = sb("r_ST", [64, 8, 64], F32)
            STd = sb("r_STd", [64, 8, 64], F32)
            STr = sb("r_STr", [64, 8, 64], F32); b_STr_all = [Buf() for _ in range(8)]
            b_par = Buf(); b_STall = [Buf() for _ in range(8)]; b_STd_all = [Buf() for _ in range(8)]
            S.dma("sync", mu[:, :], self.fm_vec(I["rwkv_mu"][l, :], 14), writes=[b_par], **nck)
            S.dma("gpsimd", lup[0:64, :], I["rwkv_w_up"][l, :, :], writes=[b_par])
            S.dma("gpsimd", lup[64:128, :], I["rwkv_a_up"][l, :, :], writes=[b_par])
            S.dma("gpsimd", gup[:, :], I["rwkv_g_up"][l, :, :], writes=[b_par])
            for i, nm in enumerate(("rwkv_w0", "rwkv_a0", "rwkv_k_k", "rwkv_k_a", "rwkv_r_k", "rwkv_lnx_g", "rwkv_lnx_b")):
                S.dma("sync", bpar[:, i, :], dram_bcast(I[nm][l:l + 1, :], 128, RW), writes=[b_par])
            S.dma("sync", tri[:, :], I["c_tri"][:, :], writes=[b_par])
            S.dma("sync", stri[:, :], I["c_stri"][:, :], writes=[b_par])
            S.dma("sync", ltri[:, :], I["c_ltri"][:, :], writes=[b_par])
            S.op("vector", lambda e: e.memset(ones[:], 1.0), writes=[b_par])
            W0, A0, KK_, KA_, RK_, LG_, LB_ = (bpar[:, i, :] for i in range(7))

            def T2(name, shape, dt=F32, single=False):
                if single:
                    t_ = sb(name, shape, dt); b_ = Buf()
                    return [t_, t_], [b_, b_]
                return [sb(f"{name}{j}", shape, dt) for j in range(2)], [Buf(), Buf()]

            pT, b_pT = T2("r_pT", [128, 14, 129])
            mT, b_mT = T2("r_mT", [128, 14, 128], single=True)
            dT, b_dT = T2("r_dT", [128, 14, 128], single=True)
            lin, b_lin = T2("r_lin", [128, 2, 128], BF16)
            names = ["r", "k", "v", "g", "a", "kk", "k2", "nlw", "epos", "eneg", "eprev", "ah", "bh", "kh", "rh", "t1", "t2", "y", "bon", "vr"]
            tm = {}; b_tm = {}
            for nm in names:
                tm[nm], b_tm[nm] = T2("r_tm_" + nm, [128, 512], single=(nm in ("kk", "k2", "epos", "eneg", "eprev", "t1", "t2", "vr")))
            sm, b_sm = T2("r_sm", [128, 8, 8])
            fmT, b_fmT = T2("r_fmT", [64, 8, 4, 128])
            pc, b_pc = T2("r_pc", [64, 8])
            yb, b_yb = T2("r_yb", [128, 512], BF16)
            ybT, b_ybT = T2("r_ybT", [128, 4, 128], BF16)
            NH = 4
            Am = [sb(f"r_Am{j}", [128, 4, 128], F32) for j in range(NH)]; b_Am = [Buf() for _ in range(NH)]
            Mx = [sb(f"r_Mx{j}", [128, 2, 128], F32) for j in range(NH)]; b_Mx = [Buf() for _ in range(NH)]
            Nx = [sb(f"r_Nx{j}", [128, 2, 128], F32) for j in range(NH)]; b_Nx = [Buf() for _ in range(NH)]
            Tt = [sb(f"r_Tt{j}", [128, 2, 128], F32) for j in range(NH)]; b_Tt = [Buf() for _ in range(NH)]
            akv = [sb(f"r_akv{j}", [128, 64], F32) for j in range(NH)]; b_akv = [Buf() for _ in range(NH)]
            wmT = [sb(f"r_wmT{j}", [64, 128], F32) for j in range(NH)]; b_wmT = [Buf() for _ in range(NH)]
            U = [sb(f"r_U{j}", [128, 64], F32) for j in range(NH)]; b_U = [Buf() for _ in range(NH)]
            sinit = sb("r_sinit", [64, 8, 64], F32); b_sinit = Buf()
            sout = sinit; b_sout = b_sinit
            hcnt = [0]

            for (col0, Tn, is_s) in ((0, SEQ, False), (SEQ, NS, True)):
                if is_s and not self.stages.get('rwkv_sample', True):
                    continue
                if is_s:
                    S.dma("sync", sinit[:, :, :], I["st_wkv"][l].rearrange("h v k -> v h k"), writes=[b_sinit])
                    for h in range(RH):
                        ps, bp = self.getps()
                        tr(ps[:64, :64], sinit[:, h, :], self.identf[:64, :64], [b_sinit], [bp])
                        cp(ST[:, h, :], ps[:64, :64], [bp], [b_STall[h]])
                else:
                    S.op("vector", lambda e: e.memset(ST[:], 0.0), writes=b_STall)
                cp((STr[:, :, :].bitcast(F32R) if self.stages.get('fp32r', True) else STr[:, :, :]), ST[:, :, :], b_STall, b_STr_all)
                nchunks = (Tn + 127) // 128
                if self.stages.get("rwkv_chunks"):
                    nchunks = min(nchunks, self.stages["rwkv_chunks"])
                for ci in range(nchunks):
                    j = ci % 2
                    t0 = col0 + ci * 128
                    n = min(128, Tn - ci * 128)
                    B = {k_: b_tm[k_][j] for k_ in names}
                    use_r = (n == 128) and self.stages.get('fp32r', True)
                    R_ = (lambda a_: a_.bitcast(F32R)) if self.stages.get('fp32r', True) else (lambda a_: a_)
                    Tm = {k_: tm[k_][j] for k_ in names}
                    if ci == 0:
                        S.dma("sync", pT[j][:, :, 1:n + 1], X["projT"][0:RPROJ, t0:t0 + n].rearrange("(f p) t -> p f t", p=128), writes=[b_pT[j]])
                        if is_s:
                            S.dma("sync", pT[j][:, :, 0], self.fm_vec(I["st_shift"][l, :], 14), writes=[b_pT[j]], **nck)
                        else:
                            S.op("vector", lambda e: e.memset(pT[j][:, :, 0:1], 0.0), writes=[b_pT[j]])
                    else:
                        S.dma("sync", pT[j][:, :, 0:n + 1], X["projT"][0:RPROJ, t0 - 1:t0 + n].rearrange("(f p) t -> p f t", p=128), writes=[b_pT[j]])
                    if ci == nchunks - 1:
                        osh = O["shift_s"] if is_s else O["shift_p"]
                        S.dma("sync", self.fm_vec(osh[l, :], 14), pT[j][:, :, n], reads=[b_pT[j]], **nck)
                    if self.stages.get('rwkv_upto', 9) < 2:
                        continue
                    pcur = pT[j][:, :, 1:n + 1]
                    vv(dT[j][:, :, :n], pT[j][:, :, 0:n], pcur, ALU.subtract, [b_pT[j]], [b_dT[j]])
                    vv(dT[j][:, :, :n], dT[j][:, :, :n], self.bc_last(mu[:, :], n), ALU.mult, [b_dT[j], b_par], [b_dT[j]])
                    vv(mT[j][:, :, :n], dT[j][:, :, :n], pcur, ALU.add, [b_dT[j], b_pT[j]], [b_mT[j]])
                    if self.stages.get('rwkv_upto', 9) < 3:
                        continue
                    act(lin[j][0:64, 0, :n], mT[j][0:64, 12, :n], AF.Tanh, [b_mT[j]], [b_lin[j]])
                    cp(lin[j][64:128, 0, :n], mT[j][64:128, 12, :n], [b_mT[j]], [b_lin[j]], eng="gpsimd")
                    act(lin[j][:, 1, :n], mT[j][:, 13, :n], AF.Sigmoid, [b_mT[j]], [b_lin[j]])
                    ps_w, bp_w = self.getps()
                    mm(ps_w[:n, :], lin[j][0:64, 0, :n], lup[0:64, :], [b_lin[j], b_par], [bp_w])
                    ps_a, bp_a = self.getps()
                    mm(ps_a[:n, :], lin[j][64:128, 0, :n], lup[64:128, :], [b_lin[j], b_par], [bp_a])
                    ps_g, bp_g = self.getps()
                    mm(ps_g[:n, :], lin[j][:, 1, :n], gup[:, :], [b_lin[j], b_par], [bp_g])
                    vv(Tm["t1"][:n, :], ps_w[:n, :], W0[:n, :], ALU.add, [bp_w, b_par], [B["t1"]])
                    vv(Tm["a"][:n, :], ps_a[:n, :], A0[:n, :], ALU.add, [bp_a, b_par], [B["a"]])
                    cp(Tm["g"][:n, :], ps_g[:n, :], [bp_g], [B["g"]], eng="scalar")
                    if self.stages.get('rwkv_upto', 9) < 4:
                        continue
                    for ti_, nm in enumerate(("r", "k", "v")):
                        ps, bp = self.getps()
                        for q in range(4):
                            tr(ps[:n, q * 128:(q + 1) * 128], mT[j][:, ti_ * 4 + q, :n], self.identf[:, :], [b_mT[j]], [bp], inc=(q == 3))
                        cp(Tm[nm][:n, :], ps[:n, :], [bp], [B[nm]], eng=("scalar" if ti_ == 1 else "vector"))
                    cp(R_(Tm["vr"][:n, :]), Tm["v"][:n, :], [B["v"]], [B["vr"]], eng="scalar")
                    if self.stages.get('rwkv_upto', 9) < 5:
                        continue
                    act(Tm["t1"][:n, :], Tm["t1"][:n, :], AF.Exp, [B["t1"]], [B["t1"]], scale=-1.0)
                    act(Tm["t1"][:n, :], Tm["t1"][:n, :], AF.Ln, [B["t1"]], [B["t1"]], bias=1.0)
                    vs(Tm["t1"][:n, :], Tm["t1"][:n, :], -1.0, -0.5, ALU.mult, ALU.add, [B["t1"]], [B["t1"]])
                    act(Tm["nlw"][:n, :], Tm["t1"][:n, :], AF.Exp, [B["t1"]], [B["nlw"]])
                    act(Tm["a"][:n, :], Tm["a"][:n, :], AF.Sigmoid, [B["a"]], [B["a"]])
                    vv(Tm["kk"][:n, :], Tm["k"][:n, :], KK_[:n, :], ALU.mult, [B["k"], b_par], [B["kk"]])
                    vv(Tm["t2"][:n, :], Tm["kk"][:n, :], Tm["kk"][:n, :], ALU.mult, [B["kk"]], [B["t2"]], eng="gpsimd")
                    S.op("vector", lambda e: e.reduce_sum(out=sm[j][:n, 0, :], in_=Tm["t2"][:n, :].rearrange("p (h k) -> p h k", h=8), axis=AX.X),
                         reads=[B["t2"]], writes=[b_sm[j]])
                    act(sm[j][:n, 0, :], sm[j][:n, 0, :], AF.Sqrt, [b_sm[j]], [b_sm[j]])
                    vs(sm[j][:n, 0, :], sm[j][:n, 0, :], 1e-12, None, ALU.max, None, [b_sm[j]], [b_sm[j]])
                    S.op("vector", lambda e: e.reciprocal(out=sm[j][:n, 1, :], in_=sm[j][:n, 0, :]), reads=[b_sm[j]], writes=[b_sm[j]])
                    vv(Tm["kk"][:n, :].rearrange("p (h k) -> p h k", h=8), Tm["kk"][:n, :].rearrange("p (h k) -> p h k", h=8),
                       self.bc_last(sm[j][:n, 1, :], 64), ALU.mult, [B["kk"], b_sm[j]], [B["kk"]])
                    stt(Tm["t2"][:n, :], Tm["a"][:n, :], -1.0, KA_[:n, :], ALU.add, ALU.mult, [B["a"], b_par], [B["t2"]])
                    stt(Tm["k2"][:n, :], Tm["t2"][:n, :], 1.0, Tm["k"][:n, :], ALU.add, ALU.mult, [B["t2"], B["k"]], [B["k2"]])
                    ps_c, bp_c = self.getps()
                    mm(ps_c[:n, :], tri[:n, :n], Tm["nlw"][:n, :], [b_par, B["nlw"]], [bp_c])
                    act(Tm["epos"][:n, :], ps_c[:n, :], AF.Exp, [bp_c], [B["epos"]], scale=-1.0)
                    act(Tm["eneg"][:n, :], ps_c[:n, :], AF.Exp, [bp_c], [B["eneg"]])
                    vv(Tm["t1"][:n, :], ps_c[:n, :], Tm["nlw"][:n, :], ALU.subtract, [bp_c, B["nlw"]], [B["t1"]])
                    act(Tm["eprev"][:n, :], Tm["t1"][:n, :], AF.Exp, [B["t1"]], [B["eprev"]], scale=-1.0)
                    stt(Tm["ah"][:n, :], Tm["kk"][:n, :], -1.0, Tm["eprev"][:n, :], ALU.mult, ALU.mult, [B["kk"], B["eprev"]], [B["ah"]])
                    vv(Tm["bh"][:n, :], Tm["kk"][:n, :], Tm["a"][:n, :], ALU.mult, [B["kk"], B["a"]], [B["bh"]], eng="gpsimd")
                    vv(Tm["bh"][:n, :], Tm["bh"][:n, :], Tm["eneg"][:n, :], ALU.mult, [B["bh"], B["eneg"]], [B["bh"]], eng="gpsimd")
                    vv(Tm["kh"][:n, :], Tm["k2"][:n, :], Tm["eneg"][:n, :], ALU.mult, [B["k2"], B["eneg"]], [B["kh"]])
                    vv(Tm["rh"][:n, :], Tm["r"][:n, :], Tm["epos"][:n, :], ALU.mult, [B["r"], B["epos"]], [B["rh"]], eng="gpsimd")
                    vv(Tm["t2"][:n, :], Tm["r"][:n, :], Tm["k2"][:n, :], ALU.mult, [B["r"], B["k2"]], [B["t2"]], eng="gpsimd")
                    vv(Tm["t2"][:n, :], Tm["t2"][:n, :], RK_[:n, :], ALU.mult, [B["t2"], b_par], [B["t2"]], eng="gpsimd")
                    S.op("vector", lambda e: e.reduce_sum(out=sm[j][:n, 2, :], in_=Tm["t2"][:n, :].rearrange("p (h k) -> p h k", h=8), axis=AX.X),
                         reads=[B["t2"]], writes=[b_sm[j]])
                    vv(Tm["bon"][:n, :].rearrange("p (h k) -> p h k", h=8), Tm["v"][:n, :].rearrange("p (h k) -> p h k", h=8),
                       self.bc_last(sm[j][:n, 2, :], 64), ALU.mult, [B["v"], b_sm[j]], [B["bon"]])
                    if self.stages.get('rwkv_upto', 9) < 6:
                        continue
                    for h in range(RH):
                        ps, bp = self.getps()
                        for q, nm in enumerate(("ah", "rh", "bh", "kh")):
                            tr(ps[:64, q * 128:q * 128 + n], Tm[nm][:n, h * 64:(h + 1) * 64], self.identf[:n, :n], [B[nm]], [bp], inc=(q == 3))
                        cp(R_(fmT[j][:, h, :, :n]), ps[:64, :].rearrange("p (q t) -> p q t", q=4)[:, :, :n], [bp], [b_fmT[j]],
                           eng=("scalar" if h % 2 == 0 else "vector"))
                    ps_p, bp_p = self.getps()
                    for h in range(RH):
                        mm(ps_p[:64, h:h + 1], Tm["nlw"][:n, h * 64:(h + 1) * 64], ones[:n, :], [B["nlw"], b_par], [bp_p], inc=(h == RH - 1))
                    act(pc[j][:, :], ps_p[:64, 0:8], AF.Exp, [bp_p], [b_pc[j]], scale=-1.0)
                    if self.stages.get('rwkv_upto', 9) < 7:
                        continue
                    nsq = max(0, int(np.ceil(np.log2(max(n, 2)))) - 1)
                    if True:
                        def head_gen(h, hj):
                            aT = fmT[j][:, h, 0, :n]; rT = fmT[j][:, h, 1, :n]; bT = fmT[j][:, h, 2, :n]; kT = fmT[j][:, h, 3, :n]
                            arT = fmT[j][:, h, 0:2, :n]
                            v_h = Tm["v"][:n, h * 64:(h + 1) * 64]
                            vr_h = Tm["vr"][:n, h * 64:(h + 1) * 64]
                            ps, bp = (own_ps[h % 4], own_pb[h % 4])
                            o4 = ps[:n, :].rearrange("p (q t) -> p q t", q=4)
                            if not self.stages.get('dbg_nomm'):
                                mm(o4[:, 0, :n], bT, aT, [b_fmT[j]], [bp], inc=False, f32r=use_r)
                                mm(o4[:, 1, :n], bT, rT, [b_fmT[j]], [bp], inc=False, f32r=use_r)
                                mm(o4[:, 2, :n], kT, aT, [b_fmT[j]], [bp], inc=False, f32r=use_r)
                                mm(o4[:, 3, :n], kT, rT, [b_fmT[j]], [bp], f32r=use_r)
                            if self.stages.get('dbg_nomask'):
                                return
                            vv(R_(Am[hj][:n, 0, :n]), o4[:, 0, :n], stri[:n, :n], ALU.mult, [bp, b_par], [b_Am[hj]])
                            vv(R_(Am[hj][:n, 1, :n]), o4[:, 1, :n], tri[:n, :n], ALU.mult, [bp, b_par], [b_Am[hj]])
                            vv(R_(Am[hj][:n, 2, :n]), o4[:, 2, :n], stri[:n, :n], ALU.mult, [bp, b_par], [b_Am[hj]])
                            vv(R_(Am[hj][:n, 3, :n]), o4[:, 3, :n], tri[:n, :n], ALU.mult, [bp, b_par], [b_Am[hj]])
                            if self.stages.get('rwkv_sub', 9) < 1:
                                return
                            yield
                            ps2, bp2 = (own_ps[h % 4], own_pb[h % 4])
                            mm(ps2[:n, :n], aT, bT, [b_fmT[j]], [bp2], f32r=use_r)
                            vv(R_(Nx[hj][:n, 0, :n]), ps2[:n, :n], ltri[:n, :n], ALU.mult, [bp2, b_par], [b_Nx[hj]])
                            if self.stages.get('rwkv_sub', 9) < 2:
                                return
                            yield
                            vv(R_(Tt[hj][:n, 0, :n]), Am[hj][:n, 0, :n], self.identf[:n, :n], ALU.add, [b_Am[hj], self.b_ident], [b_Tt[hj]])
                            cp(R_(Mx[hj][:n, 0, :n]), Am[hj][:n, 0, :n], [b_Am[hj]], [b_Mx[hj]], eng="scalar")
                            cur = 0
                            for it in range(nsq):
                                nxt = 1 - cur
                                ps3, bp3 = (own_ps[h % 4], own_pb[h % 4])
                                mm(ps3[:n, 0:n], Nx[hj][:n, cur, :n], Mx[hj][:n, cur, :n], [b_Nx[hj], b_Mx[hj]], [bp3], inc=False, f32r=use_r)
                                mm(ps3[:n, 128:128 + n], Mx[hj][:n, cur, :n], Nx[hj][:n, cur, :n], [b_Nx[hj], b_Mx[hj]], [bp3], f32r=use_r)
                                yield
                                cp(R_(Mx[hj][:n, nxt, :n]), ps3[:n, 0:n], [bp3], [b_Mx[hj]], eng="scalar")
                                cp(R_(Nx[hj][:n, nxt, :n]), ps3[:n, 128:128 + n], [bp3], [b_Nx[hj]], eng="vector")
                                ps4, bp4 = (own_ps[h % 4], own_pb[h % 4])
                                mm(ps4[:n, :n], Nx[hj][:n, nxt, :n], Tt[hj][:n, cur, :n], [b_Nx[hj], b_Tt[hj]], [bp4], f32r=use_r)
                                yield
                                vv(R_(Tt[hj][:n, nxt, :n]), ps4[:n, :n], Tt[hj][:n, cur, :n], ALU.add, [bp4, b_Tt[hj]], [b_Tt[hj]])
                                yield
                                cur = nxt
                            if self.stages.get('rwkv_sub', 9) < 3:
                                return
                            TT_ = Tt[hj][:n, cur, :n]
                            ps5, bp5 = (own_ps[h % 4], own_pb[h % 4])
                            mm(ps5[:n, 0:64], Am[hj][:n, 2, :n], vr_h, [b_Am[hj], B["vr"]], [bp5], f32r=use_r)
                            yield
                            cp(R_(akv[hj][:n, :]), ps5[:n, 0:64], [bp5], [b_akv[hj]], eng="scalar")
                            ps6, bp6 = (own_ps[h % 4], own_pb[h % 4])
                            mm(ps6[:64, :n], Tm["ah"][:n, h * 64:(h + 1) * 64], TT_, [B["ah"], b_Tt[hj]], [bp6])
                            cp(R_(wmT[hj][:, :n]), ps6[:64, :n], [bp6], [b_wmT[hj]], eng="vector")
                            if self.stages.get('rwkv_sub', 9) < 4:
                                return
                            yield
                            ps7, bp7 = (own_ps[h % 4], own_pb[h % 4])
                            mm(ps7[:n, 0:64], TT_, akv[hj][:n, :], [b_Tt[hj], b_akv[hj]], [bp7], start=True, stop=False, inc=False, f32r=use_r)
                            mm(ps7[:n, 0:64], wmT[hj][:, :n], STr[:, h, :], [b_wmT[hj], b_STr_all[h]], [bp7], start=False, stop=True, f32r=use_r)
                            yield
                            cp(R_(U[hj][:n, :]), ps7[:n, 0:64], [bp7], [b_U[hj]], eng="scalar")
                            if self.stages.get('rwkv_sub', 9) < 5:
                                return
                            yield
                            ps8, bp8 = (own_ps[h % 4], own_pb[h % 4])
                            mm(ps8[:n, 0:64], rT, STr[:, h, :], [b_fmT[j], b_STr_all[h]], [bp8], start=True, stop=False, inc=False, f32r=use_r)
                            mm(ps8[:n, 0:64], Am[hj][:n, 1, :n], U[hj][:n, :], [b_Am[hj], b_U[hj]], [bp8], start=False, stop=False, inc=False, f32r=use_r)
                            mm(ps8[:n, 0:64], Am[hj][:n, 3, :n], vr_h, [b_Am[hj], B["vr"]], [bp8], start=False, stop=True, f32r=use_r)
                            yield
                            cp(Tm["y"][:n, h * 64:(h + 1) * 64], ps8[:n, 0:64], [bp8], [B["y"]], eng="vector")
                            if self.stages.get('rwkv_sub', 9) < 6:
                                return
                            vs(STd[:, h, :], ST[:, h, :], pc[j][:, h:h + 1], None, ALU.mult, None, [b_STall[h], b_pc[j]], [b_STd_all[h]], eng="gpsimd")
                            ps9, bp9 = (own_ps[h % 4], own_pb[h % 4])
                            mm(ps9[:64, 0:64], Tm["bh"][:n, h * 64:(h + 1) * 64], U[hj][:n, :], [B["bh"], b_U[hj]], [bp9], start=True, stop=False, inc=False)
                            mm(ps9[:64, 0:64], Tm["kh"][:n, h * 64:(h + 1) * 64], v_h, [B["kh"], B["v"]], [bp9], start=False, stop=True)
                            yield
                            stt(ST[:, h, :], ps9[:64, 0:64], pc[j][:, h:h + 1], STd[:, h, :], ALU.mult, ALU.add, [bp9, b_pc[j], b_STd_all[h]], [b_STall[h]])
                            cp(R_(STr[:, h, :]), ST[:, h, :], [b_STall[h]], [b_STr_all[h]], eng="scalar")
                            yield
                        for grp in range(2):
                            gens = [head_gen(h, h % NH) for h in range(grp * 4, grp * 4 + 4)]
                            if not self.stages.get("rwkv_interleave", True):
                                for g_ in gens:
                                    for _ in g_:
                                        pass
                                gens = []
                            while gens:
                                alive = []
                                for g_ in gens:
                                    try:
                                        next(g_)
                                        alive.append(g_)
                                    except StopIteration:
                                        pass
                                gens = alive
                    if self.stages.get('rwkv_upto', 9) < 8:
                        continue
                    y3 = Tm["y"][:n, :].rearrange("p (h k) -> p h k", h=8)
                    S.op("vector", lambda e: e.reduce_sum(out=sm[j][:n, 3, :], in_=y3, axis=AX.X), reads=[B["y"]], writes=[b_sm[j]])
                    vs(sm[j][:n, 3, :], sm[j][:n, 3, :], 1.0 / 64, None, ALU.mult, None, [b_sm[j]], [b_sm[j]])
                    vv(y3, y3, self.bc_last(sm[j][:n, 3, :], 64), ALU.subtract, [B["y"], b_sm[j]], [B["y"]])
                    vv(Tm["t1"][:n, :], Tm["y"][:n, :], Tm["y"][:n, :], ALU.mult, [B["y"]], [B["t1"]], eng="gpsimd")
                    S.op("vector", lambda e: e.reduce_sum(out=sm[j][:n, 4, :], in_=Tm["t1"][:n, :].rearrange("p (h k) -> p h k", h=8), axis=AX.X),
                         reads=[B["t1"]], writes=[b_sm[j]])
                    vs(sm[j][:n, 4, :], sm[j][:n, 4, :], 1.0 / 64, 64e-5, ALU.mult, ALU.add, [b_sm[j]], [b_sm[j]])
                    act(sm[j][:n, 4, :], sm[j][:n, 4, :], AF.Sqrt, [b_sm[j]], [b_sm[j]])
                    S.op("vector", lambda e: e.reciprocal(out=sm[j][:n, 5, :], in_=sm[j][:n, 4, :]), reads=[b_sm[j]], writes=[b_sm[j]])
                    vv(y3, y3, self.bc_last(sm[j][:n, 5, :], 64), ALU.mult, [B["y"], b_sm[j]], [B["y"]])
                    vv(Tm["y"][:n, :], Tm["y"][:n, :], LG_[:n, :], ALU.mult, [B["y"], b_par], [B["y"]])
                    vv(Tm["y"][:n, :], Tm["y"][:n, :], LB_[:n, :], ALU.add, [B["y"], b_par], [B["y"]])
                    vv(Tm["y"][:n, :], Tm["y"][:n, :], Tm["bon"][:n, :], ALU.add, [B["y"], B["bon"]], [B["y"]])
                    vv(yb[j][:n, :], Tm["y"][:n, :], Tm["g"][:n, :], ALU.mult, [B["y"], B["g"]], [b_yb[j]])
                    pst, bpt = self.getps()
                    pstb = pst[:, :].bitcast(BF16)
                    for q in range(4):
                        tr(pstb[:, q * 128:q * 128 + n], yb[j][:n, q * 128:(q + 1) * 128], self.identb[:n, :n], [b_yb[j]], [bpt], inc=(q == 3))
                    cp(ybT[j][:, :, :n], pstb[:, 0:512].rearrange("p (q t) -> p q t", q=4)[:, :, :n], [bpt], [b_ybT[j]], eng="scalar")
                    S.dma("sync", X["yT"][0:RW, t0:t0 + n].rearrange("(q p) t -> p q t", p=128), ybT[j][:, :, :n], reads=[b_ybT[j]])
                for h in range(RH):
                    ps, bp = self.getps()
                    tr(ps[:64, :64], ST[:, h, :], self.identf[:64, :64], [b_STall[h]], [bp])
                    cp(sout[:, h, :], ps[:64, :64], [bp], [b_sout])
                ow = O["wkv_s"] if is_s else O["wkv_p"]
                S.dma("sync", ow[l].rearrange("h v k -> v h k"), sout[:, :, :], reads=[b_sout])
            S.barrier()

    @staticmethod
    def strided(t3, h, start, step, count):
        base = t3[:, h, start:start + 1]
        return bass.AP(tensor=base.tensor, offset=base.offset, ap=[list(base.ap[0]), [step, count]])

    def stage_attn(self, l):
        nc, S = self.nc, self.S
        I, O, X = self.I, self.O, self.X
        vv, vs, stt, act, cp, mm, tr = self.vv, self.vs, self.stt, self.act, self.cp, self.mm, self.tr
        nck = dict(allow_slow_non_contiguous=True)
        SC = float(AD ** -0.5)
        DILS = (1, 4, 16)
        with contextlib.ExitStack() as es:
            sb = lambda n, s, d: es.enter_context(self.sbt(n, s, d))
            self.ps_pool = [es.enter_context(self.pst(f"a_ps{j}", [128, 512], F32)) for j in range(7)]
            self.ps_bufs = [PBuf() for _ in range(7)]
            self.ps_next = 0
            self.ps_dummy = es.enter_context(self.pst("a_psd", [128, 512], F32))
            relb = sb("a_relb", [32, 8], F32); E = sb("a_E", [32, 387], F32)
            Gt = sb("a_Gt", [8, 3, 383], F32)
            Mb = sb("a_Mb", [128, 3, 8, 256], F32)
            b_c = Buf(); b_Gt = Buf(); b_Gd = Buf(); b_Mb = Buf()
            Gd = X["Gd"]
            S.dma("sync", relb[:, :], I["rel_bias"][:, :], writes=[b_c])
            S.dma("sync", E[:, :], I["c_onehot"][:, :], writes=[b_c])
            S.op("vector", lambda e: e.memset(Gt[:], NEG), writes=[b_Gt])
            ps, bp = self.getps()
            mm(ps[:8, 0:387], relb[:, :], E[:, :], [b_c], [bp])
            cp(Gt[:, :, 127:256], ps[:8, 0:387].rearrange("p (g j) -> p g j", g=3), [bp], [b_Gt])
            S.dma("sync", Gd[:, :, :], Gt[:, :, :], reads=[b_Gt], writes=[b_Gd])
            with contextlib.ExitStack() as es3:
                Mr = es3.enter_context(self.sbt("a_Mr", [128, 3, 8, 256], F32)); b_Mr = Buf()
                Jm = es3.enter_context(self.sbt("a_J", [128, 128], F32))
                S.dma("sync", Jm[:, :], I["c_antiident"][:, :], writes=[b_c])
                for g in range(3):
                    src = bass.AP(tensor=Gd.tensor, offset=Gd[0, g, 0].offset, ap=[[1, 128], [3 * 383, 8], [1, 256]])
                    S.dma("sync", Mr[:, g, :, :], src, reads=[b_Gd], writes=[b_Mr])
                for g in range(3):
                    for hp in range(4):
                        ps, bp = self.getps()
                        mm(ps[:, :], Jm[:, :], Mr[:, g, 2 * hp:2 * hp + 2, :].rearrange("p h k -> p (h k)"), [b_c, b_Mr], [bp])
                        cp(Mb[:, g, 2 * hp:2 * hp + 2, :].rearrange("p h k -> p (h k)"), ps[:, :], [bp], [b_Mb], eng=("scalar" if hp % 2 else "vector"))
                S.barrier()

            es2 = contextlib.ExitStack()
            sbp = lambda n, s_, d: es2.enter_context(self.sbt(n, s_, d))
            QT = sbp("a_QT", [128, 8, SEQ], BF16)
            KT = sbp("a_KT", [128, 8, SEQ], BF16)
            Vg = sbp("a_Vg", [128, 16, 1024], BF16)
            b_QT = Buf(); b_KT = Buf(); b_Vg = Buf()
            S.dma("sync", QT[:, :, :], X["qT"][:, 0:SEQ].rearrange("(h d) t -> d h t", d=128), writes=[b_QT])
            S.dma("sync", KT[:, :, :], X["kT"][:, 0:SEQ].rearrange("(h d) t -> d h t", d=128), writes=[b_KT])
            NR = 3
            sc = [sbp(f"a_sc{j}", [128, 2, 256], F32) for j in range(NR)]; b_sc = [Buf() for _ in range(NR)]
            Pm = [sbp(f"a_P{j}", [128, 2, 256], BF16) for j in range(NR)]; b_P = [Buf() for _ in range(NR)]
            PT = [sbp(f"a_PT{j}", [128, 2, 256], BF16) for j in range(NR)]; b_PT = [Buf() for _ in range(NR)]
            Ou = [sbp(f"a_Ou{j}", [128, 1024], F32) for j in range(2)]; b_Ou = [Buf(), Buf()]
            st = [sbp(f"a_st{j}", [128, 5, 8], F32) for j in range(2)]; b_st = [Buf(), Buf()]; b_sth = [[Buf() for _ in range(8)] for _ in range(2)]
            b_Os = Buf(); b_Ls = Buf()
            u = 0
            k = 0
            pend = []
            def flush_pend():
                while pend:
                    it_ = pend.pop(0)
                    if it_[0] is not None:
                        it_[0]()
                    it_[1]()
                    if it_[2] is not None:
                        it_[2]()
            do_prompt = self.stages.get("attn_prompt", True)
            for g, dil in enumerate(DILS):
                if not do_prompt:
                    break
                nb = 16 // dil
                flush_pend()
                vsrc = bass.AP(tensor=O["v_p"].tensor, offset=O["v_p"][l, 0, 0].offset,
                               ap=[[dil * 1024, 128], [1024, dil], [dil * 128 * 1024, nb], [1, 1024]])
                S.dma("gpsimd", Vg[:, :, :].rearrange("p (c n) f -> p c n f", c=dil), vsrc, writes=[b_Vg])
                for c in range(dil):
                    for n_ in range(nb):
                        blk = c * nb + n_
                        q0 = c + dil * 128 * n_
                        nk = 256 if n_ > 0 else 128
                        k0 = q0 - dil * 128 if n_ > 0 else q0
                        koff = 0 if n_ > 0 else 128
                        uj = u % 2
                        u += 1
                        for hp in range(AH // 2):
                            h = hp
                            h0 = 2 * hp
                            kj = k % NR
                            k += 1

                            def fA(uj=uj, kj=kj, hp=hp, h0=h0, g=g, q0=q0, k0=k0, nk=nk, koff=koff, dil=dil):
                                ps_s, bp_s = self.getps()
                                for i in range(2):
                                    mm(ps_s[:, i * 256:i * 256 + nk], self.strided(QT, h0 + i, q0, dil, 128), self.strided(KT, h0 + i, k0, dil, nk), [b_QT, b_KT], [bp_s],
                                       inc=(i == 1))
                                ps3 = ps_s[:, :].rearrange("p (i k) -> p i k", i=2)[:, :, :nk]
                                stt(sc[kj][:, :, :nk], ps3, SC, Mb[:, g, h0:h0 + 2, koff:koff + nk], ALU.mult, ALU.add, [bp_s, b_Mb], [b_sc[kj]])
                                S.op("vector", lambda e: e.reduce_max(out=st[uj][:, 0, h0:h0 + 2], in_=sc[kj][:, :, :nk], axis=AX.X),
                                     reads=[b_sc[kj]], writes=[b_sth[uj][hp]])
                                vs(st[uj][:, 1, h0:h0 + 2], st[uj][:, 0, h0:h0 + 2], -1.0, None, ALU.mult, None, [b_sth[uj][hp]], [b_sth[uj][hp]])
                                for i in range(2):
                                    act(Pm[kj][:, i, :nk], sc[kj][:, i, :nk], AF.Exp, [b_sc[kj], b_sth[uj][hp]], [b_P[kj], b_sth[uj][hp]],
                                        bias=st[uj][:, 1, h0 + i:h0 + i + 1], accum_out=st[uj][:, 2, h0 + i:h0 + i + 1])

                            def fB(kj=kj, nk=nk):
                                pst, bpt = self.getps()
                                pstb = pst[:, :].bitcast(BF16)
                                nkc = nk // 128
                                for i in range(2):
                                    for kc in range(nkc):
                                        tr(pstb[:, i * 256 + kc * 128:i * 256 + (kc + 1) * 128], Pm[kj][:, i, kc * 128:(kc + 1) * 128], self.identb[:, :], [b_P[kj]], [bpt],
                                           inc=(i == 1 and kc == nkc - 1))
                                cp(PT[kj][:, :, :nk], pstb[:, 0:512].rearrange("p (i k) -> p i k", i=2)[:, :, :nk], [bpt], [b_PT[kj]], eng="scalar")

                            def fC(uj=uj, kj=kj, hp=hp, h0=h0, blk=blk, nk=nk):
                                nkc = nk // 128
                                ps_o, bp_o = self.getps()
                                for i in range(2):
                                    for kc in range(nkc):
                                        bk = blk - (nkc - 1 - kc)
                                        mm(ps_o[:, i * 128:(i + 1) * 128], PT[kj][:, i, kc * 128:(kc + 1) * 128], Vg[:, bk, (h0 + i) * 128:(h0 + i + 1) * 128], [b_PT[kj], b_Vg], [bp_o],
                                           start=(kc == 0), stop=(kc == nkc - 1), inc=(i == 1 and kc == nkc - 1))
                                S.op("vector", lambda e: e.reciprocal(out=st[uj][:, 3, h0:h0 + 2], in_=st[uj][:, 2, h0:h0 + 2]), reads=[b_sth[uj][hp]], writes=[b_sth[uj][hp]])
                                vv(Ou[uj][:, h0 * 128:(h0 + 2) * 128].rearrange("p (i d) -> p i d", i=2), ps_o[:, 0:256].rearrange("p (i d) -> p i d", i=2),
                                   self.bc_last(st[uj][:, 3, h0:h0 + 2], 128), ALU.mult, [bp_o, b_sth[uj][hp]], [b_Ou[uj]])

                            fE = None
                            if hp == AH // 2 - 1:
                                def fE(uj=uj, g=g, q0=q0, dil=dil):
                                    act(st[uj][:, 4, :], st[uj][:, 2, :], AF.Ln, b_sth[uj], b_sth[uj])
                                    vv(st[uj][:, 4, :], st[uj][:, 4, :], st[uj][:, 0, :], ALU.add, b_sth[uj], b_sth[uj])
                                    dO = bass.AP(tensor=X["Os"].tensor, offset=X["Os"][g, q0, 0].offset, ap=[[dil * 1024, 128], [1, 1024]])
                                    S.dma("sync", dO, Ou[uj][:, :], reads=[b_Ou[uj]], writes=[b_Os])
                                    dL = bass.AP(tensor=X["Ls"].tensor, offset=X["Ls"][g, q0, 0].offset, ap=[[dil * 8, 128], [1, 8]])
                                    S.dma("sync", dL, st[uj][:, 4, :], reads=b_sth[uj], writes=[b_Ls])
                            fA()
                            pend.append([fB, fC, fE])
                            if len(pend) >= 2 and pend[-2][0] is not None:
                                pend[-2][0](); pend[-2][0] = None
                            if len(pend) >= 3:
                                it_ = pend.pop(0)
                                it_[1]()
                                if it_[2] is not None:
                                    it_[2]()
            flush_pend()
            Om = [sbp(f"a_Om{j}", [128, 3, 1024], F32) for j in range(2)]; b_Om = [Buf(), Buf()]
            Lm = [sbp(f"a_Lm{j}", [128, 6, 8], F32) for j in range(2)]; b_Lm = [Buf(), Buf()]
            wg = [sbp(f"a_wg{j}", [128, 3, 8], F32) for j in range(2)]; b_wg = [Buf(), Buf()]
            ym = [sbp(f"a_ym{j}", [128, 1024], F32) for j in range(2)]; b_ym = [Buf(), Buf()]
            ymb = [sbp(f"a_ymb{j}", [128, 1024], BF16) for j in range(2)]; b_ymb = [Buf(), Buf()]
            yT_ = [sbp(f"a_yT{j}", [128, 8, 128], BF16) for j in range(2)]; b_yT = [Buf(), Buf()]
            for ti in range(SEQ // 128 if do_prompt else 0):
                j = ti % 2
                t0 = ti * 128
                for g in range(3):
                    S.dma("sync", Om[j][:, g, :], X["Os"][g, t0:t0 + 128, :], reads=[b_Os], writes=[b_Om[j]])
                    S.dma("sync", Lm[j][:, g, :], X["Ls"][g, t0:t0 + 128, :], reads=[b_Ls], writes=[b_Lm[j]])
                L = Lm[j]
                vv(L[:, 3, :], L[:, 0, :], L[:, 1, :], ALU.max, [b_Lm[j]], [b_Lm[j]])
                vv(L[:, 3, :], L[:, 3, :], L[:, 2, :], ALU.max, [b_Lm[j]], [b_Lm[j]])
                for g in range(3):
                    vv(wg[j][:, g, :], L[:, g, :], L[:, 3, :], ALU.subtract, [b_Lm[j]], [b_wg[j]])
                act(wg[j][:, :, :], wg[j][:, :, :], AF.Exp, [b_wg[j]], [b_wg[j]])
                vv(L[:, 4, :], wg[j][:, 0, :], wg[j][:, 1, :], ALU.add, [b_wg[j]], [b_Lm[j]])
                vv(L[:, 4, :], L[:, 4, :], wg[j][:, 2, :], ALU.add, [b_wg[j], b_Lm[j]], [b_Lm[j]])
                S.op("vector", lambda e: e.reciprocal(out=L[:, 5, :], in_=L[:, 4, :]), reads=[b_Lm[j]], writes=[b_Lm[j]])
                for g in range(3):
                    vv(wg[j][:, g, :], wg[j][:, g, :], L[:, 5, :], ALU.mult, [b_wg[j], b_Lm[j]], [b_wg[j]])
                v3 = lambda t: t.rearrange("p (h d) -> p h d", h=8)
                vv(v3(ym[j][:, :]), v3(Om[j][:, 0, :]), self.bc_last(wg[j][:, 0, :], 128), ALU.mult, [b_Om[j], b_wg[j]], [b_ym[j]])
                vv(v3(Om[j][:, 1, :]), v3(Om[j][:, 1, :]), self.bc_last(wg[j][:, 1, :], 128), ALU.mult, [b_Om[j], b_wg[j]], [b_Om[j]], eng="gpsimd")
                vv(v3(Om[j][:, 2, :]), v3(Om[j][:, 2, :]), self.bc_last(wg[j][:, 2, :], 128), ALU.mult, [b_Om[j], b_wg[j]], [b_Om[j]], eng="gpsimd")
                vv(ym[j][:, :], ym[j][:, :], Om[j][:, 1, :], ALU.add, [b_ym[j], b_Om[j]], [b_ym[j]])
                vv(ymb[j][:, :], ym[j][:, :], Om[j][:, 2, :], ALU.add, [b_ym[j], b_Om[j]], [b_ymb[j]])
                pst, bpt = self.getps()
                pstb = pst[:, :].bitcast(BF16)
                for h in range(8):
                    tr(pstb[:, h * 128:(h + 1) * 128], ymb[j][:, h * 128:(h + 1) * 128], self.identb[:, :], [b_ymb[j]], [bpt], inc=(h == 7))
                cp(yT_[j][:, :, :], pstb[:, :].rearrange("p (h t) -> p h t", h=8), [bpt], [b_yT[j]], eng="scalar")
                S.dma("sync", X["yT"][2 * RW:D, t0:t0 + 128].rearrange("(q p) t -> p q t", p=128), yT_[j][:, :, :], reads=[b_yT[j]])

            S.barrier()
            es2.close()
            if self.stages.get("attn_sample", True):
                Kc = sb("s_Kc", [128, 9, 1024], BF16); Vc = sb("s_Vc", [128, 9, 1024], BF16)
                KTs = sb("s_KTs", [128, 9, 8, 128], BF16)
                QsT = sb("s_QsT", [128, 8, 4], BF16); KnT = sb("s_KnT", [128, 8, 4], BF16)
                Vn = sb("s_Vn", [4, 1024], BF16)
                Qm = sb("s_Qm", [128, 8, 4, 4], BF16)
                Sa = sb("s_Sa", [4, 3, 8, 132], F32); Ms = sb("s_Ms", [4, 3, 8, 132], F32)
                Pw = sb("s_Pw", [4, 3, 8, 132], BF16)
                d01 = sb("s_d01", [4, 4], F32); dneg = sb("s_dneg", [4, 4], F32); cmask = sb("s_cmask", [128, 4, 4], BF16)
                sst = sb("s_st", [4, 8, 24], F32)
                PTs = sb("s_PT", [128, 24, 4], BF16); PTm = sb("s_PTm", [128, 4, 24, 4], BF16)
                Pn = sb("s_Pn", [4, 8, 4], F32); Pnb = sb("s_Pnb", [4, 8, 4], BF16); PnT = sb("s_PnT", [4, 8, 4], BF16)
                ysb = sb("s_ysb", [4, 1024], BF16); ysT = sb("s_ysT", [128, 8, 4], BF16)
                b_Kc = Buf(); b_Vc = Buf(); b_KTs = Buf(); b_q = Buf(); b_Vn = Buf(); b_Qm = Buf(); b_Sa = Buf(); b_Ms = Buf()
                b_Pw = Buf(); b_k = Buf(); b_sst = Buf(); b_PTs = Buf(); b_PTm = Buf(); b_Pn = Buf(); b_PnT = Buf(); b_ysb = Buf(); b_ysT = Buf()
                rows = [(1920, 1)] + [(1536 + t, 4) for t in range(4)] + [(t, 16) for t in range(4)]
                for i, (r0, stp) in enumerate(rows):
                    for (dst, src_t, bb) in ((Kc, I["cache_k"], b_Kc), (Vc, I["cache_v"], b_Vc)):
                        src = bass.AP(tensor=src_t.tensor, offset=src_t[l, r0, 0].offset, ap=[[stp * 1024, 128], [1, 1024]])
                        S.dma("gpsimd", dst[:, i, :], src, writes=[bb])
                S.dma("sync", QsT[:, :, :], X["qT"][:, SEQ:SEQ + NS].rearrange("(h d) t -> d h t", d=128), writes=[b_q], **nck)
                S.dma("sync", KnT[:, :, :], X["kT"][:, SEQ:SEQ + NS].rearrange("(h d) t -> d h t", d=128), writes=[b_q], **nck)
                S.dma("gpsimd", Vn[:, :], O["v_s"][l, :, :], writes=[b_Vn])
                S.dma("sync", d01[:, :], I["c_d01"][:, :], writes=[b_k])
                S.dma("sync", dneg[:, :], I["c_dneg"][:, :], writes=[b_k])
                S.dma("gpsimd", cmask[:, :, :], I["c_colmask"][:, :, :], writes=[b_k])
                for i in range(9):
                    pst, bpt = self.getps()
                    pstb = pst[:, :].bitcast(BF16)
                    for h in range(8):
                        tr(pstb[:, h * 128:(h + 1) * 128], Kc[:, i, h * 128:(h + 1) * 128], self.identb[:, :], [b_Kc], [bpt], inc=(h == 7))
                    cp(KTs[:, i, :, :], pstb[:, :].rearrange("p (h t) -> p h t", h=8), [bpt], [b_KTs], eng=("scalar" if i % 2 else "vector"))
                S.op("vector", lambda e: e.memset(Qm[:], 0.0), writes=[b_Qm])
                for t in range(4):
                    cp(Qm[:, :, t, t], QsT[:, :, t], [b_q], [b_Qm])
                for t in range(4):
                    S.dma("sync", Ms[t:t + 1, 0, :, :], bass.AP(tensor=Gd.tensor, offset=Gd[0, 0, 127 - t].offset, ap=[[0, 1], [3 * 383, 8], [1, 132]]),
                          reads=[b_Gd], writes=[b_Ms])
                for g in (1, 2):
                    S.dma("sync", Ms[:, g, :, 0:128], bass.AP(tensor=Gd.tensor, offset=Gd[0, g, 127].offset, ap=[[0, 4], [3 * 383, 8], [1, 128]]),
                          reads=[b_Gd], writes=[b_Ms])
                    for tt in range(4):
                        S.dma("sync", Ms[:, g, :, 128 + tt], bass.AP(tensor=Gd.tensor, offset=Gd[0, g, 255].offset, ap=[[0, 4], [3 * 383, 8]]),
                              reads=[b_Gd], writes=[b_Ms], **nck)
                    blkv = Ms[:, g, :, 128:132]
                    d01b = bass.AP(tensor=d01[:, :].tensor, offset=d01[:, :].offset, ap=[list(d01[:, :].ap[0]), [0, 8], [1, 4]])
                    dngb = bass.AP(tensor=dneg[:, :].tensor, offset=dneg[:, :].offset, ap=[list(dneg[:, :].ap[0]), [0, 8], [1, 4]])
                    vv(blkv, blkv, d01b, ALU.mult, [b_Ms, b_k], [b_Ms])
                    vv(blkv, blkv, dngb, ALU.add, [b_Ms, b_k], [b_Ms])
                for g in range(3):
                    for h in range(8):
                        ps, bp = self.getps()
                        if g == 0:
                            mm(ps[:4, 0:128], QsT[:, h, :], KTs[:, 0, h, :], [b_q, b_KTs], [bp])
                        else:
                            for t in range(4):
                                mm(ps[:4, 0:128], Qm[:, h, t, :], KTs[:, 1 + 4 * (g - 1) + t, h, :], [b_Qm, b_KTs], [bp], start=(t == 0), stop=(t == 3))
                        mm(ps[:4, 128:132], QsT[:, h, :], KnT[:, h, :], [b_q], [bp])
                        stt(Sa[:, g, h, :], ps[:4, 0:132], SC, Ms[:, g, h, :], ALU.mult, ALU.add, [bp, b_Ms], [b_Sa])
                S3 = Sa[:, :, :, :].rearrange("p g h k -> p (g h) k")
                mx, den, lse, rden = sst[:, 0, :], sst[:, 1, :], sst[:, 2, :], sst[:, 3, :]
                S.op("vector", lambda e: e.reduce_max(out=mx, in_=S3, axis=AX.X), reads=[b_Sa], writes=[b_sst])
                vv(S3, S3, self.bc_last(mx, 132), ALU.subtract, [b_Sa, b_sst], [b_Sa])
                act(S3, S3, AF.Exp, [b_Sa], [b_Sa])
                S.op("vector", lambda e: e.reduce_sum(out=den, in_=S3, axis=AX.X), reads=[b_Sa], writes=[b_sst])
                act(lse, den, AF.Ln, [b_sst], [b_sst])
                vv(lse, lse, mx, ALU.add, [b_sst], [b_sst])
                mg, sg = sst[:, 4, 0:8], sst[:, 4, 8:16]
                eg = sst[:, 5, :]
                vv(mg, sst[:, 2, 0:8], sst[:, 2, 8:16], ALU.max, [b_sst], [b_sst])
                vv(mg, mg, sst[:, 2, 16:24], ALU.max, [b_sst], [b_sst])
                for g in range(3):
                    vv(sst[:, 5, g * 8:(g + 1) * 8], sst[:, 2, g * 8:(g + 1) * 8], mg, ALU.subtract, [b_sst], [b_sst])
                act(eg, eg, AF.Exp, [b_sst], [b_sst])
                vv(sg, sst[:, 5, 0:8], sst[:, 5, 8:16], ALU.add, [b_sst], [b_sst])
                vv(sg, sg, sst[:, 5, 16:24], ALU.add, [b_sst], [b_sst])
                S.op("vector", lambda e: e.reciprocal(out=sst[:, 6, 0:8], in_=sg), reads=[b_sst], writes=[b_sst])
                S.op("vector", lambda e: e.reciprocal(out=rden, in_=den), reads=[b_sst], writes=[b_sst])
                for g in range(3):
                    vv(sst[:, 5, g * 8:(g + 1) * 8], sst[:, 5, g * 8:(g + 1) * 8], sst[:, 6, 0:8], ALU.mult, [b_sst], [b_sst])
                vv(eg, eg, rden, ALU.mult, [b_sst], [b_sst])
                vv(Pw[:, :, :, :].rearrange("p g h k -> p (g h) k"), S3, self.bc_last(eg, 132), ALU.mult, [b_Sa, b_sst], [b_Pw])
                pst, bpt = self.getps()
                pstb = pst[:, :].bitcast(BF16)
                for gh in range(24):
                    tr(pstb[:, gh * 4:(gh + 1) * 4], Pw[:, gh // 8, gh % 8, 0:128], self.identb[:4, :4], [b_Pw], [bpt], inc=(gh == 23))
                cp(PTs[:, :, :], pstb[:, 0:96].rearrange("p (a t) -> p a t", t=4), [bpt], [b_PTs])
                for t in range(4):
                    cmb = bass.AP(tensor=cmask[:, t, :].tensor, offset=cmask[:, t, :].offset, ap=[list(cmask[:, t, :].ap[0]), [0, 24], [1, 4]])
                    vv(PTm[:, t, :, :], PTs[:, :, :], cmb, ALU.mult, [b_PTs, b_k], [b_PTm])
                vv(Pn[:, :, :], Pw[:, 0, :, 128:132], Pw[:, 1, :, 128:132], ALU.add, [b_Pw], [b_Pn])
                vv(Pnb[:, :, :], Pn[:, :, :], Pw[:, 2, :, 128:132], ALU.add, [b_Pw, b_Pn], [b_Pn])
                pst2, bpt2 = self.getps()
                pstb2 = pst2[:, :].bitcast(BF16)
                for h in range(8):
                    tr(pstb2[:4, h * 4:(h + 1) * 4], Pnb[:, h, :], self.identb[:4, :4], [b_Pn], [bpt2], inc=(h == 7))
                cp(PnT[:, :, :], pstb2[:4, 0:32].rearrange("p (h t) -> p h t", h=8), [bpt2], [b_PnT])
                for h in range(8):
                    ps, bp = self.getps()
                    hs = slice(h * 128, (h + 1) * 128)
                    mm(ps[:4, 0:128], PTs[:, h, :], Vc[:, 0, hs], [b_PTs, b_Vc], [bp], start=True, stop=False, inc=False)
                    for g in (1, 2):
                        for t in range(4):
                            mm(ps[:4, 0:128], PTm[:, t, g * 8 + h, :], Vc[:, 1 + 4 * (g - 1) + t, hs], [b_PTm, b_Vc], [bp], start=False, stop=False, inc=False)
                    mm(ps[:4, 0:128], PnT[:, h, :], Vn[:, hs], [b_PnT, b_Vn], [bp], start=False, stop=True)
                    cp(ysb[:, hs], ps[:4, 0:128], [bp], [b_ysb], eng=("scalar" if h % 2 else "vector"))
                pst3, bpt3 = self.getps()
                pstb3 = pst3[:, :].bitcast(BF16)
                for h in range(8):
                    tr(pstb3[:, h * 4:(h + 1) * 4], ysb[:, h * 128:(h + 1) * 128], self.identb[:4, :4], [b_ysb], [bpt3], inc=(h == 7))
                cp(ysT[:, :, :], pstb3[:, 0:32].rearrange("p (h t) -> p h t", h=8), [bpt3], [b_ysT])
                S.dma("sync", X["yT"][2 * RW:D, SEQ:SEQ + NS].rearrange("(q p) t -> p q t", p=128), ysT[:, :, :], reads=[b_ysT], **nck)
            S.barrier()

    def rstd_from_mean(self, st, n, R, W):
        self.vs(st[:n, 1:2], st[:n, 0:1], 1e-6, None, ALU.add, None, R, W)
        self.act(st[:n, 0:1], st[:n, 1:2], AF.Sqrt, W, W)
        self.S.op("vector", lambda e: e.reciprocal(out=st[:n, 1:2], in_=st[:n, 0:1]), reads=W, writes=W)

    def stage_wout(self, l, first):
        nc, S = self.nc, self.S
        I, O, X = self.I, self.O, self.X
        vv, vs, stt, act, cp, mm, tr = self.vv, self.vs, self.stt, self.act, self.cp, self.mm, self.tr
        with contextlib.ExitStack() as es:
            sb = lambda n, s, d: es.enter_context(self.sbt(n, s, d))
            self.ps_pool = [es.enter_context(self.pst(f"o_ps{j}", [128, 512], F32)) for j in range(8)]
            self.ps_bufs = [PBuf() for _ in range(8)]
            self.ps_next = 0
            wo = sb("o_wo", [128, 16, D], BF16); b_wo = Buf()
            g1 = sb("o_g1", [128, D], F32); g2 = sb("o_g2", [128, D], F32); b_g = Buf()
            yTt = [sb(f"o_yT{j}", [128, 16, 128], BF16) for j in range(2)]; b_yTt = [Buf(), Buf()]
            xt = [sb(f"o_xt{j}", [128, D], F32) for j in range(2)]; b_xt = [Buf(), Buf()]
            x1 = [sb(f"o_x1{j}", [128, D], F32) for j in range(2)]; b_x1 = [Buf(), Buf()]
            hb = [sb(f"o_hb{j}", [128, D], BF16) for j in range(2)]; b_hb = [Buf(), Buf()]
            hbT = [sb(f"o_hbT{j}", [128, 16, 128], BF16) for j in range(2)]; b_hbT = [Buf(), Buf()]
            junk = sb("o_junk", [128, D], BF16); b_junk = Buf()
            st = [sb(f"o_st{j}", [128, 8], F32) for j in range(2)]; b_st = [Buf(), Buf()]
            for q in range(4):
                S.dma("gpsimd", wo[:, q * 4:(q + 1) * 4, :], I["w_out"][l, q * 512:(q + 1) * 512, :].rearrange("(c p) n -> p c n", p=128), writes=[b_wo])
            S.dma("sync", g1[:, :], dram_bcast(I["norm_mix_post"][l:l + 1, :], 128, D), writes=[b_g])
            S.dma("sync", g2[:, :], dram_bcast(I["norm_ffn_pre"][l:l + 1, :], 128, D), writes=[b_g])
            for ti, (t0, n) in enumerate(tok_tiles()):
                j = ti % 2
                S.dma("sync", yTt[j][:, :, :n], X["yT"][:, t0:t0 + n].rearrange("(c p) t -> p c t", p=128), writes=[b_yTt[j]])
                if first:
                    src = I["xp"][t0:t0 + n, :] if t0 < SEQ else I["xs"][:, :]
                else:
                    src = X["xres"][t0:t0 + n, :]
                S.dma("sync", xt[j][:n, :], src, writes=[b_xt[j]])
                pss = []
                for db in range(4):
                    ps, bp = self.getps()
                    pss.append((ps, bp))
                    for c in range(16):
                        mm(ps[:n, :], yTt[j][:, c, :n], wo[:, c, db * 512:(db + 1) * 512], [b_yTt[j], b_wo], [bp], start=(c == 0), stop=(c == 15))
                    act(junk[:n, 0:512], ps[:n, :], AF.Square, [bp], [b_junk, b_st[j]], scale=float(D ** -0.5), accum_out=st[j][:n, 2 + db:3 + db])
                S.op("vector", lambda e: e.reduce_sum(out=st[j][:n, 0:1], in_=st[j][:n, 2:6], axis=AX.X), reads=[b_st[j]], writes=[b_st[j]])
                self.rstd_from_mean(st[j], n, [b_st[j]], [b_st[j]])
                for db in range(4):
                    ps, bp = pss[db]
                    sl = slice(db * 512, (db + 1) * 512)
                    stt(x1[j][:n, sl], ps[:n, :], st[j][:n, 1:2], g1[:n, sl], ALU.mult, ALU.mult, [bp, b_st[j], b_g], [b_x1[j]])
                vv(x1[j][:n, :], x1[j][:n, :], xt[j][:n, :], ALU.add, [b_x1[j], b_xt[j]], [b_x1[j]], eng="gpsimd")
                S.dma("sync", X["x1s"][t0:t0 + n, :], x1[j][:n, :], reads=[b_x1[j]])
                act(junk[:n, :], x1[j][:n, :], AF.Square, [b_x1[j]], [b_junk, b_st[j]], scale=float(D ** -0.5), accum_out=st[j][:n, 0:1])
                self.rstd_from_mean(st[j], n, [b_st[j]], [b_st[j]])
                stt(hb[j][:n, :], x1[j][:n, :], st[j][:n, 1:2], g2[:n, :], ALU.mult, ALU.mult, [b_x1[j], b_st[j], b_g], [b_hb[j]])
                for half in range(2):
                    pst, bpt = self.getps()
                    pstb = pst[:, :].bitcast(BF16)
                    for c in range(8):
                        cc = half * 8 + c
                        tr(pstb[:, c * 128:c * 128 + n], hb[j][:n, cc * 128:(cc + 1) * 128], self.identb[:n, :n], [b_hb[j]], [bpt], inc=(c == 7))
                    cp(hbT[j][:, half * 8:(half + 1) * 8, :n], pstb[:, :].rearrange("p (c t) -> p c t", c=8)[:, :, :n], [bpt], [b_hbT[j]],
                       eng=("scalar" if half == 0 else "vector"))
                S.dma("sync", X["hfT"][:, t0:t0 + n].rearrange("(c p) t -> p c t", p=128), hbT[j][:, :, :n], reads=[b_hbT[j]])
            S.barrier()

    def stage_ffn(self, l, last):
        nc, S = self.nc, self.S
        I, O, X = self.I, self.O, self.X
        vv, vs, stt, act, cp, mm, tr = self.vv, self.vs, self.stt, self.act, self.cp, self.mm, self.tr
        with contextlib.ExitStack() as es:
            sb = lambda n, s, d: es.enter_context(self.sbt(n, s, d))
            self.ps_pool = [es.enter_context(self.pst(f"f_ps{j}", [128, 512], F32)) for j in range(8)]
            self.ps_bufs = [PBuf() for _ in range(8)]
            self.ps_next = 0
            NSB = 1028
            hfs = sb("f_hfs", [128, 16, NSB], BF16); b_hfs = Buf()
            facc = sb("f_facc", [128, 9, D], F32); b_facc = [Buf() for _ in range(9)]
            w1 = [sb(f"f_w1{j}", [128, 16, 512], BF16) for j in range(2)]; b_w1 = [Buf(), Buf()]
            w2 = [sb(f"f_w2{j}", [128, 4, D], BF16) for j in range(2)]; b_w2 = [Buf(), Buf()]
            aT = [sb(f"f_aT{j}", [128, 4, NSB], BF16) for j in range(2)]; b_aT = [Buf(), Buf()]
            rl = [sb(f"f_rl{j}", [128, 512], F32) for j in range(2)]; b_rl = [Buf() for _ in range(2)]
            g3 = sb("f_g3", [128, D], F32); b_g = Buf()
            x1_ = sb("f_x1", [128, D], F32); x1 = [x1_, x1_]; _bx = Buf(); b_x1 = [_bx, _bx]
            junk = sb("f_junk", [128, 512], BF16); b_junk = Buf()
            st = [sb(f"f_st{j}", [128, 8], F32) for j in range(2)]; b_st = [Buf(), Buf()]
            S.dma("sync", g3[:, :], dram_bcast(I["norm_ffn_post"][l:l + 1, :], 128, D), writes=[b_g])
            sbs = [(0, 1024), (1024, TT - 1024)]
            nfb = self.stages.get("ffn_blocks", 16)
            wcnt = 0
            rcnt = 0

            def load_w(fb, j):
                if self.stages.get('dbg_noload') and fb >= 2:
                    return
                src1 = I["ffn_w1"][l, :, fb * 512:(fb + 1) * 512].rearrange("(c p) n -> p c n", p=128)
                S.dma("gpsimd", w1[j][:, 0:8, :], src1[:, 0:8, :], writes=[b_w1[j]])
                S.dma("gpsimd", w1[j][:, 8:16, :], src1[:, 8:16, :], writes=[b_w1[j]])
                src2 = I["ffn_w2"][l, fb * 512:(fb + 1) * 512, :].rearrange("(c p) n -> p c n", p=128)
                S.dma("gpsimd", w2[j][:, :, :], src2, writes=[b_w2[j]])


            def emit_w1(fb, j, tblocks):
                nonlocal rcnt
                for fc in range(4):
                    pss = []
                    for (b0, nb_) in tblocks:
                        pss.append(self.getps())
                    for c in range(16):
                        for bi_, (b0, nb_) in enumerate(tblocks):
                            ps, bp = pss[bi_]
                            mm(ps[:, :nb_], w1[j][:, c, fc * 128:(fc + 1) * 128], hfs[:, c, b0:b0 + nb_], [b_w1[j], b_hfs], [bp], start=(c == 0), stop=(c == 15))
                    for bi_, (b0, nb_) in enumerate(tblocks):
                        ps, bp = pss[bi_]
                        rj = rcnt % 2
                        rcnt += 1
                        act(rl[rj][:, :nb_], ps[:, :nb_], AF.Relu, [bp], [b_rl[rj]])
                        act(aT[j][:, fc, b0:b0 + nb_], rl[rj][:, :nb_], AF.Square, [b_rl[rj]], [b_aT[j]])

            def emit_w2(fb, j, tiles):
                for ti, (c0, n) in enumerate(tiles):
                    for db in range(4):
                        ps, bp = self.getps()
                        for fc in range(4):
                            mm(ps[:n, :], aT[j][:, fc, c0:c0 + n], w2[j][:, fc, db * 512:(db + 1) * 512], [b_aT[j], b_w2[j]], [bp], start=(fc == 0), stop=(fc == 3))
                        sl = slice(db * 512, (db + 1) * 512)
                        if fb == 0:
                            cp(facc[:n, ti, sl], ps[:n, :], [bp], [b_facc[ti]], eng="scalar")
                        else:
                            vv(facc[:n, ti, sl], facc[:n, ti, sl], ps[:n, :], ALU.add, [bp, b_facc[ti]], [b_facc[ti]])

            for (T0, NT) in sbs:
                S.dma("sync", hfs[:, :, :NT], X["hfT"][:, T0:T0 + NT].rearrange("(c p) t -> p c t", p=128), writes=[b_hfs])
                tiles = [(i * 128, min(128, NT - i * 128)) for i in range((NT + 127) // 128)]
                tblocks = [(i * 512, min(512, NT - i * 512)) for i in range((NT + 511) // 512)]
                load_w(0, wcnt % 2)
                jj = wcnt % 2
                wcnt += 1
                if nfb > 1:
                    load_w(1, wcnt % 2)
                emit_w1(0, jj, tblocks)
                for fb in range(nfb):
                    jn = wcnt % 2
                    if fb + 1 < nfb:
                        wcnt += 1
                        emit_w1(fb + 1, jn, tblocks)
                    emit_w2(fb, jj, tiles)
                    if fb + 2 < nfb:
                        load_w(fb + 2, jj)
                    jj = jn
                for ti, (c0, n) in enumerate(tiles):
                    j = ti % 2
                    t0 = T0 + c0
                    S.dma("sync", x1[j][:n, :], X["x1s"][t0:t0 + n, :], writes=[b_x1[j]])
                    for db in range(4):
                        act(junk[:n, :], facc[:n, ti, db * 512:(db + 1) * 512], AF.Square, [b_facc[ti]], [b_junk, b_st[j]], scale=float(D ** -0.5),
                            accum_out=st[j][:n, 2 + db:3 + db])
                    S.op("vector", lambda e: e.reduce_sum(out=st[j][:n, 0:1], in_=st[j][:n, 2:6], axis=AX.X), reads=[b_st[j]], writes=[b_st[j]])
                    self.rstd_from_mean(st[j], n, [b_st[j]], [b_st[j]])
                    stt(facc[:n, ti, :], facc[:n, ti, :], st[j][:n, 1:2], g3[:n, :], ALU.mult, ALU.mult, [b_facc[ti], b_st[j], b_g], [b_facc[ti]])
                    vv(x1[j][:n, :], x1[j][:n, :], facc[:n, ti, :], ALU.add, [b_x1[j], b_facc[ti]], [b_x1[j]])
                    if last:
                        dst = O["yp"][t0:t0 + n, :] if t0 < SEQ else O["ys"][:, :]
                    else:
                        dst = X["xres"][t0:t0 + n, :]
                    S.dma("sync", dst, x1[j][:n, :], reads=[b_x1[j]])
            S.barrier()

    def build(self):
        nc, S = self.nc, self.S
        self.declare()
        self.load_consts()
        for l in range(DEPTH):
            if l >= self.stages.get("layers", DEPTH):
                break
            with contextlib.ExitStack() as es:
                hT = es.enter_context(self.sbt("hT", [128, 16, TT], BF16))
                b_hT = Buf("hT")
                self.stage_norm_T(l, l == 0, hT, b_hT, "norm_mix_pre")
                self.stage_win(l, hT, b_hT)
                S.barrier()
            if self.stages.get('lru', True):
                self.stage_lru(l)
            if self.stages.get('rwkv', True):
                self.stage_rwkv(l)
            if self.stages.get('attn', True):
                self.stage_attn(l)
            if self.stages.get('ffn', True):
                self.stage_wout(l, l == 0)
                self.stage_ffn(l, l == DEPTH - 1)
        S.finish()
        self.es.close()
        return nc


def _t5_onehot():
    E = np.zeros((32, 3 * 129), np.float32)
    for g, dil in enumerate((1, 4, 16)):
        for j in range(129):
            dist = np.int32((128 - j) * dil)
            d = np.float32(max(int(dist), 1))
            large = 16 + int(np.float32(np.log(d / np.float32(16.0))) / np.float32(np.log(2048.0 / 16.0)) * np.float32(16.0))
            b = int(dist) if dist < 16 else min(large, 31)
            E[b, g * 129 + j] = 1.0
    return E


PROMPT_CORES = (0, 1, 4, 5)


def _per_core_inputs(inp, c):
    m = {}
    if c in PROMPT_CORES:
        m["xp"] = np.ascontiguousarray(inp["x_prompt"][PROMPT_CORES.index(c)])
    else:
        m["xp"] = np.zeros((SEQ, D), np.float32)
    m["xs"] = np.ascontiguousarray(inp["x_sample"][c])
    m["st_wkv"] = np.ascontiguousarray(inp["state_rwkv_wkv"][:, c])
    m["st_shift"] = np.ascontiguousarray(inp["state_rwkv_shift"][:, c])
    m["st_lruh"] = np.ascontiguousarray(inp["state_lru_h"][:, c])
    m["st_conv"] = np.ascontiguousarray(inp["state_lru_conv"][:, c])
    m["cache_k"] = np.ascontiguousarray(inp["cache_attn_k"][:, c]).reshape(DEPTH, SEQ, AW)
    m["cache_v"] = np.ascontiguousarray(inp["cache_attn_v"][:, c]).reshape(DEPTH, SEQ, AW)
    for n in ("rel_bias", "norm_mix_pre", "norm_mix_post", "norm_ffn_pre", "norm_ffn_post", "w_in", "w_out",
              "rwkv_mu", "rwkv_w0", "rwkv_w_up", "rwkv_a0", "rwkv_a_up", "rwkv_g_up", "rwkv_k_k", "rwkv_k_a",
              "rwkv_lnx_g", "rwkv_lnx_b", "lru_conv_w", "lru_conv_b", "lru_wa", "lru_ba", "lru_wx", "lru_bx",
              "lru_lambda", "ffn_w1", "ffn_w2"):
        m[n] = inp[n]
    m["rwkv_r_k"] = inp["rwkv_r_k"].reshape(DEPTH, RW)
    m["c_ident"] = np.eye(128, dtype=np.float32)
    m["c_tri"] = np.triu(np.ones((128, 128), np.float32))
    m["c_stri"] = np.triu(np.ones((128, 128), np.float32), 1)
    m["c_ltri"] = np.tril(np.ones((128, 128), np.float32), -1)
    m["c_onehot"] = _t5_onehot()
    m["c_d01"] = np.eye(4, dtype=np.float32)
    m["c_dneg"] = ((1.0 - np.eye(4)) * NEG).astype(np.float32)
    cm = np.zeros((128, 4, 4), np.float32)
    for t in range(4):
        cm[:, t, t] = 1.0
    m["c_colmask"] = cm
    m["c_antiident"] = np.ascontiguousarray(np.eye(128, dtype=np.float32)[::-1])
    return m


def kernel(stages=None, **inp):
    inp = {k: np.asarray(v) for k, v in inp.items()}
    prog = Prog(stages or {})
    nc = prog.build()
    ncores = prog.stages.get("ncores", 8)
    in_maps = [_per_core_inputs(inp, c) for c in range(ncores)]
    res = run_bass_kernel_spmd(nc, in_maps, core_ids=list(range(ncores)))
    R = list(res.results)
    while len(R) < 8:
        R.append(R[0])
    B = 4
    PC = PROMPT_CORES

    def stackp(name, shape_tail):
        return np.stack([R[b][name] for b in range(B)], axis=0)

    def stacks(name):
        return np.stack([R[c][name] for c in range(8)], axis=0)

    y_p = np.stack([R[PC[b]]["yp"] for b in range(B)], 0)
    y_s = stacks("ys")
    wkv_p = np.stack([R[PC[b]]["wkv_p"] for b in range(B)], 1)
    wkv_s = np.stack([R[c]["wkv_s"] for c in range(8)], 1)
    shift_p = np.stack([R[PC[b]]["shift_p"] for b in range(B)], 1)
    shift_s = np.stack([R[c]["shift_s"] for c in range(8)], 1)
    lruh_p = np.stack([R[PC[b]]["lruh_p"] for b in range(B)], 1)
    lruh_s = np.stack([R[c]["lruh_s"] for c in range(8)], 1)
    conv_p = np.stack([R[PC[b]]["conv_p"] for b in range(B)], 1)
    conv_s = np.stack([R[c]["conv_s"] for c in range(8)], 1)
    k_p = np.stack([R[PC[b]]["k_p"] for b in range(B)], 1).reshape(DEPTH, B, SEQ, AH, AD)
    k_s = np.stack([R[c]["k_s"] for c in range(8)], 1).reshape(DEPTH, 8, NS, AH, AD)
    v_p = np.stack([R[PC[b]]["v_p"] for b in range(B)], 1).reshape(DEPTH, B, SEQ, AH, AD)
    v_s = np.stack([R[c]["v_s"] for c in range(8)], 1).reshape(DEPTH, 8, NS, AH, AD)
    outs = (y_p, y_s, wkv_p, wkv_s, shift_p, shift_s, lruh_p, lruh_s, conv_p, conv_s, k_p, k_s, v_p, v_s)
    return tuple(np.ascontiguousarray(o, dtype=np.float32) for o in outs)
```

```python
import contextlib
import numpy as np
import concourse.bass as bass
import concourse.mybir as mybir
from concourse.bass_utils import run_bass_kernel_spmd

F32 = mybir.dt.float32
BF16 = mybir.dt.bfloat16
F32R = mybir.dt.float32r
AF = mybir.ActivationFunctionType
ALU = mybir.AluOpType
AX = mybir.AxisListType

D = 2048
SEQ = 2048
DEPTH = 2
NS = 4
TT = SEQ + NS
RW = 512
RH = 8
RN = 64
RPROJ = 1792
LW = 512
AW = 1024
AH = 8
AD = 128
N_IN = 5888
DFF = 8192
NEG = -1e30
C_P, C_LX, C_LG, C_Q, C_K, C_V = 0, 1792, 2304, 2816, 3840, 4864
NFM = 4864


class Buf:
    __slots__ = ("name", "w", "r", "x")

    def __init__(self, name="", x=False):
        self.name = name
        self.w = None
        self.r = {}
        self.x = x


def PBuf():
    return Buf("psum", True)


class Tok:
    __slots__ = ("eng", "sem", "val")

    def __init__(self, eng, sem, val):
        self.eng, self.sem, self.val = eng, sem, val


class Eng:
    def __init__(self, name, eng, sems):
        self.name, self.eng, self.sems = name, eng, sems
        self.si = 0
        self.count = 0
        self.waited = {}


class Sched:
    EPOCH = 30000

    def __init__(self, nc, es):
        self.nc = nc
        self.engs = {}
        for name in ("tensor", "vector", "scalar", "gpsimd", "sync"):
            sems = [es.enter_context(nc.semaphore(f"p_{name}_{i}")) for i in range(3)]
            self.engs[name] = Eng(name, getattr(nc, name), sems)
        self.dsem = {}
        self.dval = {}
        self.dnext = {}
        for q, cnt in (("sync", 24), ("gpsimd", 12), ("scalar", 4)):
            self.dsem[q] = [es.enter_context(nc.semaphore(f"dq_{q}_{i}")) for i in range(cnt)]
            self.dval[q] = [0] * cnt
            self.dnext[q] = 0
        self.dma_toks = []
        self.uid = 0

    def _wait(self, E, tok):
        key = id(tok.sem)
        if E.waited.get(key, 0) >= tok.val:
            return
        E.eng.wait_ge(tok.sem, tok.val)
        E.waited[key] = tok.val

    def _deps(self, E, reads, writes):
        for b in reads:
            if b.w is not None:
                self._dep1(E, b.w)
            if b.x:
                for t in b.r.values():
                    if t.eng is not E:
                        self._dep1(E, t)
        for b in writes:
            if b.w is not None:
                self._dep1(E, b.w)
            for t in b.r.values():
                self._dep1(E, t)

    def _dep1(self, E, tok):
        if tok.eng is E and E.name == "tensor":
            return
        self._wait(E, tok)

    def _mark(self, tok, reads, writes):
        for b in writes:
            b.w = tok
            b.r = {}
        for b in reads:
            if tok.eng is None:
                self.uid += 1
                b.r[("d", self.uid)] = tok
            else:
                b.r[tok.eng.name] = tok

    def op(self, engname, fn, reads=(), writes=(), inc=True):
        E = self.engs[engname]
        self._deps(E, reads, writes)
        ins = fn(E.eng)
        if inc:
            if E.count >= self.EPOCH:
                E.si += 1
                E.count = 0
            E.count += 1
            ins.then_inc(E.sems[E.si], 1)
            tok = Tok(E, E.sems[E.si], E.count)
        else:
            tok = Tok(E, E.sems[E.si], E.count + 1)
        self._mark(tok, reads, writes)
        return tok

    def dma(self, qname, out, in_, reads=(), writes=(), **kw):
        E = self.engs[qname]
        self._deps(E, reads, writes)
        i = self.dnext[qname]
        self.dnext[qname] = (i + 1) % len(self.dsem[qname])
        sem = self.dsem[qname][i]
        dv = self.dval[qname]
        if dv[i] > 0:
            self._wait(E, Tok(None, sem, dv[i]))
        dv[i] += 16
        E.eng.dma_start(out=out, in_=in_, **kw).then_inc(sem, 16)
        tok = Tok(None, sem, dv[i])
        self._mark(tok, reads, writes)
        return tok

    def barrier(self):
        names = list(self.engs)
        for a in names:
            A = self.engs[a]
            for b in names:
                if a == b:
                    continue
                B = self.engs[b]
                if B.count > 0:
                    self._wait(A, Tok(B, B.sems[B.si], B.count))
            for q in self.dsem:
                for i, sem in enumerate(self.dsem[q]):
                    if self.dval[q][i] > 0:
                        self._wait(A, Tok(None, sem, self.dval[q][i]))

    def finish(self):
        self.barrier()


def dram_bcast(ap2d_row, nparts, n):
    return bass.AP(tensor=ap2d_row.tensor, offset=ap2d_row.offset, ap=[[0, nparts], [1, n]])


def tok_tiles():
    tl = [(i * 128, 128) for i in range(SEQ // 128)]
    tl.append((SEQ, NS))
    return tl


def tok_blocks():
    bl = [(i * 512, 512) for i in range(SEQ // 512)]
    bl.append((SEQ, NS))
    return bl


class Prog:
    def __init__(self, stages):
        self.stages = stages
        nc = self.nc = bass.Bass("TRN2", target_bir_lowering=False)
        self.es = contextlib.ExitStack()
        self.S = Sched(nc, self.es)
        self.I = {}
        self.O = {}
        self.X = {}
        self.uid = 0

    def sbt(self, name, shape, dt):
        self.uid += 1
        return self.nc.sbuf_tensor(f"{name}_u{self.uid}", shape, dt)

    def pst(self, name, shape, dt):
        self.uid += 1
        return self.nc.psum_tensor(f"{name}_u{self.uid}", shape, dt)

    def din(self, name, shape, dt=F32):
        self.I[name] = self.nc.dram_tensor(name, list(shape), dt, kind="ExternalInput").ap()
        return self.I[name]

    def dout(self, name, shape, dt=F32):
        self.O[name] = self.nc.dram_tensor(name, list(shape), dt, kind="ExternalOutput").ap()
        return self.O[name]

    def dscr(self, name, shape, dt=F32):
        self.X[name] = self.nc.dram_tensor(name, list(shape), dt, kind="Internal").ap()
        return self.X[name]

    def declare(self):
        din, dout, dscr = self.din, self.dout, self.dscr
        din("xp", [SEQ, D]); din("xs", [NS, D])
        din("st_wkv", [DEPTH, RH, RN, RN]); din("st_shift", [DEPTH, RPROJ])
        din("st_lruh", [DEPTH, LW]); din("st_conv", [DEPTH, 3, LW])
        din("cache_k", [DEPTH, SEQ, AW]); din("cache_v", [DEPTH, SEQ, AW])
        din("rel_bias", [32, AH])
        for n in ("norm_mix_pre", "norm_mix_post", "norm_ffn_pre", "norm_ffn_post"):
            din(n, [DEPTH, D])
        din("w_in", [DEPTH, D, N_IN]); din("w_out", [DEPTH, D, D])
        din("rwkv_mu", [DEPTH, RPROJ]); din("rwkv_w0", [DEPTH, RW]); din("rwkv_w_up", [DEPTH, 64, RW])
        din("rwkv_a0", [DEPTH, RW]); din("rwkv_a_up", [DEPTH, 64, RW]); din("rwkv_g_up", [DEPTH, 128, RW])
        din("rwkv_k_k", [DEPTH, RW]); din("rwkv_k_a", [DEPTH, RW]); din("rwkv_r_k", [DEPTH, RW])
        din("rwkv_lnx_g", [DEPTH, RW]); din("rwkv_lnx_b", [DEPTH, RW])
        din("lru_conv_w", [DEPTH, 4, LW]); din("lru_conv_b", [DEPTH, LW])
        din("lru_wa", [DEPTH, 8, 64, 64]); din("lru_ba", [DEPTH, LW])
        din("lru_wx", [DEPTH, 8, 64, 64]); din("lru_bx", [DEPTH, LW]); din("lru_lambda", [DEPTH, LW])
        din("ffn_w1", [DEPTH, D, DFF]); din("ffn_w2", [DEPTH, DFF, D])
        din("c_ident", [128, 128]); din("c_tri", [128, 128]); din("c_stri", [128, 128]); din("c_ltri", [128, 128])
        din("c_onehot", [32, 387]); din("c_d01", [4, 4]); din("c_dneg", [4, 4]); din("c_colmask", [128, 4, 4]); din("c_antiident", [128, 128])
        dout("yp", [SEQ, D]); dout("ys", [NS, D])
        dout("wkv_p", [DEPTH, RH, RN, RN]); dout("wkv_s", [DEPTH, RH, RN, RN])
        dout("shift_p", [DEPTH, RPROJ]); dout("shift_s", [DEPTH, RPROJ])
        dout("lruh_p", [DEPTH, LW]); dout("lruh_s", [DEPTH, LW])
        dout("conv_p", [DEPTH, 3, LW]); dout("conv_s", [DEPTH, 3, LW])
        dout("k_p", [DEPTH, SEQ, AW]); dout("k_s", [DEPTH, NS, AW])
        dout("v_p", [DEPTH, SEQ, AW]); dout("v_s", [DEPTH, NS, AW])
        dscr("projT", [NFM - 2048, TT])
        dscr("qT", [AW, TT], BF16)
        dscr("kT", [AW, TT], BF16)
        dscr("xres", [TT, D])
        dscr("yT", [D, TT], BF16)
        dscr("Gd", [8, 3, 383])
        dscr("x1s", [TT, D])
        dscr("hfT", [D, TT], BF16)
        dscr("Os", [3, SEQ, AW])
        dscr("Ls", [3, SEQ, 8])

    def load_consts(self):
        nc, S, es = self.nc, self.S, self.es
        self.identf = es.enter_context(self.sbt("identf", [128, 128], F32))
        self.identb = es.enter_context(self.sbt("identb", [128, 128], BF16))
        self.b_ident = Buf("ident")
        S.dma("sync", self.identf[:], self.I["c_ident"][:, :], writes=[self.b_ident])
        S.dma("gpsimd", self.identb[:], self.I["c_ident"][:, :], writes=[self.b_ident])

    def stage_norm_T(self, l, first, hT, b_hT, gname):
        nc, S = self.nc, self.S
        I = self.I
        with contextlib.ExitStack() as es:
            sb = lambda n, s, d: es.enter_context(self.sbt(n, s, d))
            xt = [sb(f"n_xt{j}", [128, D], F32) for j in range(2)]
            hb = [sb(f"n_hb{j}", [128, D], BF16) for j in range(2)]
            junk = sb("n_junk", [128, D], BF16)
            gt = sb("n_gt", [128, D], F32)
            ss = [sb(f"n_ss{j}", [128, 2], F32) for j in range(2)]
            pt = [es.enter_context(self.pst(f"n_pt{j}", [128, D], BF16)) for j in range(2)]
            b_xt = [Buf(), Buf()]; b_hb = [Buf(), Buf()]; b_junk = Buf(); b_gt = Buf()
            b_ss = [Buf(), Buf()]; b_pt = [PBuf(), PBuf()]
            S.dma("sync", gt[:], dram_bcast(I[gname][l:l + 1, :], 128, D), writes=[b_gt])
            for ti, (c0, n) in enumerate(tok_tiles()):
                j = ti % 2
                if c0 < SEQ:
                    src = (I["xp"] if first else self.X["xres"])[c0:c0 + n, :]
                else:
                    src = I["xs"][:, :] if first else self.X["xres"][c0:c0 + n, :]
                S.dma("sync", xt[j][:n, :], src, writes=[b_xt[j]])
                S.op("scalar", lambda e: e.activation(out=junk[:n, :], in_=xt[j][:n, :], func=AF.Square,
                                                      scale=float(D ** -0.5), accum_out=ss[j][:n, 0:1]),
                     reads=[b_xt[j]], writes=[b_junk, b_ss[j]])
                S.op("vector", lambda e: e.tensor_scalar(out=ss[j][:n, 1:2], in0=ss[j][:n, 0:1], scalar1=1e-6,
                                                         scalar2=None, op0=ALU.add),
                     reads=[b_ss[j]], writes=[b_ss[j]])
                S.op("scalar", lambda e: e.activation(out=ss[j][:n, 0:1], in_=ss[j][:n, 1:2], func=AF.Sqrt),
                     reads=[b_ss[j]], writes=[b_ss[j]])
                S.op("vector", lambda e: e.reciprocal(out=ss[j][:n, 1:2], in_=ss[j][:n, 0:1]),
                     reads=[b_ss[j]], writes=[b_ss[j]])
                S.op("vector", lambda e: e.scalar_tensor_tensor(out=hb[j][:n, :], in0=xt[j][:n, :],
                                                                scalar=ss[j][:n, 1:2], in1=gt[:n, :],
                                                                op0=ALU.mult, op1=ALU.mult),
                     reads=[b_xt[j], b_ss[j], b_gt], writes=[b_hb[j]])
                for c in range(16):
                    S.op("tensor", lambda e: e.transpose(out=pt[j][:, c * 128:c * 128 + n],
                                                         in_=hb[j][:n, c * 128:(c + 1) * 128],
                                                         identity=self.identb[:n, :n]),
                         reads=[b_hb[j], self.b_ident], writes=[b_pt[j]], inc=(c == 15))
                src3 = pt[j][:, :].rearrange("p (c t) -> p c t", c=16)[:, :, 0:n]
                eng = "scalar" if ti % 2 == 0 else "vector"
                if eng == "scalar":
                    S.op("scalar", lambda e: e.copy(out=hT[:, :, c0:c0 + n], in_=src3),
                         reads=[b_pt[j]], writes=[b_hT])
                else:
                    S.op("vector", lambda e: e.tensor_copy(out=hT[:, :, c0:c0 + n], in_=src3),
                         reads=[b_pt[j]], writes=[b_hT])
            S.barrier()

    def stage_win(self, l, hT, b_hT):
        nc, S = self.nc, self.S
        I, O, X = self.I, self.O, self.X
        w_in = I["w_in"]
        blocks = []
        for c0 in range(0, 3584, 512):
            blocks.append((c0, 512, True, False))
        blocks.append((3584, 256, True, False))
        blocks.append((3840, 512, True, True))
        blocks.append((4352, 512, True, True))
        blocks.append((4864, 512, False, True))
        blocks.append((5376, 512, False, True))
        with contextlib.ExitStack() as es:
            sb = lambda n, s, d: es.enter_context(self.sbt(n, s, d))
            wb = [sb(f"w_wb{j}", [128, 16, 512], BF16) for j in range(2)]
            b_wb = [Buf(), Buf()]
            NST = 4
            st = [sb(f"w_st{j}", [128, 512], F32) for j in range(NST)]
            stb = [sb(f"w_stb{j}", [128, 512], BF16) for j in range(NST)]
            b_st = [Buf() for _ in range(NST)]
            b_stb = [Buf() for _ in range(NST)]
            NPS = 6
            ps = [es.enter_context(self.pst(f"w_ps{j}", [128, 512], F32)) for j in range(NPS)]
            b_ps = [PBuf() for _ in range(NPS)]
            cnt = {"ps": 0, "st": 0, "ev": 0}

            def load_block(bi):
                c0, ncols, fm, tm = blocks[bi]
                j = bi % 2
                src = w_in[l, :, c0:c0 + ncols].rearrange("(c p) n -> p c n", p=128)
                S.dma("gpsimd", wb[j][:, 0:8, 0:ncols], src[:, 0:8, :], writes=[b_wb[j]])
                S.dma("gpsimd", wb[j][:, 8:16, 0:ncols], src[:, 8:16, :], writes=[b_wb[j]])

            def evac(pj, n_p, n_f, dst_kind, dst_ap):
                k = cnt["st"] % NST
                cnt["st"] += 1
                use_act = cnt["ev"] % 2 == 0
                cnt["ev"] += 1
                if dst_kind == "f32":
                    tgt, bt = st[k], b_st[k]
                else:
                    tgt, bt = stb[k], b_stb[k]
                if use_act:
                    S.op("scalar", lambda e: e.copy(out=tgt[:n_p, :n_f], in_=ps[pj][:n_p, :n_f]),
                         reads=[b_ps[pj]], writes=[bt])
                else:
                    S.op("vector", lambda e: e.tensor_copy(out=tgt[:n_p, :n_f], in_=ps[pj][:n_p, :n_f]),
                         reads=[b_ps[pj]], writes=[bt])
                S.dma("sync", dst_ap, tgt[:n_p, :n_f], reads=[bt])

            load_block(0)
            for bi, (c0, ncols, fm, tm) in enumerate(blocks):
                j = bi % 2
                if bi + 1 < len(blocks):
                    load_block(bi + 1)
                if fm:
                    for ft in range(ncols // 128):
                        f0 = c0 + ft * 128
                        for (t0, nt) in tok_blocks():
                            pj = cnt["ps"] % NPS
                            cnt["ps"] += 1
                            for c in range(16):
                                S.op("tensor", lambda e: e.matmul(ps[pj][:, :nt], lhsT=wb[j][:, c, ft * 128:(ft + 1) * 128],
                                                                  rhs=hT[:, c, t0:t0 + nt], start=(c == 0), stop=(c == 15)),
                                     reads=[b_wb[j], b_hT], writes=[b_ps[pj]], inc=(c == 15))
                            if f0 < C_Q:
                                evac(pj, 128, nt, "f32", X["projT"][f0:f0 + 128, t0:t0 + nt])
                            elif f0 < C_K:
                                evac(pj, 128, nt, "bf16", X["qT"][f0 - C_Q:f0 - C_Q + 128, t0:t0 + nt])
                            else:
                                evac(pj, 128, nt, "bf16", X["kT"][f0 - C_K:f0 - C_K + 128, t0:t0 + nt])
                if tm:
                    for (t0, nt) in tok_tiles():
                        pj = cnt["ps"] % NPS
                        cnt["ps"] += 1
                        for c in range(16):
                            S.op("tensor", lambda e: e.matmul(ps[pj][:nt, :ncols], lhsT=hT[:, c, t0:t0 + nt],
                                                              rhs=wb[j][:, c, 0:ncols], start=(c == 0), stop=(c == 15)),
                                 reads=[b_wb[j], b_hT], writes=[b_ps[pj]], inc=(c == 15))
                        if c0 < C_V:
                            dp, dsm, off = O["k_p"], O["k_s"], c0 - C_K
                        else:
                            dp, dsm, off = O["v_p"], O["v_s"], c0 - C_V
                        if t0 < SEQ:
                            dst = dp[l, t0:t0 + nt, off:off + ncols]
                        else:
                            dst = dsm[l, 0:nt, off:off + ncols]
                        evac(pj, nt, ncols, "f32", dst)
            S.barrier()

    def fm_vec(self, ap1d, ntiles):
        return bass.AP(tensor=ap1d.tensor, offset=ap1d.offset, ap=[[1, 128], [128, ntiles]])

    def stage_lru(self, l):
        nc, S = self.nc, self.S
        I, O, X = self.I, self.O, self.X
        with contextlib.ExitStack() as es:
            sb = lambda n, s, d: es.enter_context(self.sbt(n, s, d))
            T = SEQ
            cw = sb("l_cw", [128, 4, 4], F32)
            cb = sb("l_cb", [128, 4], F32)
            ba = sb("l_ba", [128, 4], F32)
            bx = sb("l_bx", [128, 4], F32)
            lam = sb("l_lam", [128, 4], F32)
            sp = sb("l_sp", [128, 6, 4], F32)
            h0 = sb("l_h0", [128, 4], F32)
            wbd = sb("l_wbd", [128, 2, 4, 128], BF16)
            wk_names = ("xpad", "gb", "xc", "gr", "gi", "tmp", "hh", "xcb", "yb")
            wk = {}
            wkb = {}
            for nm_ in wk_names:
                shp_ = [128, T + 3] if nm_ == "xpad" else [128, T]
                dt_ = BF16 if nm_ in ("xcb", "yb") else F32
                wk[nm_] = [sb(f"l_{nm_}{jj_}", shp_, dt_) for jj_ in range(2)]
                wkb[nm_] = [Buf(), Buf()]
            lcnt = 0
            ps = [es.enter_context(self.pst(f"l_ps{j}", [128, 512], F32)) for j in range(2)]
            b_par = Buf(); b_wbd = Buf(); b_xpad = Buf(); b_gb = Buf(); b_xc = Buf(); b_gr = Buf(); b_gi = Buf()
            b_tmp = Buf(); b_hh = Buf(); b_xcb = Buf(); b_yb = Buf(); b_ps = [PBuf(), PBuf()]; b_h0 = Buf()
            nck = dict(allow_slow_non_contiguous=True)
            for tap in range(4):
                S.dma("sync", cw[:, tap, :], self.fm_vec(I["lru_conv_w"][l, tap, :], 4), writes=[b_par], **nck)
            S.dma("sync", cb[:, :], self.fm_vec(I["lru_conv_b"][l, :], 4), writes=[b_par], **nck)
            S.dma("sync", ba[:, :], self.fm_vec(I["lru_ba"][l, :], 4), writes=[b_par], **nck)
            S.dma("sync", bx[:, :], self.fm_vec(I["lru_bx"][l, :], 4), writes=[b_par], **nck)
            S.dma("sync", lam[:, :], self.fm_vec(I["lru_lambda"][l, :], 4), writes=[b_par], **nck)
            S.dma("sync", h0[:, :], self.fm_vec(I["st_lruh"][l, :], 4), writes=[b_h0], **nck)
            S.op("vector", lambda e: e.memset(wbd[:], 0.0), writes=[b_wbd])
            for gi_, wn in enumerate(("lru_wa", "lru_wx")):
                for n in range(8):
                    ct, hf = n // 2, n % 2
                    S.dma("gpsimd", wbd[hf * 64:(hf + 1) * 64, gi_, ct, hf * 64:(hf + 1) * 64], I[wn][l, n, :, :],
                          writes=[b_wbd])
            e_, ln_, ser, msk, res = sp[:, 0, :], sp[:, 1, :], sp[:, 2, :], sp[:, 3, :], sp[:, 4, :]
            S.op("scalar", lambda e: e.activation(out=e_, in_=lam[:, :], func=AF.Exp, scale=-1.0), reads=[b_par], writes=[b_par])
            S.op("scalar", lambda e: e.activation(out=ln_, in_=e_, func=AF.Ln, bias=1.0), reads=[b_par], writes=[b_par])
            S.op("vector", lambda e: e.tensor_scalar(out=ser, in0=e_, scalar1=-0.25, scalar2=1.0 / 3.0, op0=ALU.mult, op1=ALU.add), reads=[b_par], writes=[b_par])
            S.op("vector", lambda e: e.tensor_tensor(out=ser, in0=ser, in1=e_, op=ALU.mult), reads=[b_par], writes=[b_par])
            S.op("vector", lambda e: e.tensor_scalar(out=ser, in0=ser, scalar1=-1.0, scalar2=0.5, op0=ALU.mult, op1=ALU.add), reads=[b_par], writes=[b_par])
            S.op("vector", lambda e: e.tensor_tensor(out=ser, in0=ser, in1=e_, op=ALU.mult), reads=[b_par], writes=[b_par])
            S.op("vector", lambda e: e.tensor_scalar(out=ser, in0=ser, scalar1=-1.0, scalar2=1.0, op0=ALU.mult, op1=ALU.add), reads=[b_par], writes=[b_par])
            S.op("vector", lambda e: e.tensor_tensor(out=ser, in0=ser, in1=e_, op=ALU.mult), reads=[b_par], writes=[b_par])
            S.op("vector", lambda e: e.tensor_single_scalar(out=msk, in_=e_, scalar=0.05, op=ALU.is_lt), reads=[b_par], writes=[b_par])
            S.op("vector", lambda e: e.tensor_tensor(out=ser, in0=ser, in1=ln_, op=ALU.subtract), reads=[b_par], writes=[b_par])
            S.op("vector", lambda e: e.tensor_tensor(out=ser, in0=ser, in1=msk, op=ALU.mult), reads=[b_par], writes=[b_par])
            S.op("vector", lambda e: e.tensor_tensor(out=res, in0=ser, in1=ln_, op=ALU.add), reads=[b_par], writes=[b_par])
            S.op("vector", lambda e: e.tensor_scalar(out=res, in0=res, scalar1=-8.0, scalar2=None, op0=ALU.mult), reads=[b_par], writes=[b_par])
            m8sp = res

            items_ = [(col0, Tn, is_s, ct) for (col0, Tn, is_s) in ((0, SEQ, False), (SEQ, NS, True)) for ct in range(4)]

            def lru_loads(idx):
                col0, Tn, is_s, ct = items_[idx]
                jq = idx % 2
                xpad_, gb_ = wk["xpad"][jq], wk["gb"][jq]
                S.dma("sync", xpad_[:, 3:3 + Tn], X["projT"][C_LX + ct * 128:C_LX + (ct + 1) * 128, col0:col0 + Tn], writes=[wkb["xpad"][jq]])
                S.dma("sync", gb_[:, :Tn], X["projT"][C_LG + ct * 128:C_LG + (ct + 1) * 128, col0:col0 + Tn], writes=[wkb["gb"][jq]])
                if is_s:
                    src = bass.AP(tensor=I["st_conv"].tensor, offset=I["st_conv"][l, 0, ct * 128:(ct + 1) * 128].offset,
                                  ap=[[1, 128], [LW, 3]])
                    S.dma("sync", xpad_[:, 0:3], src, writes=[wkb["xpad"][jq]], **nck)
                else:
                    S.op("vector", lambda e: e.memset(xpad_[:, 0:3], 0.0), writes=[wkb["xpad"][jq]])

            lru_loads(0)
            for idx_, (col0, Tn, is_s, ct) in enumerate(items_):
                if True:
                    jb_ = idx_ % 2
                    xpad, gb, xc, gr, gi, tmp, hh, xcb, yb = (wk[nm_][jb_] for nm_ in wk_names)
                    b_xpad, b_gb, b_xc, b_gr, b_gi, b_tmp, b_hh, b_xcb, b_yb = (wkb[nm_][jb_] for nm_ in wk_names)
                    xa = xpad[:, 3:3 + Tn]
                    if idx_ + 1 < len(items_):
                        lru_loads(idx_ + 1)
                    S.op("vector", lambda e: e.tensor_scalar(out=xc[:, :Tn], in0=xa, scalar1=cw[:, 3, ct:ct + 1], scalar2=cb[:, ct:ct + 1],
                                                             op0=ALU.mult, op1=ALU.add), reads=[b_xpad, b_par], writes=[b_xc])
                    for tap in range(3):
                        S.op("vector", lambda e: e.scalar_tensor_tensor(out=xc[:, :Tn], in0=xpad[:, tap:tap + Tn], scalar=cw[:, tap, ct:ct + 1],
                                                                        in1=xc[:, :Tn], op0=ALU.mult, op1=ALU.add),
                             reads=[b_xpad, b_par, b_xc], writes=[b_xc])
                    S.op("scalar", lambda e: e.copy(out=xcb[:, :Tn], in_=xc[:, :Tn]), reads=[b_xc], writes=[b_xcb])
                    k = 0
                    for t0 in range(0, Tn, 512):
                        nt = min(512, Tn - t0)
                        for gsel, (dst, bd, bias) in enumerate(((gr, b_gr, ba), (gi, b_gi, bx))):
                            pj = k % 2
                            k += 1
                            S.op("tensor", lambda e: e.matmul(ps[pj][:, :nt], lhsT=wbd[:, gsel, ct, :], rhs=xcb[:, t0:t0 + nt], start=True, stop=True),
                                 reads=[b_wbd, b_xcb], writes=[b_ps[pj]])
                            S.op("scalar", lambda e: e.activation(out=dst[:, t0:t0 + nt], in_=ps[pj][:, :nt], func=AF.Sigmoid, bias=bias[:, ct:ct + 1]),
                                 reads=[b_ps[pj], b_par], writes=[bd])
                    S.op("vector", lambda e: e.tensor_scalar(out=gr[:, :Tn], in0=gr[:, :Tn], scalar1=m8sp[:, ct:ct + 1], scalar2=None, op0=ALU.mult),
                         reads=[b_gr, b_par], writes=[b_gr])
                    S.op("scalar", lambda e: e.activation(out=tmp[:, :Tn], in_=gr[:, :Tn], func=AF.Exp, scale=2.0), reads=[b_gr], writes=[b_tmp])
                    S.op("scalar", lambda e: e.activation(out=gr[:, :Tn], in_=gr[:, :Tn], func=AF.Exp), reads=[b_gr], writes=[b_gr])
                    S.op("vector", lambda e: e.tensor_scalar(out=tmp[:, :Tn], in0=tmp[:, :Tn], scalar1=-1.0, scalar2=1.0, op0=ALU.mult, op1=ALU.add),
                         reads=[b_tmp], writes=[b_tmp])
                    S.op("scalar", lambda e: e.activation(out=tmp[:, :Tn], in_=tmp[:, :Tn], func=AF.Sqrt), reads=[b_tmp], writes=[b_tmp])
                    S.op("vector", lambda e: e.tensor_tensor(out=gi[:, :Tn], in0=gi[:, :Tn], in1=xc[:, :Tn], op=ALU.mult), reads=[b_gi, b_xc], writes=[b_gi])
                    S.op("vector", lambda e: e.tensor_tensor(out=gi[:, :Tn], in0=gi[:, :Tn], in1=tmp[:, :Tn], op=ALU.mult), reads=[b_gi, b_tmp], writes=[b_gi])
                    init = h0[:, ct:ct + 1] if is_s else 0.0
                    S.op("vector", lambda e: e.tensor_tensor_scan(out=hh[:, :Tn], data0=gr[:, :Tn], data1=gi[:, :Tn], initial=init, op0=ALU.mult, op1=ALU.add),
                         reads=[b_gr, b_gi, b_h0], writes=[b_hh])
                    S.op("gpsimd", lambda e: e.tensor_tensor(out=tmp[:, :Tn], in0=gb[:, :Tn], in1=gb[:, :Tn], op=ALU.mult), reads=[b_gb], writes=[b_tmp])
                    S.op("gpsimd", lambda e: e.tensor_scalar(out=tmp[:, :Tn], in0=tmp[:, :Tn], scalar1=0.044715, scalar2=1.0, op0=ALU.mult, op1=ALU.add),
                         reads=[b_tmp], writes=[b_tmp])
                    S.op("gpsimd", lambda e: e.tensor_tensor(out=tmp[:, :Tn], in0=tmp[:, :Tn], in1=gb[:, :Tn], op=ALU.mult), reads=[b_tmp, b_gb], writes=[b_tmp])
                    S.op("scalar", lambda e: e.activation(out=tmp[:, :Tn], in_=tmp[:, :Tn], func=AF.Sigmoid, scale=1.5957691216057308), reads=[b_tmp], writes=[b_tmp])
                    S.op("gpsimd", lambda e: e.tensor_tensor(out=tmp[:, :Tn], in0=tmp[:, :Tn], in1=gb[:, :Tn], op=ALU.mult), reads=[b_tmp, b_gb], writes=[b_tmp])
                    S.op("vector", lambda e: e.tensor_tensor(out=yb[:, :Tn], in0=tmp[:, :Tn], in1=hh[:, :Tn], op=ALU.mult), reads=[b_tmp, b_hh], writes=[b_yb])
                    S.dma("sync", X["yT"][RW + ct * 128:RW + (ct + 1) * 128, col0:col0 + Tn], yb[:, :Tn], reads=[b_yb])
                    oh = O["lruh_s"] if is_s else O["lruh_p"]
                    oc = O["conv_s"] if is_s else O["conv_p"]
                    S.dma("sync", bass.AP(tensor=oh.tensor, offset=oh[l, ct * 128:(ct + 1) * 128].offset, ap=[[1, 128], [1, 1]]),
                          hh[:, Tn - 1:Tn], reads=[b_hh], **nck)
                    S.dma("sync", bass.AP(tensor=oc.tensor, offset=oc[l, 0, ct * 128:(ct + 1) * 128].offset, ap=[[1, 128], [LW, 3]]),
                          xpad[:, Tn:Tn + 3], reads=[b_xpad], **nck)
            S.barrier()

    def vv(self, out, in0, in1, op, R, W, eng="vector"):
        return self.S.op(eng, lambda e: e.tensor_tensor(out=out, in0=in0, in1=in1, op=op), reads=R, writes=W)

    def vs(self, out, in0, s1, s2, op0, op1, R, W, eng="vector"):
        if op1 is None:
            return self.S.op(eng, lambda e: e.tensor_scalar(out=out, in0=in0, scalar1=s1, scalar2=None, op0=op0), reads=R, writes=W)
        return self.S.op(eng, lambda e: e.tensor_scalar(out=out, in0=in0, scalar1=s1, scalar2=s2, op0=op0, op1=op1), reads=R, writes=W)

    def stt(self, out, in0, scalar, in1, op0, op1, R, W):
        return self.S.op("vector", lambda e: e.scalar_tensor_tensor(out=out, in0=in0, scalar=scalar, in1=in1, op0=op0, op1=op1), reads=R, writes=W)

    def act(self, out, in_, func, R, W, **kw):
        return self.S.op("scalar", lambda e: e.activation(out=out, in_=in_, func=func, **kw), reads=R, writes=W)

    def cp(self, out, in_, R, W, eng="vector"):
        if eng == "scalar":
            return self.S.op("scalar", lambda e: e.copy(out=out, in_=in_), reads=R, writes=W)
        return self.S.op(eng, lambda e: e.tensor_copy(out=out, in_=in_), reads=R, writes=W)

    def pe_fence(self):
        self.S.op("tensor", lambda e: e.matmul(self.ps_dummy[0:1, 0:1], lhsT=self.identb[:, 0:1], rhs=self.identb[:, 0:1], start=True, stop=True),
                  reads=[self.b_ident], writes=[], inc=True)

    def mm(self, out, lhsT, rhs, R, W, start=True, stop=True, inc=None, f32r=False):
        if inc is None:
            inc = stop
        guard = inc and (lhsT.dtype == F32) and not (f32r and self.stages.get('nofence_r', True))
        if f32r:
            lhsT = lhsT.bitcast(F32R)
            rhs = rhs.bitcast(F32R)
        t = self.S.op("tensor", lambda e: e.matmul(out, lhsT=lhsT, rhs=rhs, start=start, stop=stop), reads=R, writes=W, inc=(inc and not guard))
        if guard:
            self.pe_fence()
        return t

    def tr(self, out, in_, ident, R, W, inc=True):
        guard = inc and (in_.dtype == F32)
        t = self.S.op("tensor", lambda e: e.transpose(out=out, in_=in_, identity=ident), reads=list(R) + [self.b_ident], writes=W, inc=(inc and not guard))
        if guard:
            self.pe_fence()
        return t

    def getps(self):
        i = self.ps_next
        self.ps_next = (i + 1) % len(self.ps_pool)
        return self.ps_pool[i], self.ps_bufs[i]

    @staticmethod
    def bc_last(a, k):
        return bass.AP(tensor=a.tensor, offset=a.offset, ap=[list(x) for x in a.ap] + [[0, k]])

    def stage_rwkv(self, l):
        nc, S = self.nc, self.S
        I, O, X = self.I, self.O, self.X
        vv, vs, stt, act, cp, mm, tr = self.vv, self.vs, self.stt, self.act, self.cp, self.mm, self.tr
        nck = dict(allow_slow_non_contiguous=True)
        with contextlib.ExitStack() as es:
            sb = lambda n, s, d: es.enter_context(self.sbt(n, s, d))
            allps = [es.enter_context(self.pst(f"r_ps{j}", [128, 512], F32)) for j in range(7)]
            allpb = [PBuf() for _ in range(7)]
            own_ps, own_pb = allps[0:4], allpb[0:4]
            self.ps_pool = allps[4:7]
            self.ps_bufs = allpb[4:7]
            self.ps_next = 0
            self.ps_dummy = es.enter_context(self.pst("r_psd", [128, 512], F32))
            mu = sb("r_mu", [128, 14], F32)
            lup = sb("r_lup", [128, 512], BF16)
            gup = sb("r_gup", [128, 512], BF16)
            bpar = sb("r_bpar", [128, 7, 512], F32)
            tri = sb("r_tri", [128, 128], F32)
            stri = sb("r_stri", [128, 128], F32)
            ltri = sb("r_ltri", [128, 128], F32)
            ones = sb("r_ones", [128, 1], F32)
            ST = sb("r_ST", [64, 8, 64], F32)
            STd = sb("r_STd", [64, 8, 64], F32)
            STr = sb("r_STr", [64, 8, 64], F32); b_STr_all = [Buf() for _ in range(8)]
            b_par = Buf(); b_STall = [Buf() for _ in range(8)]; b_STd_all = [Buf() for _ in range(8)]
            S.dma("sync", mu[:, :], self.fm_vec(I["rwkv_mu"][l, :], 14), writes=[b_par], **nck)
            S.dma("gpsimd", lup[0:64, :], I["rwkv_w_up"][l, :, :], writes=[b_par])
            S.dma("gpsimd", lup[64:128, :], I["rwkv_a_up"][l, :, :], writes=[b_par])
            S.dma("gpsimd", gup[:, :], I["rwkv_g_up"][l, :, :], writes=[b_par])
            for i, nm in enumerate(("rwkv_w0", "rwkv_a0", "rwkv_k_k", "rwkv_k_a", "rwkv_r_k", "rwkv_lnx_g", "rwkv_lnx_b")):
                S.dma("sync", bpar[:, i, :], dram_bcast(I[nm][l:l + 1, :], 128, RW), writes=[b_par])
            S.dma("sync", tri[:, :], I["c_tri"][:, :], writes=[b_par])
            S.dma("sync", stri[:, :], I["c_stri"][:, :], writes=[b_par])
            S.dma("sync", ltri[:, :], I["c_ltri"][:, :], writes=[b_par])
            S.op("vector", lambda e: e.memset(ones[:], 1.0), writes=[b_par])
            W0, A0, KK_, KA_, RK_, LG_, LB_ = (bpar[:, i, :] for i in range(7))

            def T2(name, shape, dt=F32, single=False):
                if single:
                    t_ = sb(name, shape, dt); b_ = Buf()
                    return [t_, t_], [b_, b_]
                return [sb(f"{name}{j}", shape, dt) for j in range(2)], [Buf(), Buf()]

            pT, b_pT = T2("r_pT", [128, 14, 129])
            mT, b_mT = T2("r_mT", [128, 14, 128], single=True)
            dT, b_dT = T2("r_dT", [128, 14, 128], single=True)
            lin, b_lin = T2("r_lin", [128, 2, 128], BF16)
            names = ["r", "k", "v", "g", "a", "kk", "k2", "nlw", "epos", "eneg", "eprev", "ah", "bh", "kh", "rh", "t1", "t2", "y", "bon", "vr"]
            tm = {}; b_tm = {}
            for nm in names:
                tm[nm], b_tm[nm] = T2("r_tm_" + nm, [128, 512], single=(nm in ("kk", "k2", "epos", "eneg", "eprev", "t1", "t2", "vr")))
            sm, b_sm = T2("r_sm", [128, 8, 8])
            fmT, b_fmT = T2("r_fmT", [64, 8, 4, 128])
            pc, b_pc = T2("r_pc", [64, 8])
            yb, b_yb = T2("r_yb", [128, 512], BF16)
            ybT, b_ybT = T2("r_ybT", [128, 4, 128], BF16)
            NH = 4
            Am = [sb(f"r_Am{j}", [128, 4, 128], F32) for j in range(NH)]; b_Am = [Buf() for _ in range(NH)]
            Mx = [sb(f"r_Mx{j}", [128, 2, 128], F32) for j in range(NH)]; b_Mx = [Buf() for _ in range(NH)]
            Nx = [sb(f"r_Nx{j}", [128, 2, 128], F32) for j in range(NH)]; b_Nx = [Buf() for _ in range(NH)]
            Tt = [sb(f"r_Tt{j}", [128, 2, 128], F32) for j in range(NH)]; b_Tt = [Buf() for _ in range(NH)]
            akv = [sb(f"r_akv{j}", [128, 64], F32) for j in range(NH)]; b_akv = [Buf() for _ in range(NH)]
            wmT = [sb(f"r_wmT{j}", [64, 128], F32) for j in range(NH)]; b_wmT = [Buf() for _ in range(NH)]
            U = [sb(f"r_U{j}", [128, 64], F32) for j in range(NH)]; b_U = [Buf() for _ in range(NH)]
            sinit = sb("r_sinit", [64, 8, 64], F32); b_sinit = Buf()
            sout = sinit; b_sout = b_sinit
            hcnt = [0]

            for (col0, Tn, is_s) in ((0, SEQ, False), (SEQ, NS, True)):
                if is_s and not self.stages.get('rwkv_sample', True):
                    continue
                if is_s:
                    S.dma("sync", sinit[:, :, :], I["st_wkv"][l].rearrange("h v k -> v h k"), writes=[b_sinit])
                    for h in range(RH):
                        ps, bp = self.getps()
                        tr(ps[:64, :64], sinit[:, h, :], self.identf[:64, :64], [b_sinit], [bp])
                        cp(ST[:, h, :], ps[:64, :64], [bp], [b_STall[h]])
                else:
                    S.op("vector", lambda e: e.memset(ST[:], 0.0), writes=b_STall)
                cp((STr[:, :, :].bitcast(F32R) if self.stages.get('fp32r', True) else STr[:, :, :]), ST[:, :, :], b_STall, b_STr_all)
                nchunks = (Tn + 127) // 128
                if self.stages.get("rwkv_chunks"):
                    nchunks = min(nchunks, self.stages["rwkv_chunks"])
                for ci in range(nchunks):
                    j = ci % 2
                    t0 = col0 + ci * 128
                    n = min(128, Tn - ci * 128)
                    B = {k_: b_tm[k_][j] for k_ in names}
                    use_r = (n == 128) and self.stages.get('fp32r', True)
                    R_ = (lambda a_: a_.bitcast(F32R)) if self.stages.get('fp32r', True) else (lambda a_: a_)
                    Tm = {k_: tm[k_][j] for k_ in names}
                    if ci == 0:
                        S.dma("sync", pT[j][:, :, 1:n + 1], X["projT"][0:RPROJ, t0:t0 + n].rearrange("(f p) t -> p f t", p=128), writes=[b_pT[j]])
                        if is_s:
                            S.dma("sync", pT[j][:, :, 0], self.fm_vec(I["st_shift"][l, :], 14), writes=[b_pT[j]], **nck)
                        else:
                            S.op("vector", lambda e: e.memset(pT[j][:, :, 0:1], 0.0), writes=[b_pT[j]])
                    else:
                        S.dma("sync", pT[j][:, :, 0:n + 1], X["projT"][0:RPROJ, t0 - 1:t0 + n].rearrange("(f p) t -> p f t", p=128), writes=[b_pT[j]])
                    if ci == nchunks - 1:
                        osh = O["shift_s"] if is_s else O["shift_p"]
                        S.dma("sync", self.fm_vec(osh[l, :], 14), pT[j][:, :, n], reads=[b_pT[j]], **nck)
                    if self.stages.get('rwkv_upto', 9) < 2:
                        continue
                    pcur = pT[j][:, :, 1:n + 1]
                    vv(dT[j][:, :, :n], pT[j][:, :, 0:n], pcur, ALU.subtract, [b_pT[j]], [b_dT[j]])
                    vv(dT[j][:, :, :n], dT[j][:, :, :n], self.bc_last(mu[:, :], n), ALU.mult, [b_dT[j], b_par], [b_dT[j]])
                    vv(mT[j][:, :, :n], dT[j][:, :, :n], pcur, ALU.add, [b_dT[j], b_pT[j]], [b_mT[j]])
                    if self.stages.get('rwkv_upto', 9) < 3:
                        continue
                    act(lin[j][0:64, 0, :n], mT[j][0:64, 12, :n], AF.Tanh, [b_mT[j]], [b_lin[j]])
                    cp(lin[j][64:128, 0, :n], mT[j][64:128, 12, :n], [b_mT[j]], [b_lin[j]], eng="gpsimd")
                    act(lin[j][:, 1, :n], mT[j][:, 13, :n], AF.Sigmoid, [b_mT[j]], [b_lin[j]])
                    ps_w, bp_w = self.getps()
                    mm(ps_w[:n, :], lin[j][0:64, 0, :n], lup[0:64, :], [b_lin[j], b_par], [bp_w])
                    ps_a, bp_a = self.getps()
                    mm(ps_a[:n, :], lin[j][64:128, 0, :n], lup[64:128, :], [b_lin[j], b_par], [bp_a])
                    ps_g, bp_g = self.getps()
                    mm(ps_g[:n, :], lin[j][:, 1, :n], gup[:, :], [b_lin[j], b_par], [bp_g])
                    vv(Tm["t1"][:n, :], ps_w[:n, :], W0[:n, :], ALU.add, [bp_w, b_par], [B["t1"]])
                    vv(Tm["a"][:n, :], ps_a[:n, :], A0[:n, :], ALU.add, [bp_a, b_par], [B["a"]])
                    cp(Tm["g"][:n, :], ps_g[:n, :], [bp_g], [B["g"]], eng="scalar")
                    if self.stages.get('rwkv_upto', 9) < 4:
                        continue
                    for ti_, nm in enumerate(("r", "k", "v")):
                        ps, bp = self.getps()
                        for q in range(4):
                            tr(ps[:n, q * 128:(q + 1) * 128], mT[j][:, ti_ * 4 + q, :n], self.identf[:, :], [b_mT[j]], [bp], inc=(q == 3))
                        cp(Tm[nm][:n, :], ps[:n, :], [bp], [B[nm]], eng=("scalar" if ti_ == 1 else "vector"))
                    cp(R_(Tm["vr"][:n, :]), Tm["v"][:n, :], [B["v"]], [B["vr"]], eng="scalar")
                    if self.stages.get('rwkv_upto', 9) < 5:
                        continue
                    act(Tm["t1"][:n, :], Tm["t1"][:n, :], AF.Exp, [B["t1"]], [B["t1"]], scale=-1.0)
                    act(Tm["t1"][:n, :], Tm["t1"][:n, :], AF.Ln, [B["t1"]], [B["t1"]], bias=1.0)
                    vs(Tm["t1"][:n, :], Tm["t1"][:n, :], -1.0, -0.5, ALU.mult, ALU.add, [B["t1"]], [B["t1"]])
                    act(Tm["nlw"][:n, :], Tm["t1"][:n, :], AF.Exp, [B["t1"]], [B["nlw"]])
                    act(Tm["a"][:n, :], Tm["a"][:n, :], AF.Sigmoid, [B["a"]], [B["a"]])
                    vv(Tm["kk"][:n, :], Tm["k"][:n, :], KK_[:n, :], ALU.mult, [B["k"], b_par], [B["kk"]])
                    vv(Tm["t2"][:n, :], Tm["kk"][:n, :], Tm["kk"][:n, :], ALU.mult, [B["kk"]], [B["t2"]], eng="gpsimd")
                    S.op("vector", lambda e: e.reduce_sum(out=sm[j][:n, 0, :], in_=Tm["t2"][:n, :].rearrange("p (h k) -> p h k", h=8), axis=AX.X),
                         reads=[B["t2"]], writes=[b_sm[j]])
                    act(sm[j][:n, 0, :], sm[j][:n, 0, :], AF.Sqrt, [b_sm[j]], [b_sm[j]])
                    vs(sm[j][:n, 0, :], sm[j][:n, 0, :], 1e-12, None, ALU.max, None, [b_sm[j]], [b_sm[j]])
                    S.op("vector", lambda e: e.reciprocal(out=sm[j][:n, 1, :], in_=sm[j][:n, 0, :]), reads=[b_sm[j]], writes=[b_sm[j]])
                    vv(Tm["kk"][:n, :].rearrange("p (h k) -> p h k", h=8), Tm["kk"][:n, :].rearrange("p (h k) -> p h k", h=8),
                       self.bc_last(sm[j][:n, 1, :], 64), ALU.mult, [B["kk"], b_sm[j]], [B["kk"]])
                    stt(Tm["t2"][:n, :], Tm["a"][:n, :], -1.0, KA_[:n, :], ALU.add, ALU.mult, [B["a"], b_par], [B["t2"]])
                    stt(Tm["k2"][:n, :], Tm["t2"][:n, :], 1.0, Tm["k"][:n, :], ALU.add, ALU.mult, [B["t2"], B["k"]], [B["k2"]])
                    ps_c, bp_c = self.getps()
                    mm(ps_c[:n, :], tri[:n, :n], Tm["nlw"][:n, :], [b_par, B["nlw"]], [bp_c])
                    act(Tm["epos"][:n, :], ps_c[:n, :], AF.Exp, [bp_c], [B["epos"]], scale=-1.0)
                    act(Tm["eneg"][:n, :], ps_c[:n, :], AF.Exp, [bp_c], [B["eneg"]])
                    vv(Tm["t1"][:n, :], ps_c[:n, :], Tm["nlw"][:n, :], ALU.subtract, [bp_c, B["nlw"]], [B["t1"]])
                    act(Tm["eprev"][:n, :], Tm["t1"][:n, :], AF.Exp, [B["t1"]], [B["eprev"]], scale=-1.0)
                    stt(Tm["ah"][:n, :], Tm["kk"][:n, :], -1.0, Tm["eprev"][:n, :], ALU.mult, ALU.mult, [B["kk"], B["eprev"]], [B["ah"]])
                    vv(Tm["bh"][:n, :], Tm["kk"][:n, :], Tm["a"][:n, :], ALU.mult, [B["kk"], B["a"]], [B["bh"]], eng="gpsimd")
                    vv(Tm["bh"][:n, :], Tm["bh"][:n, :], Tm["eneg"][:n, :], ALU.mult, [B["bh"], B["eneg"]], [B["bh"]], eng="gpsimd")
                    vv(Tm["kh"][:n, :], Tm["k2"][:n, :], Tm["eneg"][:n, :], ALU.mult, [B["k2"], B["eneg"]], [B["kh"]])
                    vv(Tm["rh"][:n, :], Tm["r"][:n, :], Tm["epos"][:n, :], ALU.mult, [B["r"], B["epos"]], [B["rh"]], eng="gpsimd")
                    vv(Tm["t2"][:n, :], Tm["r"][:n, :], Tm["k2"][:n, :], ALU.mult, [B["r"], B["k2"]], [B["t2"]], eng="gpsimd")
                    vv(Tm["t2"][:n, :], Tm["t2"][:n, :], RK_[:n, :], ALU.mult, [B["t2"], b_par], [B["t2"]], eng="gpsimd")
                    S.op("vector", lambda e: e.reduce_sum(out=sm[j][:n, 2, :], in_=Tm["t2"][:n, :].rearrange("p (h k) -> p h k", h=8), axis=AX.X),
                         reads=[B["t2"]], writes=[b_sm[j]])
                    vv(Tm["bon"][:n, :].rearrange("p (h k) -> p h k", h=8), Tm["v"][:n, :].rearrange("p (h k) -> p h k", h=8),
                       self.bc_last(sm[j][:n, 2, :], 64), ALU.mult, [B["v"], b_sm[j]], [B["bon"]])
                    if self.stages.get('rwkv_upto', 9) < 6:
                        continue
                    for h in range(RH):
                        ps, bp = self.getps()
                        for q, nm in enumerate(("ah", "rh", "bh", "kh")):
                            tr(ps[:64, q * 128:q * 128 + n], Tm[nm][:n, h * 64:(h + 1) * 64], self.identf[:n, :n], [B[nm]], [bp], inc=(q == 3))
                        cp(R_(fmT[j][:, h, :, :n]), ps[:64, :].rearrange("p (q t) -> p q t", q=4)[:, :, :n], [bp], [b_fmT[j]],
                           eng=("scalar" if h % 2 == 0 else "vector"))
                    ps_p, bp_p = self.getps()
                    for h in range(RH):
                        mm(ps_p[:64, h:h + 1], Tm["nlw"][:n, h * 64:(h + 1) * 64], ones[:n, :], [B["nlw"], b_par], [bp_p], inc=(h == RH - 1))
                    act(pc[j][:, :], ps_p[:64, 0:8], AF.Exp, [bp_p], [b_pc[j]], scale=-1.0)
                    if self.stages.get('rwkv_upto', 9) < 7:
                        continue
                    nsq = max(0, int(np.ceil(np.log2(max(n, 2)))) - 1)
                    if True:
                        def head_gen(h, hj):
                            aT = fmT[j][:, h, 0, :n]; rT = fmT[j][:, h, 1, :n]; bT = fmT[j][:, h, 2, :n]; kT = fmT[j][:, h, 3, :n]
                            arT = fmT[j][:, h, 0:2, :n]
                            v_h = Tm["v"][:n, h * 64:(h + 1) * 64]
                            vr_h = Tm["vr"][:n, h * 64:(h + 1) * 64]
                            ps, bp = (own_ps[h % 4], own_pb[h % 4])
                            o4 = ps[:n, :].rearrange("p (q t) -> p q t", q=4)
                            if not self.stages.get('dbg_nomm'):
                                mm(o4[:, 0, :n], bT, aT, [b_fmT[j]], [bp], inc=False, f32r=use_r)
                                mm(o4[:, 1, :n], bT, rT, [b_fmT[j]], [bp], inc=False, f32r=use_r)
                                mm(o4[:, 2, :n], kT, aT, [b_fmT[j]], [bp], inc=False, f32r=use_r)
                                mm(o4[:, 3, :n], kT, rT, [b_fmT[j]], [bp], f32r=use_r)
                            if self.stages.get('dbg_nomask'):
                                return
                            vv(R_(Am[hj][:n, 0, :n]), o4[:, 0, :n], stri[:n, :n], ALU.mult, [bp, b_par], [b_Am[hj]])
                            vv(R_(Am[hj][:n, 1, :n]), o4[:, 1, :n], tri[:n, :n], ALU.mult, [bp, b_par], [b_Am[hj]])
                            vv(R_(Am[hj][:n, 2, :n]), o4[:, 2, :n], stri[:n, :n], ALU.mult, [bp, b_par], [b_Am[hj]])
                            vv(R_(Am[hj][:n, 3, :n]), o4[:, 3, :n], tri[:n, :n], ALU.mult, [bp, b_par], [b_Am[hj]])
                            if self.stages.get('rwkv_sub', 9) < 1:
                                return
                            yield
                            ps2, bp2 = (own_ps[h % 4], own_pb[h % 4])
                            mm(ps2[:n, :n], aT, bT, [b_fmT[j]], [bp2], f32r=use_r)
                            vv(R_(Nx[hj][:n, 0, :n]), ps2[:n, :n], ltri[:n, :n], ALU.mult, [bp2, b_par], [b_Nx[hj]])
                            if self.stages.get('rwkv_sub', 9) < 2:
                                return
                            yield
                            vv(R_(Tt[hj][:n, 0, :n]), Am[hj][:n, 0, :n], self.identf[:n, :n], ALU.add, [b_Am[hj], self.b_ident], [b_Tt[hj]])
                            cp(R_(Mx[hj][:n, 0, :n]), Am[hj][:n, 0, :n], [b_Am[hj]], [b_Mx[hj]], eng="scalar")
                            cur = 0
                            for it in range(nsq):
                                nxt = 1 - cur
                                ps3, bp3 = (own_ps[h % 4], own_pb[h % 4])
                                mm(ps3[:n, 0:n], Nx[hj][:n, cur, :n], Mx[hj][:n, cur, :n], [b_Nx[hj], b_Mx[hj]], [bp3], inc=False, f32r=use_r)
                                mm(ps3[:n, 128:128 + n], Mx[hj][:n, cur, :n], Nx[hj][:n, cur, :n], [b_Nx[hj], b_Mx[hj]], [bp3], f32r=use_r)
                                yield
                                cp(R_(Mx[hj][:n, nxt, :n]), ps3[:n, 0:n], [bp3], [b_Mx[hj]], eng="scalar")
                                cp(R_(Nx[hj][:n, nxt, :n]), ps3[:n, 128:128 + n], [bp3], [b_Nx[hj]], eng="vector")
                                ps4, bp4 = (own_ps[h % 4], own_pb[h % 4])
                                mm(ps4[:n, :n], Nx[hj][:n, nxt, :n], Tt[hj][:n, cur, :n], [b_Nx[hj], b_Tt[hj]], [bp4], f32r=use_r)
                                yield
                                vv(R_(Tt[hj][:n, nxt, :n]), ps4[:n, :n], Tt[hj][:n, cur, :n], ALU.add, [bp4, b_Tt[hj]], [b_Tt[hj]])
                                yield
                                cur = nxt
                            if self.stages.get('rwkv_sub', 9) < 3:
                                return
                            TT_ = Tt[hj][:n, cur, :n]
                            ps5, bp5 = (own_ps[h % 4], own_pb[h % 4])
                            mm(ps5[:n, 0:64], Am[hj][:n, 2, :n], vr_h, [b_Am[hj], B["vr"]], [bp5], f32r=use_r)
                            yield
                            cp(R_(akv[hj][:n, :]), ps5[:n, 0:64], [bp5], [b_akv[hj]], eng="scalar")
                            ps6, bp6 = (own_ps[h % 4], own_pb[h % 4])
                            mm(ps6[:64, :n], Tm["ah"][:n, h * 64:(h + 1) * 64], TT_, [B["ah"], b_Tt[hj]], [bp6])
                            cp(R_(wmT[hj][:, :n]), ps6[:64, :n], [bp6], [b_wmT[hj]], eng="vector")
                            if self.stages.get('rwkv_sub', 9) < 4:
                                return
                            yield
                            ps7, bp7 = (own_ps[h % 4], own_pb[h % 4])
                            mm(ps7[:n, 0:64], TT_, akv[hj][:n, :], [b_Tt[hj], b_akv[hj]], [bp7], start=True, stop=False, inc=False, f32r=use_r)
                            mm(ps7[:n, 0:64], wmT[hj][:, :n], STr[:, h, :], [b_wmT[hj], b_STr_all[h]], [bp7], start=False, stop=True, f32r=use_r)
                            yield
                            cp(R_(U[hj][:n, :]), ps7[:n, 0:64], [bp7], [b_U[hj]], eng="scalar")
                            if self.stages.get('rwkv_sub', 9) < 5:
                                return
                            yield
                            ps8, bp8 = (own_ps[h % 4], own_pb[h % 4])
                            mm(ps8[:n, 0:64], rT, STr[:, h, :], [b_fmT[j], b_STr_all[h]], [bp8], start=True, stop=False, inc=False, f32r=use_r)
                            mm(ps8[:n, 0:64], Am[hj][:n, 1, :n], U[hj][:n, :], [b_Am[hj], b_U[hj]], [bp8], start=False, stop=False, inc=False, f32r=use_r)
                            mm(ps8[:n, 0:64], Am[hj][:n, 3, :n], vr_h, [b_Am[hj], B["vr"]], [bp8], start=False, stop=True, f32r=use_r)
                            yield
                            cp(Tm["y"][:n, h * 64:(h + 1) * 64], ps8[:n, 0:64], [bp8], [B["y"]], eng="vector")
                            if self.stages.get('rwkv_sub', 9) < 6:
                                return
                            vs(STd[:, h, :], ST[:, h, :], pc[j][:, h:h + 1], None, ALU.mult, None, [b_STall[h], b_pc[j]], [b_STd_all[h]], eng="gpsimd")
                            ps9, bp9 = (own_ps[h % 4], own_pb[h % 4])
                            mm(ps9[:64, 0:64], Tm["bh"][:n, h * 64:(h + 1) * 64], U[hj][:n, :], [B["bh"], b_U[hj]], [bp9], start=True, stop=False, inc=False)
                            mm(ps9[:64, 0:64], Tm["kh"][:n, h * 64:(h + 1) * 64], v_h, [B["kh"], B["v"]], [bp9], start=False, stop=True)
                            yield
                            stt(ST[:, h, :], ps9[:64, 0:64], pc[j][:, h:h + 1], STd[:, h, :], ALU.mult, ALU.add, [bp9, b_pc[j], b_STd_all[h]], [b_STall[h]])
                            cp(R_(STr[:, h, :]), ST[:, h, :], [b_STall[h]], [b_STr_all[h]], eng="scalar")
                            yield
                        for grp in range(2):
                            gens = [head_gen(h, h % NH) for h in range(grp * 4, grp * 4 + 4)]
                            if not self.stages.get("rwkv_interleave", True):
                                for g_ in gens:
                                    for _ in g_:
                                        pass
                                gens = []
                            while gens:
                                alive = []
                                for g_ in gens:
                                    try:
                                        next(g_)
                                        alive.append(g_)
                                    except StopIteration:
                                        pass
                                gens = alive
                    if self.stages.get('rwkv_upto', 9) < 8:
                        continue
                    y3 = Tm["y"][:n, :].rearrange("p (h k) -> p h k", h=8)
                    S.op("vector", lambda e: e.reduce_sum(out=sm[j][:n, 3, :], in_=y3, axis=AX.X), reads=[B["y"]], writes=[b_sm[j]])
                    vs(sm[j][:n, 3, :], sm[j][:n, 3, :], 1.0 / 64, None, ALU.mult, None, [b_sm[j]], [b_sm[j]])
                    vv(y3, y3, self.bc_last(sm[j][:n, 3, :], 64), ALU.subtract, [B["y"], b_sm[j]], [B["y"]])
                    vv(Tm["t1"][:n, :], Tm["y"][:n, :], Tm["y"][:n, :], ALU.mult, [B["y"]], [B["t1"]], eng="gpsimd")
                    S.op("vector", lambda e: e.reduce_sum(out=sm[j][:n, 4, :], in_=Tm["t1"][:n, :].rearrange("p (h k) -> p h k", h=8), axis=AX.X),
                         reads=[B["t1"]], writes=[b_sm[j]])
                    vs(sm[j][:n, 4, :], sm[j][:n, 4, :], 1.0 / 64, 64e-5, ALU.mult, ALU.add, [b_sm[j]], [b_sm[j]])
                    act(sm[j][:n, 4, :], sm[j][:n, 4, :], AF.Sqrt, [b_sm[j]], [b_sm[j]])
                    S.op("vector", lambda e: e.reciprocal(out=sm[j][:n, 5, :], in_=sm[j][:n, 4, :]), reads=[b_sm[j]], writes=[b_sm[j]])
                    vv(y3, y3, self.bc_last(sm[j][:n, 5, :], 64), ALU.mult, [B["y"], b_sm[j]], [B["y"]])
                    vv(Tm["y"][:n, :], Tm["y"][:n, :], LG_[:n, :], ALU.mult, [B["y"], b_par], [B["y"]])
                    vv(Tm["y"][:n, :], Tm["y"][:n, :], LB_[:n, :], ALU.add, [B["y"], b_par], [B["y"]])
                    vv(Tm["y"][:n, :], Tm["y"][:n, :], Tm["bon"][:n, :], ALU.add, [B["y"], B["bon"]], [B["y"]])
                    vv(yb[j][:n, :], Tm["y"][:n, :], Tm["g"][:n, :], ALU.mult, [B["y"], B["g"]], [b_yb[j]])
                    pst, bpt = self.getps()
                    pstb = pst[:, :].bitcast(BF16)
                    for q in range(4):
                        tr(pstb[:, q * 128:q * 128 + n], yb[j][:n, q * 128:(q + 1) * 128], self.identb[:n, :n], [b_yb[j]], [bpt], inc=(q == 3))
                    cp(ybT[j][:, :, :n], pstb[:, 0:512].rearrange("p (q t) -> p q t", q=4)[:, :, :n], [bpt], [b_ybT[j]], eng="scalar")
                    S.dma("sync", X["yT"][0:RW, t0:t0 + n].rearrange("(q p) t -> p q t", p=128), ybT[j][:, :, :n], reads=[b_ybT[j]])
                for h in range(RH):
                    ps, bp = self.getps()
                    tr(ps[:64, :64], ST[:, h, :], self.identf[:64, :64], [b_STall[h]], [bp])
                    cp(sout[:, h, :], ps[:64, :64], [bp], [b_sout])
                ow = O["wkv_s"] if is_s else O["wkv_p"]
                S.dma("sync", ow[l].rearrange("h v k -> v h k"), sout[:, :, :], reads=[b_sout])
            S.barrier()

    @staticmethod
    def strided(t3, h, start, step, count):
        base = t3[:, h, start:start + 1]
        return bass.AP(tensor=base.tensor, offset=base.offset, ap=[list(base.ap[0]), [step, count]])

    def stage_attn(self, l):
        nc, S = self.nc, self.S
        I, O, X = self.I, self.O, self.X
        vv, vs, stt, act, cp, mm, tr = self.vv, self.vs, self.stt, self.act, self.cp, self.mm, self.tr
        nck = dict(allow_slow_non_contiguous=True)
        SC = float(AD ** -0.5)
        DILS = (1, 4, 16)
        with contextlib.ExitStack() as es:
            sb = lambda n, s, d: es.enter_context(self.sbt(n, s, d))
            self.ps_pool = [es.enter_context(self.pst(f"a_ps{j}", [128, 512], F32)) for j in range(7)]
            self.ps_bufs = [PBuf() for _ in range(7)]
            self.ps_next = 0
            self.ps_dummy = es.enter_context(self.pst("a_psd", [128, 512], F32))
            relb = sb("a_relb", [32, 8], F32); E = sb("a_E", [32, 387], F32)
            Gt = sb("a_Gt", [8, 3, 383], F32)
            Mb = sb("a_Mb", [128, 3, 8, 256], F32)
            b_c = Buf(); b_Gt = Buf(); b_Gd = Buf(); b_Mb = Buf()
            Gd = X["Gd"]
            S.dma("sync", relb[:, :], I["rel_bias"][:, :], writes=[b_c])
            S.dma("sync", E[:, :], I["c_onehot"][:, :], writes=[b_c])
            S.op("vector", lambda e: e.memset(Gt[:], NEG), writes=[b_Gt])
            ps, bp = self.getps()
            mm(ps[:8, 0:387], relb[:, :], E[:, :], [b_c], [bp])
            cp(Gt[:, :, 127:256], ps[:8, 0:387].rearrange("p (g j) -> p g j", g=3), [bp], [b_Gt])
            S.dma("sync", Gd[:, :, :], Gt[:, :, :], reads=[b_Gt], writes=[b_Gd])
            with contextlib.ExitStack() as es3:
                Mr = es3.enter_context(self.sbt("a_Mr", [128, 3, 8, 256], F32)); b_Mr = Buf()
                Jm = es3.enter_context(self.sbt("a_J", [128, 128], F32))
                S.dma("sync", Jm[:, :], I["c_antiident"][:, :], writes=[b_c])
                for g in range(3):
                    src = bass.AP(tensor=Gd.tensor, offset=Gd[0, g, 0].offset, ap=[[1, 128], [3 * 383, 8], [1, 256]])
                    S.dma("sync", Mr[:, g, :, :], src, reads=[b_Gd], writes=[b_Mr])
                for g in range(3):
                    for hp in range(4):
                        ps, bp = self.getps()
                        mm(ps[:, :], Jm[:, :], Mr[:, g, 2 * hp:2 * hp + 2, :].rearrange("p h k -> p (h k)"), [b_c, b_Mr], [bp])
                        cp(Mb[:, g, 2 * hp:2 * hp + 2, :].rearrange("p h k -> p (h k)"), ps[:, :], [bp], [b_Mb], eng=("scalar" if hp % 2 else "vector"))
                S.barrier()

            es2 = contextlib.ExitStack()
            sbp = lambda n, s_, d: es2.enter_context(self.sbt(n, s_, d))
            QT = sbp("a_QT", [128, 8, SEQ], BF16)
            KT = sbp("a_KT", [128, 8, SEQ], BF16)
            Vg = sbp("a_Vg", [128, 16, 1024], BF16)
            b_QT = Buf(); b_KT = Buf(); b_Vg = Buf()
            S.dma("sync", QT[:, :, :], X["qT"][:, 0:SEQ].rearrange("(h d) t -> d h t", d=128), writes=[b_QT])
            S.dma("sync", KT[:, :, :], X["kT"][:, 0:SEQ].rearrange("(h d) t -> d h t", d=128), writes=[b_KT])
            NR = 3
            sc = [sbp(f"a_sc{j}", [128, 2, 256], F32) for j in range(NR)]; b_sc = [Buf() for _ in range(NR)]
            Pm = [sbp(f"a_P{j}", [128, 2, 256], BF16) for j in range(NR)]; b_P = [Buf() for _ in range(NR)]
            PT = [sbp(f"a_PT{j}", [128, 2, 256], BF16) for j in range(NR)]; b_PT = [Buf() for _ in range(NR)]
            Ou = [sbp(f"a_Ou{j}", [128, 1024], F32) for j in range(2)]; b_Ou = [Buf(), Buf()]
            st = [sbp(f"a_st{j}", [128, 5, 8], F32) for j in range(2)]; b_st = [Buf(), Buf()]; b_sth = [[Buf() for _ in range(8)] for _ in range(2)]
            b_Os = Buf(); b_Ls = Buf()
            u = 0
            k = 0
            pend = []
            def flush_pend():
                while pend:
                    it_ = pend.pop(0)
                    if it_[0] is not None:
                        it_[0]()
                    it_[1]()
                    if it_[2] is not None:
                        it_[2]()
            do_prompt = self.stages.get("attn_prompt", True)
            for g, dil in enumerate(DILS):
                if not do_prompt:
                    break
                nb = 16 // dil
                flush_pend()
                vsrc = bass.AP(tensor=O["v_p"].tensor, offset=O["v_p"][l, 0, 0].offset,
                               ap=[[dil * 1024, 128], [1024, dil], [dil * 128 * 1024, nb], [1, 1024]])
                S.dma("gpsimd", Vg[:, :, :].rearrange("p (c n) f -> p c n f", c=dil), vsrc, writes=[b_Vg])
                for c in range(dil):
                    for n_ in range(nb):
                        blk = c * nb + n_
                        q0 = c + dil * 128 * n_
                        nk = 256 if n_ > 0 else 128
                        k0 = q0 - dil * 128 if n_ > 0 else q0
                        koff = 0 if n_ > 0 else 128
                        uj = u % 2
                        u += 1
                        for hp in range(AH // 2):
                            h = hp
                            h0 = 2 * hp
                            kj = k % NR
                            k += 1

                            def fA(uj=uj, kj=kj, hp=hp, h0=h0, g=g, q0=q0, k0=k0, nk=nk, koff=koff, dil=dil):
                                ps_s, bp_s = self.getps()
                                for i in range(2):
                                    mm(ps_s[:, i * 256:i * 256 + nk], self.strided(QT, h0 + i, q0, dil, 128), self.strided(KT, h0 + i, k0, dil, nk), [b_QT, b_KT], [bp_s],
                                       inc=(i == 1))
                                ps3 = ps_s[:, :].rearrange("p (i k) -> p i k", i=2)[:, :, :nk]
                                stt(sc[kj][:, :, :nk], ps3, SC, Mb[:, g, h0:h0 + 2, koff:koff + nk], ALU.mult, ALU.add, [bp_s, b_Mb], [b_sc[kj]])
                                S.op("vector", lambda e: e.reduce_max(out=st[uj][:, 0, h0:h0 + 2], in_=sc[kj][:, :, :nk], axis=AX.X),
                                     reads=[b_sc[kj]], writes=[b_sth[uj][hp]])
                                vs(st[uj][:, 1, h0:h0 + 2], st[uj][:, 0, h0:h0 + 2], -1.0, None, ALU.mult, None, [b_sth[uj][hp]], [b_sth[uj][hp]])
                                for i in range(2):
                                    act(Pm[kj][:, i, :nk], sc[kj][:, i, :nk], AF.Exp, [b_sc[kj], b_sth[uj][hp]], [b_P[kj], b_sth[uj][hp]],
                                        bias=st[uj][:, 1, h0 + i:h0 + i + 1], accum_out=st[uj][:, 2, h0 + i:h0 + i + 1])

                            def fB(kj=kj, nk=nk):
                                pst, bpt = self.getps()
                                pstb = pst[:, :].bitcast(BF16)
                                nkc = nk // 128
                                for i in range(2):
                                    for kc in range(nkc):
                                        tr(pstb[:, i * 256 + kc * 128:i * 256 + (kc + 1) * 128], Pm[kj][:, i, kc * 128:(kc + 1) * 128], self.identb[:, :], [b_P[kj]], [bpt],
                                           inc=(i == 1 and kc == nkc - 1))
                                cp(PT[kj][:, :, :nk], pstb[:, 0:512].rearrange("p (i k) -> p i k", i=2)[:, :, :nk], [bpt], [b_PT[kj]], eng="scalar")

                            def fC(uj=uj, kj=kj, hp=hp, h0=h0, blk=blk, nk=nk):
                                nkc = nk // 128
                                ps_o, bp_o = self.getps()
                                for i in range(2):
                                    for kc in range(nkc):
                                        bk = blk - (nkc - 1 - kc)
                                        mm(ps_o[:, i * 128:(i + 1) * 128], PT[kj][:, i, kc * 128:(kc + 1) * 128], Vg[:, bk, (h0 + i) * 128:(h0 + i + 1) * 128], [b_PT[kj], b_Vg], [bp_o],
                                           start=(kc == 0), stop=(kc == nkc - 1), inc=(i == 1 and kc == nkc - 1))
                                S.op("vector", lambda e: e.reciprocal(out=st[uj][:, 3, h0:h0 + 2], in_=st[uj][:, 2, h0:h0 + 2]), reads=[b_sth[uj][hp]], writes=[b_sth[uj][hp]])
                                vv(Ou[uj][:, h0 * 128:(h0 + 2) * 128].rearrange("p (i d) -> p i d", i=2), ps_o[:, 0:256].rearrange("p (i d) -> p i d", i=2),
                                   self.bc_last(st[uj][:, 3, h0:h0 + 2], 128), ALU.mult, [bp_o, b_sth[uj][hp]], [b_Ou[uj]])

                            fE = None
                            if hp == AH // 2 - 1:
                                def fE(uj=uj, g=g, q0=q0, dil=dil):
                                    act(st[uj][:, 4, :], st[uj][:, 2, :], AF.Ln, b_sth[uj], b_sth[uj])
                                    vv(st[uj][:, 4, :], st[uj][:, 4, :], st[uj][:, 0, :], ALU.add, b_sth[uj], b_sth[uj])
                                    dO = bass.AP(tensor=X["Os"].tensor, offset=X["Os"][g, q0, 0].offset, ap=[[dil * 1024, 128], [1, 1024]])
                                    S.dma("sync", dO, Ou[uj][:, :], reads=[b_Ou[uj]], writes=[b_Os])
                                    dL = bass.AP(tensor=X["Ls"].tensor, offset=X["Ls"][g, q0, 0].offset, ap=[[dil * 8, 128], [1, 8]])
                                    S.dma("sync", dL, st[uj][:, 4, :], reads=b_sth[uj], writes=[b_Ls])
                            fA()
                            pend.append([fB, fC, fE])
                            if len(pend) >= 2 and pend[-2][0] is not None:
                                pend[-2][0](); pend[-2][0] = None
                            if len(pend) >= 3:
                                it_ = pend.pop(0)
                                it_[1]()
                                if it_[2] is not None:
                                    it_[2]()
            flush_pend()
            Om = [sbp(f"a_Om{j}", [128, 3, 1024], F32) for j in range(2)]; b_Om = [Buf(), Buf()]
            Lm = [sbp(f"a_Lm{j}", [128, 6, 8], F32) for j in range(2)]; b_Lm = [Buf(), Buf()]
            wg = [sbp(f"a_wg{j}", [128, 3, 8], F32) for j in range(2)]; b_wg = [Buf(), Buf()]
            ym = [sbp(f"a_ym{j}", [128, 1024], F32) for j in range(2)]; b_ym = [Buf(), Buf()]
            ymb = [sbp(f"a_ymb{j}", [128, 1024], BF16) for j in range(2)]; b_ymb = [Buf(), Buf()]
            yT_ = [sbp(f"a_yT{j}", [128, 8, 128], BF16) for j in range(2)]; b_yT = [Buf(), Buf()]
            for ti in range(SEQ // 128 if do_prompt else 0):
                j = ti % 2
                t0 = ti * 128
                for g in range(3):
                    S.dma("sync", Om[j][:, g, :], X["Os"][g, t0:t0 + 128, :], reads=[b_Os], writes=[b_Om[j]])
                    S.dma("sync", Lm[j][:, g, :], X["Ls"][g, t0:t0 + 128, :], reads=[b_Ls], writes=[b_Lm[j]])
                L = Lm[j]
                vv(L[:, 3, :], L[:, 0, :], L[:, 1, :], ALU.max, [b_Lm[j]], [b_Lm[j]])
                vv(L[:, 3, :], L[:, 3, :], L[:, 2, :], ALU.max, [b_Lm[j]], [b_Lm[j]])
                for g in range(3):
                    vv(wg[j][:, g, :], L[:, g, :], L[:, 3, :], ALU.subtract, [b_Lm[j]], [b_wg[j]])
                act(wg[j][:, :, :], wg[j][:, :, :], AF.Exp, [b_wg[j]], [b_wg[j]])
                vv(L[:, 4, :], wg[j][:, 0, :], wg[j][:, 1, :], ALU.add, [b_wg[j]], [b_Lm[j]])
                vv(L[:, 4, :], L[:, 4, :], wg[j][:, 2, :], ALU.add, [b_wg[j], b_Lm[j]], [b_Lm[j]])
                S.op("vector", lambda e: e.reciprocal(out=L[:, 5, :], in_=L[:, 4, :]), reads=[b_Lm[j]], writes=[b_Lm[j]])
                for g in range(3):
                    vv(wg[j][:, g, :], wg[j][:, g, :], L[:, 5, :], ALU.mult, [b_wg[j], b_Lm[j]], [b_wg[j]])
                v3 = lambda t: t.rearrange("p (h d) -> p h d", h=8)
                vv(v3(ym[j][:, :]), v3(Om[j][:, 0, :]), self.bc_last(wg[j][:, 0, :], 128), ALU.mult, [b_Om[j], b_wg[j]], [b_ym[j]])
                vv(v3(Om[j][:, 1, :]), v3(Om[j][:, 1, :]), self.bc_last(wg[j][:, 1, :], 128), ALU.mult, [b_Om[j], b_wg[j]], [b_Om[j]], eng="gpsimd")
                vv(v3(Om[j][:, 2, :]), v3(Om[j][:, 2, :]), self.bc_last(wg[j][:, 2, :], 128), ALU.mult, [b_Om[j], b_wg[j]], [b_Om[j]], eng="gpsimd")
                vv(ym[j][:, :], ym[j][:, :], Om[j][:, 1, :], ALU.add, [b_ym[j], b_Om[j]], [b_ym[j]])
                vv(ymb[j][:, :], ym[j][:, :], Om[j][:, 2, :], ALU.add, [b_ym[j], b_Om[j]], [b_ymb[j]])
                pst, bpt = self.getps()
                pstb = pst[:, :].bitcast(BF16)
                for h in range(8):
                    tr(pstb[:, h * 128:(h + 1) * 128], ymb[j][:, h * 128:(h + 1) * 128], self.identb[:, :], [b_ymb[j]], [bpt], inc=(h == 7))
                cp(yT_[j][:, :, :], pstb[:, :].rearrange("p (h t) -> p h t", h=8), [bpt], [b_yT[j]], eng="scalar")
                S.dma("sync", X["yT"][2 * RW:D, t0:t0 + 128].rearrange("(q p) t -> p q t", p=128), yT_[j][:, :, :], reads=[b_yT[j]])

            S.barrier()
            es2.close()
            if self.stages.get("attn_sample", True):
                Kc = sb("s_Kc", [128, 9, 1024], BF16); Vc = sb("s_Vc", [128, 9, 1024], BF16)
                KTs = sb("s_KTs", [128, 9, 8, 128], BF16)
                QsT = sb("s_QsT", [128, 8, 4], BF16); KnT = sb("s_KnT", [128, 8, 4], BF16)
                Vn = sb("s_Vn", [4, 1024], BF16)
                Qm = sb("s_Qm", [128, 8, 4, 4], BF16)
                Sa = sb("s_Sa", [4, 3, 8, 132], F32); Ms = sb("s_Ms", [4, 3, 8, 132], F32)
                Pw = sb("s_Pw", [4, 3, 8, 132], BF16)
                d01 = sb("s_d01", [4, 4], F32); dneg = sb("s_dneg", [4, 4], F32); cmask = sb("s_cmask", [128, 4, 4], BF16)
                sst = sb("s_st", [4, 8, 24], F32)
                PTs = sb("s_PT", [128, 24, 4], BF16); PTm = sb("s_PTm", [128, 4, 24, 4], BF16)
                Pn = sb("s_Pn", [4, 8, 4], F32); Pnb = sb("s_Pnb", [4, 8, 4], BF16); PnT = sb("s_PnT", [4, 8, 4], BF16)
                ysb = sb("s_ysb", [4, 1024], BF16); ysT = sb("s_ysT", [128, 8, 4], BF16)
                b_Kc = Buf(); b_Vc = Buf(); b_KTs = Buf(); b_q = Buf(); b_Vn = Buf(); b_Qm = Buf(); b_Sa = Buf(); b_Ms = Buf()
                b_Pw = Buf(); b_k = Buf(); b_sst = Buf(); b_PTs = Buf(); b_PTm = Buf(); b_Pn = Buf(); b_PnT = Buf(); b_ysb = Buf(); b_ysT = Buf()
                rows = [(1920, 1)] + [(1536 + t, 4) for t in range(4)] + [(t, 16) for t in range(4)]
                for i, (r0, stp) in enumerate(rows):
                    for (dst, src_t, bb) in ((Kc, I["cache_k"], b_Kc), (Vc, I["cache_v"], b_Vc)):
                        src = bass.AP(tensor=src_t.tensor, offset=src_t[l, r0, 0].offset, ap=[[stp * 1024, 128], [1, 1024]])
                        S.dma("gpsimd", dst[:, i, :], src, writes=[bb])
                S.dma("sync", QsT[:, :, :], X["qT"][:, SEQ:SEQ + NS].rearrange("(h d) t -> d h t", d=128), writes=[b_q], **nck)
                S.dma("sync", KnT[:, :, :], X["kT"][:, SEQ:SEQ + NS].rearrange("(h d) t -> d h t", d=128), writes=[b_q], **nck)
                S.dma("gpsimd", Vn[:, :], O["v_s"][l, :, :], writes=[b_Vn])
                S.dma("sync", d01[:, :], I["c_d01"][:, :], writes=[b_k])
                S.dma("sync", dneg[:, :], I["c_dneg"][:, :], writes=[b_k])
                S.dma("gpsimd", cmask[:, :, :], I["c_colmask"][:, :, :], writes=[b_k])
                for i in range(9):
                    pst, bpt = self.getps()
                    pstb = pst[:, :].bitcast(BF16)
                    for h in range(8):
                        tr(pstb[:, h * 128:(h + 1) * 128], Kc[:, i, h * 128:(h + 1) * 128], self.identb[:, :], [b_Kc], [bpt], inc=(h == 7))
                    cp(KTs[:, i, :, :], pstb[:, :].rearrange("p (h t) -> p h t", h=8), [bpt], [b_KTs], eng=("scalar" if i % 2 else "vector"))
                S.op("vector", lambda e: e.memset(Qm[:], 0.0), writes=[b_Qm])
                for t in range(4):
                    cp(Qm[:, :, t, t], QsT[:, :, t], [b_q], [b_Qm])
                for t in range(4):
                    S.dma("sync", Ms[t:t + 1, 0, :, :], bass.AP(tensor=Gd.tensor, offset=Gd[0, 0, 127 - t].offset, ap=[[0, 1], [3 * 383, 8], [1, 132]]),
                          reads=[b_Gd], writes=[b_Ms])
                for g in (1, 2):
                    S.dma("sync", Ms[:, g, :, 0:128], bass.AP(tensor=Gd.tensor, offset=Gd[0, g, 127].offset, ap=[[0, 4], [3 * 383, 8], [1, 128]]),
                          reads=[b_Gd], writes=[b_Ms])
                    for tt in range(4):
                        S.dma("sync", Ms[:, g, :, 128 + tt], bass.AP(tensor=Gd.tensor, offset=Gd[0, g, 255].offset, ap=[[0, 4], [3 * 383, 8]]),
                              reads=[b_Gd], writes=[b_Ms], **nck)
                    blkv = Ms[:, g, :, 128:132]
                    d01b = bass.AP(tensor=d01[:, :].tensor, offset=d01[:, :].offset, ap=[list(d01[:, :].ap[0]), [0, 8], [1, 4]])
                    dngb = bass.AP(tensor=dneg[:, :].tensor, offset=dneg[:, :].offset, ap=[list(dneg[:, :].ap[0]), [0, 8], [1, 4]])
                    vv(blkv, blkv, d01b, ALU.mult, [b_Ms, b_k], [b_Ms])
                    vv(blkv, blkv, dngb, ALU.add, [b_Ms, b_k], [b_Ms])
                for g in range(3):
                    for h in range(8):
                        ps, bp = self.getps()
                        if g == 0:
                            mm(ps[:4, 0:128], QsT[:, h, :], KTs[:, 0, h, :], [b_q, b_KTs], [bp])
                        else:
                            for t in range(4):
                                mm(ps[:4, 0:128], Qm[:, h, t, :], KTs[:, 1 + 4 * (g - 1) + t, h, :], [b_Qm, b_KTs], [bp], start=(t == 0), stop=(t == 3))
                        mm(ps[:4, 128:132], QsT[:, h, :], KnT[:, h, :], [b_q], [bp])
                        stt(Sa[:, g, h, :], ps[:4, 0:132], SC, Ms[:, g, h, :], ALU.mult, ALU.add, [bp, b_Ms], [b_Sa])
                S3 = Sa[:, :, :, :].rearrange("p g h k -> p (g h) k")
                mx, den, lse, rden = sst[:, 0, :], sst[:, 1, :], sst[:, 2, :], sst[:, 3, :]
                S.op("vector", lambda e: e.reduce_max(out=mx, in_=S3, axis=AX.X), reads=[b_Sa], writes=[b_sst])
                vv(S3, S3, self.bc_last(mx, 132), ALU.subtract, [b_Sa, b_sst], [b_Sa])
                act(S3, S3, AF.Exp, [b_Sa], [b_Sa])
                S.op("vector", lambda e: e.reduce_sum(out=den, in_=S3, axis=AX.X), reads=[b_Sa], writes=[b_sst])
                act(lse, den, AF.Ln, [b_sst], [b_sst])
                vv(lse, lse, mx, ALU.add, [b_sst], [b_sst])
                mg, sg = sst[:, 4, 0:8], sst[:, 4, 8:16]
                eg = sst[:, 5, :]
                vv(mg, sst[:, 2, 0:8], sst[:, 2, 8:16], ALU.max, [b_sst], [b_sst])
                vv(mg, mg, sst[:, 2, 16:24], ALU.max, [b_sst], [b_sst])
                for g in range(3):
                    vv(sst[:, 5, g * 8:(g + 1) * 8], sst[:, 2, g * 8:(g + 1) * 8], mg, ALU.subtract, [b_sst], [b_sst])
                act(eg, eg, AF.Exp, [b_sst], [b_sst])
                vv(sg, sst[:, 5, 0:8], sst[:, 5, 8:16], ALU.add, [b_sst], [b_sst])
                vv(sg, sg, sst[:, 5, 16:24], ALU.add, [b_sst], [b_sst])
                S.op("vector", lambda e: e.reciprocal(out=sst[:, 6, 0:8], in_=sg), reads=[b_sst], writes=[b_sst])
                S.op("vector", lambda e: e.reciprocal(out=rden, in_=den), reads=[b_sst], writes=[b_sst])
                for g in range(3):
                    vv(sst[:, 5, g * 8:(g + 1) * 8], sst[:, 5, g * 8:(g + 1) * 8], sst[:, 6, 0:8], ALU.mult, [b_sst], [b_sst])
                vv(eg, eg, rden, ALU.mult, [b_sst], [b_sst])
                vv(Pw[:, :, :, :].rearrange("p g h k -> p (g h) k"), S3, self.bc_last(eg, 132), ALU.mult, [b_Sa, b_sst], [b_Pw])
                pst, bpt = self.getps()
                pstb = pst[:, :].bitcast(BF16)
                for gh in range(24):
                    tr(pstb[:, gh * 4:(gh + 1) * 4], Pw[:, gh // 8, gh % 8, 0:128], self.identb[:4, :4], [b_Pw], [bpt], inc=(gh == 23))
                cp(PTs[:, :, :], pstb[:, 0:96].rearrange("p (a t) -> p a t", t=4), [bpt], [b_PTs])
                for t in range(4):
                    cmb = bass.AP(tensor=cmask[:, t, :].tensor, offset=cmask[:, t, :].offset, ap=[list(cmask[:, t, :].ap[0]), [0, 24], [1, 4]])
                    vv(PTm[:, t, :, :], PTs[:, :, :], cmb, ALU.mult, [b_PTs, b_k], [b_PTm])
                vv(Pn[:, :, :], Pw[:, 0, :, 128:132], Pw[:, 1, :, 128:132], ALU.add, [b_Pw], [b_Pn])
                vv(Pnb[:, :, :], Pn[:, :, :], Pw[:, 2, :, 128:132], ALU.add, [b_Pw, b_Pn], [b_Pn])
                pst2, bpt2 = self.getps()
                pstb2 = pst2[:, :].bitcast(BF16)
                for h in range(8):
                    tr(pstb2[:4, h * 4:(h + 1) * 4], Pnb[:, h, :], self.identb[:4, :4], [b_Pn], [bpt2], inc=(h == 7))
                cp(PnT[:, :, :], pstb2[:4, 0:32].rearrange("p (h t) -> p h t", h=8), [bpt2], [b_PnT])
                for h in range(8):
                    ps, bp = self.getps()
                    hs = slice(h * 128, (h + 1) * 128)
                    mm(ps[:4, 0:128], PTs[:, h, :], Vc[:, 0, hs], [b_PTs, b_Vc], [bp], start=True, stop=False, inc=False)
                    for g in (1, 2):
                        for t in range(4):
                            mm(ps[:4, 0:128], PTm[:, t, g * 8 + h, :], Vc[:, 1 + 4 * (g - 1) + t, hs], [b_PTm, b_Vc], [bp], start=False, stop=False, inc=False)
                    mm(ps[:4, 0:128], PnT[:, h, :], Vn[:, hs], [b_PnT, b_Vn], [bp], start=False, stop=True)
                    cp(ysb[:, hs], ps[:4, 0:128], [bp], [b_ysb], eng=("scalar" if h % 2 else "vector"))
                pst3, bpt3 = self.getps()
                pstb3 = pst3[:, :].bitcast(BF16)
                for h in range(8):
                    tr(pstb3[:, h * 4:(h + 1) * 4], ysb[:, h * 128:(h + 1) * 128], self.identb[:4, :4], [b_ysb], [bpt3], inc=(h == 7))
                cp(ysT[:, :, :], pstb3[:, 0:32].rearrange("p (h t) -> p h t", h=8), [bpt3], [b_ysT])
                S.dma("sync", X["yT"][2 * RW:D, SEQ:SEQ + NS].rearrange("(q p) t -> p q t", p=128), ysT[:, :, :], reads=[b_ysT], **nck)
            S.barrier()

    def rstd_from_mean(self, st, n, R, W):
        self.vs(st[:n, 1:2], st[:n, 0:1], 1e-6, None, ALU.add, None, R, W)
        self.act(st[:n, 0:1], st[:n, 1:2], AF.Sqrt, W, W)
        self.S.op("vector", lambda e: e.reciprocal(out=st[:n, 1:2], in_=st[:n, 0:1]), reads=W, writes=W)

    def stage_wout(self, l, first):
        nc, S = self.nc, self.S
        I, O, X = self.I, self.O, self.X
        vv, vs, stt, act, cp, mm, tr = self.vv, self.vs, self.stt, self.act, self.cp, self.mm, self.tr
        with contextlib.ExitStack() as es:
            sb = lambda n, s, d: es.enter_context(self.sbt(n, s, d))
            self.ps_pool = [es.enter_context(self.pst(f"o_ps{j}", [128, 512], F32)) for j in range(8)]
            self.ps_bufs = [PBuf() for _ in range(8)]
            self.ps_next = 0
            wo = sb("o_wo", [128, 16, D], BF16); b_wo = Buf()
            g1 = sb("o_g1", [128, D], F32); g2 = sb("o_g2", [128, D], F32); b_g = Buf()
            yTt = [sb(f"o_yT{j}", [128, 16, 128], BF16) for j in range(2)]; b_yTt = [Buf(), Buf()]
            xt = [sb(f"o_xt{j}", [128, D], F32) for j in range(2)]; b_xt = [Buf(), Buf()]
            x1 = [sb(f"o_x1{j}", [128, D], F32) for j in range(2)]; b_x1 = [Buf(), Buf()]
            hb = [sb(f"o_hb{j}", [128, D], BF16) for j in range(2)]; b_hb = [Buf(), Buf()]
            hbT = [sb(f"o_hbT{j}", [128, 16, 128], BF16) for j in range(2)]; b_hbT = [Buf(), Buf()]
            junk = sb("o_junk", [128, D], BF16); b_junk = Buf()
            st = [sb(f"o_st{j}", [128, 8], F32) for j in range(2)]; b_st = [Buf(), Buf()]
            for q in range(4):
                S.dma("gpsimd", wo[:, q * 4:(q + 1) * 4, :], I["w_out"][l, q * 512:(q + 1) * 512, :].rearrange("(c p) n -> p c n", p=128), writes=[b_wo])
            S.dma("sync", g1[:, :], dram_bcast(I["norm_mix_post"][l:l + 1, :], 128, D), writes=[b_g])
            S.dma("sync", g2[:, :], dram_bcast(I["norm_ffn_pre"][l:l + 1, :], 128, D), writes=[b_g])
            for ti, (t0, n) in enumerate(tok_tiles()):
                j = ti % 2
                S.dma("sync", yTt[j][:, :, :n], X["yT"][:, t0:t0 + n].rearrange("(c p) t -> p c t", p=128), writes=[b_yTt[j]])
                if first:
                    src = I["xp"][t0:t0 + n, :] if t0 < SEQ else I["xs"][:, :]
                else:
                    src = X["xres"][t0:t0 + n, :]
                S.dma("sync", xt[j][:n, :], src, writes=[b_xt[j]])
                pss = []
                for db in range(4):
                    ps, bp = self.getps()
                    pss.append((ps, bp))
                    for c in range(16):
                        mm(ps[:n, :], yTt[j][:, c, :n], wo[:, c, db * 512:(db + 1) * 512], [b_yTt[j], b_wo], [bp], start=(c == 0), stop=(c == 15))
                    act(junk[:n, 0:512], ps[:n, :], AF.Square, [bp], [b_junk, b_st[j]], scale=float(D ** -0.5), accum_out=st[j][:n, 2 + db:3 + db])
                S.op("vector", lambda e: e.reduce_sum(out=st[j][:n, 0:1], in_=st[j][:n, 2:6], axis=AX.X), reads=[b_st[j]], writes=[b_st[j]])
                self.rstd_from_mean(st[j], n, [b_st[j]], [b_st[j]])
                for db in range(4):
                    ps, bp = pss[db]
                    sl = slice(db * 512, (db + 1) * 512)
                    stt(x1[j][:n, sl], ps[:n, :], st[j][:n, 1:2], g1[:n, sl], ALU.mult, ALU.mult, [bp, b_st[j], b_g], [b_x1[j]])
                vv(x1[j][:n, :], x1[j][:n, :], xt[j][:n, :], ALU.add, [b_x1[j], b_xt[j]], [b_x1[j]], eng="gpsimd")
                S.dma("sync", X["x1s"][t0:t0 + n, :], x1[j][:n, :], reads=[b_x1[j]])
                act(junk[:n, :], x1[j][:n, :], AF.Square, [b_x1[j]], [b_junk, b_st[j]], scale=float(D ** -0.5), accum_out=st[j][:n, 0:1])
                self.rstd_from_mean(st[j], n, [b_st[j]], [b_st[j]])
                stt(hb[j][:n, :], x1[j][:n, :], st[j][:n, 1:2], g2[:n, :], ALU.mult, ALU.mult, [b_x1[j], b_st[j], b_g], [b_hb[j]])
                for half in range(2):
                    pst, bpt = self.getps()
                    pstb = pst[:, :].bitcast(BF16)
                    for c in range(8):
                        cc = half * 8 + c
                        tr(pstb[:, c * 128:c * 128 + n], hb[j][:n, cc * 128:(cc + 1) * 128], self.identb[:n, :n], [b_hb[j]], [bpt], inc=(c == 7))
                    cp(hbT[j][:, half * 8:(half + 1) * 8, :n], pstb[:, :].rearrange("p (c t) -> p c t", c=8)[:, :, :n], [bpt], [b_hbT[j]],
                       eng=("scalar" if half == 0 else "vector"))
                S.dma("sync", X["hfT"][:, t0:t0 + n].rearrange("(c p) t -> p c t", p=128), hbT[j][:, :, :n], reads=[b_hbT[j]])
            S.barrier()

    def stage_ffn(self, l, last):
        nc, S = self.nc, self.S
        I, O, X = self.I, self.O, self.X
        vv, vs, stt, act, cp, mm, tr = self.vv, self.vs, self.stt, self.act, self.cp, self.mm, self.tr
        with contextlib.ExitStack() as es:
            sb = lambda n, s, d: es.enter_context(self.sbt(n, s, d))
            self.ps_pool = [es.enter_context(self.pst(f"f_ps{j}", [128, 512], F32)) for j in range(8)]
            self.ps_bufs = [PBuf() for _ in range(8)]
            self.ps_next = 0
            NSB = 1028
            hfs = sb("f_hfs", [128, 16, NSB], BF16); b_hfs = Buf()
            facc = sb("f_facc", [128, 9, D], F32); b_facc = [Buf() for _ in range(9)]
            w1 = [sb(f"f_w1{j}", [128, 16, 512], BF16) for j in range(2)]; b_w1 = [Buf(), Buf()]
            w2 = [sb(f"f_w2{j}", [128, 4, D], BF16) for j in range(2)]; b_w2 = [Buf(), Buf()]
            aT = [sb(f"f_aT{j}", [128, 4, NSB], BF16) for j in range(2)]; b_aT = [Buf(), Buf()]
            rl = [sb(f"f_rl{j}", [128, 512], F32) for j in range(2)]; b_rl = [Buf() for _ in range(2)]
            g3 = sb("f_g3", [128, D], F32); b_g = Buf()
            x1_ = sb("f_x1", [128, D], F32); x1 = [x1_, x1_]; _bx = Buf(); b_x1 = [_bx, _bx]
            junk = sb("f_junk", [128, 512], BF16); b_junk = Buf()
            st = [sb(f"f_st{j}", [128, 8], F32) for j in range(2)]; b_st = [Buf(), Buf()]
            S.dma("sync", g3[:, :], dram_bcast(I["norm_ffn_post"][l:l + 1, :], 128, D), writes=[b_g])
            sbs = [(0, 1024), (1024, TT - 1024)]
            nfb = self.stages.get("ffn_blocks", 16)
            wcnt = 0
            rcnt = 0

            def load_w(fb, j):
                if self.stages.get('dbg_noload') and fb >= 2:
                    return
                src1 = I["ffn_w1"][l, :, fb * 512:(fb + 1) * 512].rearrange("(c p) n -> p c n", p=128)
                S.dma("gpsimd", w1[j][:, 0:8, :], src1[:, 0:8, :], writes=[b_w1[j]])
                S.dma("gpsimd", w1[j][:, 8:16, :], src1[:, 8:16, :], writes=[b_w1[j]])
                src2 = I["ffn_w2"][l, fb * 512:(fb + 1) * 512, :].rearrange("(c p) n -> p c n", p=128)
                S.dma("gpsimd", w2[j][:, :, :], src2, writes=[b_w2[j]])


            def emit_w1(fb, j, tblocks):
                nonlocal rcnt
                for fc in range(4):
                    pss = []
                    for (b0, nb_) in tblocks:
                        pss.append(self.getps())
                    for c in range(16):
                        for bi_, (b0, nb_) in enumerate(tblocks):
                            ps, bp = pss[bi_]
                            mm(ps[:, :nb_], w1[j][:, c, fc * 128:(fc + 1) * 128], hfs[:, c, b0:b0 + nb_], [b_w1[j], b_hfs], [bp], start=(c == 0), stop=(c == 15))
                    for bi_, (b0, nb_) in enumerate(tblocks):
                        ps, bp = pss[bi_]
                        rj = rcnt % 2
                        rcnt += 1
                        act(rl[rj][:, :nb_], ps[:, :nb_], AF.Relu, [bp], [b_rl[rj]])
                        act(aT[j][:, fc, b0:b0 + nb_], rl[rj][:, :nb_], AF.Square, [b_rl[rj]], [b_aT[j]])

            def emit_w2(fb, j, tiles):
                for ti, (c0, n) in enumerate(tiles):
                    for db in range(4):
                        ps, bp = self.getps()
                        for fc in range(4):
                            mm(ps[:n, :], aT[j][:, fc, c0:c0 + n], w2[j][:, fc, db * 512:(db + 1) * 512], [b_aT[j], b_w2[j]], [bp], start=(fc == 0), stop=(fc == 3))
                        sl = slice(db * 512, (db + 1) * 512)
                        if fb == 0:
                            cp(facc[:n, ti, sl], ps[:n, :], [bp], [b_facc[ti]], eng="scalar")
                        else:
                            vv(facc[:n, ti, sl], facc[:n, ti, sl], ps[:n, :], ALU.add, [bp, b_facc[ti]], [b_facc[ti]])

            for (T0, NT) in sbs:
                S.dma("sync", hfs[:, :, :NT], X["hfT"][:, T0:T0 + NT].rearrange("(c p) t -> p c t", p=128), writes=[b_hfs])
                tiles = [(i * 128, min(128, NT - i * 128)) for i in range((NT + 127) // 128)]
                tblocks = [(i * 512, min(512, NT - i * 512)) for i in range((NT + 511) // 512)]
                load_w(0, wcnt % 2)
                jj = wcnt % 2
                wcnt += 1
                if nfb > 1:
                    load_w(1, wcnt % 2)
                emit_w1(0, jj, tblocks)
                for fb in range(nfb):
                    jn = wcnt % 2
                    if fb + 1 < nfb:
                        wcnt += 1
                        emit_w1(fb + 1, jn, tblocks)
                    emit_w2(fb, jj, tiles)
                    if fb + 2 < nfb:
                        load_w(fb + 2, jj)
                    jj = jn
                for ti, (c0, n) in enumerate(tiles):
                    j = ti % 2
                    t0 = T0 + c0
                    S.dma("sync", x1[j][:n, :], X["x1s"][t0:t0 + n, :], writes=[b_x1[j]])
                    for db in range(4):
                        act(junk[:n, :], facc[:n, ti, db * 512:(db + 1) * 512], AF.Square, [b_facc[ti]], [b_junk, b_st[j]], scale=float(D ** -0.5),
                            accum_out=st[j][:n, 2 + db:3 + db])
                    S.op("vector", lambda e: e.reduce_sum(out=st[j][:n, 0:1], in_=st[j][:n, 2:6], axis=AX.X), reads=[b_st[j]], writes=[b_st[j]])
                    self.rstd_from_mean(st[j], n, [b_st[j]], [b_st[j]])
                    stt(facc[:n, ti, :], facc[:n, ti, :], st[j][:n, 1:2], g3[:n, :], ALU.mult, ALU.mult, [b_facc[ti], b_st[j], b_g], [b_facc[ti]])
                    vv(x1[j][:n, :], x1[j][:n, :], facc[:n, ti, :], ALU.add, [b_x1[j], b_facc[ti]], [b_x1[j]])
                    if last:
                        dst = O["yp"][t0:t0 + n, :] if t0 < SEQ else O["ys"][:, :]
                    else:
                        dst = X["xres"][t0:t0 + n, :]
                    S.dma("sync", dst, x1[j][:n, :], reads=[b_x1[j]])
            S.barrier()

    def build(self):
        nc, S = self.nc, self.S
        self.declare()
        self.load_consts()
        for l in range(DEPTH):
            if l >= self.stages.get("layers", DEPTH):
                break
            with contextlib.ExitStack() as es:
                hT = es.enter_context(self.sbt("hT", [128, 16, TT], BF16))
                b_hT = Buf("hT")
                self.stage_norm_T(l, l == 0, hT, b_hT, "norm_mix_pre")
                self.stage_win(l, hT, b_hT)
                S.barrier()
            if self.stages.get('lru', True):
                self.stage_lru(l)
            if self.stages.get('rwkv', True):
                self.stage_rwkv(l)
            if self.stages.get('attn', True):
                self.stage_attn(l)
            if self.stages.get('ffn', True):
                self.stage_wout(l, l == 0)
                self.stage_ffn(l, l == DEPTH - 1)
        S.finish()
        self.es.close()
        return nc


def _t5_onehot():
    E = np.zeros((32, 3 * 129), np.float32)
    for g, dil in enumerate((1, 4, 16)):
        for j in range(129):
            dist = np.int32((128 - j) * dil)
            d = np.float32(max(int(dist), 1))
            large = 16 + int(np.float32(np.log(d / np.float32(16.0))) / np.float32(np.log(2048.0 / 16.0)) * np.float32(16.0))
            b = int(dist) if dist < 16 else min(large, 31)
            E[b, g * 129 + j] = 1.0
    return E


PROMPT_CORES = (0, 1, 4, 5)


def _per_core_inputs(inp, c):
    m = {}
    if c in PROMPT_CORES:
        m["xp"] = np.ascontiguousarray(inp["x_prompt"][PROMPT_CORES.index(c)])
    else:
        m["xp"] = np.zeros((SEQ, D), np.float32)
    m["xs"] = np.ascontiguousarray(inp["x_sample"][c])
    m["st_wkv"] = np.ascontiguousarray(inp["state_rwkv_wkv"][:, c])
    m["st_shift"] = np.ascontiguousarray(inp["state_rwkv_shift"][:, c])
    m["st_lruh"] = np.ascontiguousarray(inp["state_lru_h"][:, c])
    m["st_conv"] = np.ascontiguousarray(inp["state_lru_conv"][:, c])
    m["cache_k"] = np.ascontiguousarray(inp["cache_attn_k"][:, c]).reshape(DEPTH, SEQ, AW)
    m["cache_v"] = np.ascontiguousarray(inp["cache_attn_v"][:, c]).reshape(DEPTH, SEQ, AW)
    for n in ("rel_bias", "norm_mix_pre", "norm_mix_post", "norm_ffn_pre", "norm_ffn_post", "w_in", "w_out",
              "rwkv_mu", "rwkv_w0", "rwkv_w_up", "rwkv_a0", "rwkv_a_up", "rwkv_g_up", "rwkv_k_k", "rwkv_k_a",
              "rwkv_lnx_g", "rwkv_lnx_b", "lru_conv_w", "lru_conv_b", "lru_wa", "lru_ba", "lru_wx", "lru_bx",
              "lru_lambda", "ffn_w1", "ffn_w2"):
        m[n] = inp[n]
    m["rwkv_r_k"] = inp["rwkv_r_k"].reshape(DEPTH, RW)
    m["c_ident"] = np.eye(128, dtype=np.float32)
    m["c_tri"] = np.triu(np.ones((128, 128), np.float32))
    m["c_stri"] = np.triu(np.ones((128, 128), np.float32), 1)
    m["c_ltri"] = np.tril(np.ones((128, 128), np.float32), -1)
    m["c_onehot"] = _t5_onehot()
    m["c_d01"] = np.eye(4, dtype=np.float32)
    m["c_dneg"] = ((1.0 - np.eye(4)) * NEG).astype(np.float32)
    cm = np.zeros((128, 4, 4), np.float32)
    for t in range(4):
        cm[:, t, t] = 1.0
    m["c_colmask"] = cm
    m["c_antiident"] = np.ascontiguousarray(np.eye(128, dtype=np.float32)[::-1])
    return m


def kernel(stages=None, **inp):
    inp = {k: np.asarray(v) for k, v in inp.items()}
    prog = Prog(stages or {})
    nc = prog.build()
    ncores = prog.stages.get("ncores", 8)
    in_maps = [_per_core_inputs(inp, c) for c in range(ncores)]
    res = run_bass_kernel_spmd(nc, in_maps, core_ids=list(range(ncores)))
    R = list(res.results)
    while len(R) < 8:
        R.append(R[0])
    B = 4
    PC = PROMPT_CORES

    def stackp(name, shape_tail):
        return np.stack([R[b][name] for b in range(B)], axis=0)

    def stacks(name):
        return np.stack([R[c][name] for c in range(8)], axis=0)

    y_p = np.stack([R[PC[b]]["yp"] for b in range(B)], 0)
    y_s = stacks("ys")
    wkv_p = np.stack([R[PC[b]]["wkv_p"] for b in range(B)], 1)
    wkv_s = np.stack([R[c]["wkv_s"] for c in range(8)], 1)
    shift_p = np.stack([R[PC[b]]["shift_p"] for b in range(B)], 1)
    shift_s = np.stack([R[c]["shift_s"] for c in range(8)], 1)
    lruh_p = np.stack([R[PC[b]]["lruh_p"] for b in range(B)], 1)
    lruh_s = np.stack([R[c]["lruh_s"] for c in range(8)], 1)
    conv_p = np.stack([R[PC[b]]["conv_p"] for b in range(B)], 1)
    conv_s = np.stack([R[c]["conv_s"] for c in range(8)], 1)
    k_p = np.stack([R[PC[b]]["k_p"] for b in range(B)], 1).reshape(DEPTH, B, SEQ, AH, AD)
    k_s = np.stack([R[c]["k_s"] for c in range(8)], 1).reshape(DEPTH, 8, NS, AH, AD)
    v_p = np.stack([R[PC[b]]["v_p"] for b in range(B)], 1).reshape(DEPTH, B, SEQ, AH, AD)
    v_s = np.stack([R[c]["v_s"] for c in range(8)], 1).reshape(DEPTH, 8, NS, AH, AD)
    outs = (y_p, y_s, wkv_p, wkv_s, shift_p, shift_s, lruh_p, lruh_s, conv_p, conv_s, k_p, k_s, v_p, v_s)
    return tuple(np.ascontiguousarray(o, dtype=np.float32) for o in outs)
```

```python
import contextlib
import numpy as np
import concourse.bass as bass
import concourse.mybir as mybir
from concourse.bass_utils import run_bass_kernel_spmd

F32 = mybir.dt.float32
BF16 = mybir.dt.bfloat16
F32R = mybir.dt.float32r
AF = mybir.ActivationFunctionType
ALU = mybir.AluOpType
AX = mybir.AxisListType

D = 2048
SEQ = 2048
DEPTH = 2
NS = 4
TT = SEQ + NS
RW = 512
RH = 8
RN = 64
RPROJ = 1792
LW = 512
AW = 1024
AH = 8
AD = 128
N_IN = 5888
DFF = 8192
NEG = -1e30
C_P, C_LX, C_LG, C_Q, C_K, C_V = 0, 1792, 2304, 2816, 3840, 4864
NFM = 4864


class Buf:
    __slots__ = ("name", "w", "r", "x")

    def __init__(self, name="", x=False):
        self.name = name
        self.w = None
        self.r = {}
        self.x = x


def PBuf():
    return Buf("psum", True)


class Tok:
    __slots__ = ("eng", "sem", "val")

    def __init__(self, eng, sem, val):
        self.eng, self.sem, self.val = eng, sem, val


class Eng:
    def __init__(self, name, eng, sems):
        self.name, self.eng, self.sems = name, eng, sems
        self.si = 0
        self.count = 0
        self.waited = {}


class Sched:
    EPOCH = 30000

    def __init__(self, nc, es):
        self.nc = nc
        self.engs = {}
        for name in ("tensor", "vector", "scalar", "gpsimd", "sync"):
            sems = [es.enter_context(nc.semaphore(f"p_{name}_{i}")) for i in range(3)]
            self.engs[name] = Eng(name, getattr(nc, name), sems)
        self.dsem = {}
        self.dval = {}
        self.dnext = {}
        for q, cnt in (("sync", 24), ("gpsimd", 12), ("scalar", 4)):
            self.dsem[q] = [es.enter_context(nc.semaphore(f"dq_{q}_{i}")) for i in range(cnt)]
            self.dval[q] = [0] * cnt
            self.dnext[q] = 0
        self.dma_toks = []
        self.uid = 0

    def _wait(self, E, tok):
        key = id(tok.sem)
        if E.waited.get(key, 0) >= tok.val:
            return
        E.eng.wait_ge(tok.sem, tok.val)
        E.waited[key] = tok.val

    def _deps(self, E, reads, writes):
        for b in reads:
            if b.w is not None:
                self._dep1(E, b.w)
            if b.x:
                for t in b.r.values():
                    if t.eng is not E:
                        self._dep1(E, t)
        for b in writes:
            if b.w is not None:
                self._dep1(E, b.w)
            for t in b.r.values():
                self._dep1(E, t)

    def _dep1(self, E, tok):
        if tok.eng is E and E.name == "tensor":
            return
        self._wait(E, tok)

    def _mark(self, tok, reads, writes):
        for b in writes:
            b.w = tok
            b.r = {}
        for b in reads:
            if tok.eng is None:
                self.uid += 1
                b.r[("d", self.uid)] = tok
            else:
                b.r[tok.eng.name] = tok

    def op(self, engname, fn, reads=(), writes=(), inc=True):
        E = self.engs[engname]
        self._deps(E, reads, writes)
        ins = fn(E.eng)
        if inc:
            if E.count >= self.EPOCH:
                E.si += 1
                E.count = 0
            E.count += 1
            ins.then_inc(E.sems[E.si], 1)
            tok = Tok(E, E.sems[E.si], E.count)
        else:
            tok = Tok(E, E.sems[E.si], E.count + 1)
        self._mark(tok, reads, writes)
        return tok

    def dma(self, qname, out, in_, reads=(), writes=(), **kw):
        E = self.engs[qname]
        self._deps(E, reads, writes)
        i = self.dnext[qname]
        self.dnext[qname] = (i + 1) % len(self.dsem[qname])
        sem = self.dsem[qname][i]
        dv = self.dval[qname]
        if dv[i] > 0:
            self._wait(E, Tok(None, sem, dv[i]))
        dv[i] += 16
        E.eng.dma_start(out=out, in_=in_, **kw).then_inc(sem, 16)
        tok = Tok(None, sem, dv[i])
        self._mark(tok, reads, writes)
        return tok

    def barrier(self):
        names = list(self.engs)
        for a in names:
            A = self.engs[a]
            for b in names:
                if a == b:
                    continue
                B = self.engs[b]
                if B.count > 0:
                    self._wait(A, Tok(B, B.sems[B.si], B.count))
            for q in self.dsem:
                for i, sem in enumerate(self.dsem[q]):
                    if self.dval[q][i] > 0:
                        self._wait(A, Tok(None, sem, self.dval[q][i]))

    def finish(self):
        self.barrier()


def dram_bcast(ap2d_row, nparts, n):
    return bass.AP(tensor=ap2d_row.tensor, offset=ap2d_row.offset, ap=[[0, nparts], [1, n]])


def tok_tiles():
    tl = [(i * 128, 128) for i in range(SEQ // 128)]
    tl.append((SEQ, NS))
    return tl


def tok_blocks():
    bl = [(i * 512, 512) for i in range(SEQ // 512)]
    bl.append((SEQ, NS))
    return bl


class Prog:
    def __init__(self, stages):
        self.stages = stages
        nc = self.nc = bass.Bass("TRN2", target_bir_lowering=False)
        self.es = contextlib.ExitStack()
        self.S = Sched(nc, self.es)
        self.I = {}
        self.O = {}
        self.X = {}
        self.uid = 0

    def sbt(self, name, shape, dt):
        self.uid += 1
        return self.nc.sbuf_tensor(f"{name}_u{self.uid}", shape, dt)

    def pst(self, name, shape, dt):
        self.uid += 1
        return self.nc.psum_tensor(f"{name}_u{self.uid}", shape, dt)

    def din(self, name, shape, dt=F32):
        self.I[name] = self.nc.dram_tensor(name, list(shape), dt, kind="ExternalInput").ap()
        return self.I[name]

    def dout(self, name, shape, dt=F32):
        self.O[name] = self.nc.dram_tensor(name, list(shape), dt, kind="ExternalOutput").ap()
        return self.O[name]

    def dscr(self, name, shape, dt=F32):
        self.X[name] = self.nc.dram_tensor(name, list(shape), dt, kind="Internal").ap()
        return self.X[name]

    def declare(self):
        din, dout, dscr = self.din, self.dout, self.dscr
        din("xp", [SEQ, D]); din("xs", [NS, D])
        din("st_wkv", [DEPTH, RH, RN, RN]); din("st_shift", [DEPTH, RPROJ])
        din("st_lruh", [DEPTH, LW]); din("st_conv", [DEPTH, 3, LW])
        din("cache_k", [DEPTH, SEQ, AW]); din("cache_v", [DEPTH, SEQ, AW])
        din("rel_bias", [32, AH])
        for n in ("norm_mix_pre", "norm_mix_post", "norm_ffn_pre", "norm_ffn_post"):
            din(n, [DEPTH, D])
        din("w_in", [DEPTH, D, N_IN]); din("w_out", [DEPTH, D, D])
        din("rwkv_mu", [DEPTH, RPROJ]); din("rwkv_w0", [DEPTH, RW]); din("rwkv_w_up", [DEPTH, 64, RW])
        din("rwkv_a0", [DEPTH, RW]); din("rwkv_a_up", [DEPTH, 64, RW]); din("rwkv_g_up", [DEPTH, 128, RW])
        din("rwkv_k_k", [DEPTH, RW]); din("rwkv_k_a", [DEPTH, RW]); din("rwkv_r_k", [DEPTH, RW])
        din("rwkv_lnx_g", [DEPTH, RW]); din("rwkv_lnx_b", [DEPTH, RW])
        din("lru_conv_w", [DEPTH, 4, LW]); din("lru_conv_b", [DEPTH, LW])
        din("lru_wa", [DEPTH, 8, 64, 64]); din("lru_ba", [DEPTH, LW])
        din("lru_wx", [DEPTH, 8, 64, 64]); din("lru_bx", [DEPTH, LW]); din("lru_lambda", [DEPTH, LW])
        din("ffn_w1", [DEPTH, D, DFF]); din("ffn_w2", [DEPTH, DFF, D])
        din("c_ident", [128, 128]); din("c_tri", [128, 128]); din("c_stri", [128, 128]); din("c_ltri", [128, 128])
        din("c_onehot", [32, 387]); din("c_d01", [4, 4]); din("c_dneg", [4, 4]); din("c_colmask", [128, 4, 4]); din("c_antiident", [128, 128])
        dout("yp", [SEQ, D]); dout("ys", [NS, D])
        dout("wkv_p", [DEPTH, RH, RN, RN]); dout("wkv_s", [DEPTH, RH, RN, RN])
        dout("shift_p", [DEPTH, RPROJ]); dout("shift_s", [DEPTH, RPROJ])
        dout("lruh_p", [DEPTH, LW]); dout("lruh_s", [DEPTH, LW])
        dout("conv_p", [DEPTH, 3, LW]); dout("conv_s", [DEPTH, 3, LW])
        dout("k_p", [DEPTH, SEQ, AW]); dout("k_s", [DEPTH, NS, AW])
        dout("v_p", [DEPTH, SEQ, AW]); dout("v_s", [DEPTH, NS, AW])
        dscr("projT", [NFM - 2048, TT])
        dscr("qT", [AW, TT], BF16)
        dscr("kT", [AW, TT], BF16)
        dscr("xres", [TT, D])
        dscr("yT", [D, TT], BF16)
        dscr("Gd", [8, 3, 383])
        dscr("x1s", [TT, D])
        dscr("hfT", [D, TT], BF16)
        dscr("Os", [3, SEQ, AW])
        dscr("Ls", [3, SEQ, 8])

    def load_consts(self):
        nc, S, es = self.nc, self.S, self.es
        self.identf = es.enter_context(self.sbt("identf", [128, 128], F32))
        self.identb = es.enter_context(self.sbt("identb", [128, 128], BF16))
        self.b_ident = Buf("ident")
        S.dma("sync", self.identf[:], self.I["c_ident"][:, :], writes=[self.b_ident])
        S.dma("gpsimd", self.identb[:], self.I["c_ident"][:, :], writes=[self.b_ident])

    def stage_norm_T(self, l, first, hT, b_hT, gname):
        nc, S = self.nc, self.S
        I = self.I
        with contextlib.ExitStack() as es:
            sb = lambda n, s, d: es.enter_context(self.sbt(n, s, d))
            xt = [sb(f"n_xt{j}", [128, D], F32) for j in range(2)]
            hb = [sb(f"n_hb{j}", [128, D], BF16) for j in range(2)]
            junk = sb("n_junk", [128, D], BF16)
            gt = sb("n_gt", [128, D], F32)
            ss = [sb(f"n_ss{j}", [128, 2], F32) for j in range(2)]
            pt = [es.enter_context(self.pst(f"n_pt{j}", [128, D], BF16)) for j in range(2)]
            b_xt = [Buf(), Buf()]; b_hb = [Buf(), Buf()]; b_junk = Buf(); b_gt = Buf()
            b_ss = [Buf(), Buf()]; b_pt = [PBuf(), PBuf()]
            S.dma("sync", gt[:], dram_bcast(I[gname][l:l + 1, :], 128, D), writes=[b_gt])
            for ti, (c0, n) in enumerate(tok_tiles()):
                j = ti % 2
                if c0 < SEQ:
                    src = (I["xp"] if first else self.X["xres"])[c0:c0 + n, :]
                else:
                    src = I["xs"][:, :] if first else self.X["xres"][c0:c0 + n, :]
                S.dma("sync", xt[j][:n, :], src, writes=[b_xt[j]])
                S.op("scalar", lambda e: e.activation(out=junk[:n, :], in_=xt[j][:n, :], func=AF.Square,
                                                      scale=float(D ** -0.5), accum_out=ss[j][:n, 0:1]),
                     reads=[b_xt[j]], writes=[b_junk, b_ss[j]])
                S.op("vector", lambda e: e.tensor_scalar(out=ss[j][:n, 1:2], in0=ss[j][:n, 0:1], scalar1=1e-6,
                                                         scalar2=None, op0=ALU.add),
                     reads=[b_ss[j]], writes=[b_ss[j]])
                S.op("scalar", lambda e: e.activation(out=ss[j][:n, 0:1], in_=ss[j][:n, 1:2], func=AF.Sqrt),
                     reads=[b_ss[j]], writes=[b_ss[j]])
                S.op("vector", lambda e: e.reciprocal(out=ss[j][:n, 1:2], in_=ss[j][:n, 0:1]),
                     reads=[b_ss[j]], writes=[b_ss[j]])
                S.op("vector", lambda e: e.scalar_tensor_tensor(out=hb[j][:n, :], in0=xt[j][:n, :],
                                                                scalar=ss[j][:n, 1:2], in1=gt[:n, :],
                                                                op0=ALU.mult, op1=ALU.mult),
                     reads=[b_xt[j], b_ss[j], b_gt], writes=[b_hb[j]])
                for c in range(16):
                    S.op("tensor", lambda e: e.transpose(out=pt[j][:, c * 128:c * 128 + n],
                                                         in_=hb[j][:n, c * 128:(c + 1) * 128],
                                                         identity=self.identb[:n, :n]),
                         reads=[b_hb[j], self.b_ident], writes=[b_pt[j]], inc=(c == 15))
                src3 = pt[j][:, :].rearrange("p (c t) -> p c t", c=16)[:, :, 0:n]
                eng = "scalar" if ti % 2 == 0 else "vector"
                if eng == "scalar":
                    S.op("scalar", lambda e: e.copy(out=hT[:, :, c0:c0 + n], in_=src3),
                         reads=[b_pt[j]], writes=[b_hT])
                else:
                    S.op("vector", lambda e: e.tensor_copy(out=hT[:, :, c0:c0 + n], in_=src3),
                         reads=[b_pt[j]], writes=[b_hT])
            S.barrier()

    def stage_win(self, l, hT, b_hT):
        nc, S = self.nc, self.S
        I, O, X = self.I, self.O, self.X
        w_in = I["w_in"]
        blocks = []
        for c0 in range(0, 3584, 512):
            blocks.append((c0, 512, True, False))
        blocks.append((3584, 256, True, False))
        blocks.append((3840, 512, True, True))
        blocks.append((4352, 512, True, True))
        blocks.append((4864, 512, False, True))
        blocks.append((5376, 512, False, True))
        with contextlib.ExitStack() as es:
            sb = lambda n, s, d: es.enter_context(self.sbt(n, s, d))
            wb = [sb(f"w_wb{j}", [128, 16, 512], BF16) for j in range(2)]
            b_wb = [Buf(), Buf()]
            NST = 4
            st = [sb(f"w_st{j}", [128, 512], F32) for j in range(NST)]
            stb = [sb(f"w_stb{j}", [128, 512], BF16) for j in range(NST)]
            b_st = [Buf() for _ in range(NST)]
            b_stb = [Buf() for _ in range(NST)]
            NPS = 6
            ps = [es.enter_context(self.pst(f"w_ps{j}", [128, 512], F32)) for j in range(NPS)]
            b_ps = [PBuf() for _ in range(NPS)]
            cnt = {"ps": 0, "st": 0, "ev": 0}

            def load_block(bi):
                c0, ncols, fm, tm = blocks[bi]
                j = bi % 2
                src = w_in[l, :, c0:c0 + ncols].rearrange("(c p) n -> p c n", p=128)
                S.dma("gpsimd", wb[j][:, 0:8, 0:ncols], src[:, 0:8, :], writes=[b_wb[j]])
                S.dma("gpsimd", wb[j][:, 8:16, 0:ncols], src[:, 8:16, :], writes=[b_wb[j]])

            def evac(pj, n_p, n_f, dst_kind, dst_ap):
                k = cnt["st"] % NST
                cnt["st"] += 1
                use_act = cnt["ev"] % 2 == 0
                cnt["ev"] += 1
                if dst_kind == "f32":
                    tgt, bt = st[k], b_st[k]
                else:
                    tgt, bt = stb[k], b_stb[k]
                if use_act:
                    S.op("scalar", lambda e: e.copy(out=tgt[:n_p, :n_f], in_=ps[pj][:n_p, :n_f]),
                         reads=[b_ps[pj]], writes=[bt])
                else:
                    S.op("vector", lambda e: e.tensor_copy(out=tgt[:n_p, :n_f], in_=ps[pj][:n_p, :n_f]),
                         reads=[b_ps[pj]], writes=[bt])
                S.dma("sync", dst_ap, tgt[:n_p, :n_f], reads=[bt])

            load_block(0)
            for bi, (c0, ncols, fm, tm) in enumerate(blocks):
                j = bi % 2
                if bi + 1 < len(blocks):
                    load_block(bi + 1)
                if fm:
                    for ft in range(ncols // 128):
                        f0 = c0 + ft * 128
                        for (t0, nt) in tok_blocks():
                            pj = cnt["ps"] % NPS
                            cnt["ps"] += 1
                            for c in range(16):
                                S.op("tensor", lambda e: e.matmul(ps[pj][:, :nt], lhsT=wb[j][:, c, ft * 128:(ft + 1) * 128],
                                                                  rhs=hT[:, c, t0:t0 + nt], start=(c == 0), stop=(c == 15)),
                                     reads=[b_wb[j], b_hT], writes=[b_ps[pj]], inc=(c == 15))
                            if f0 < C_Q:
                                evac(pj, 128, nt, "f32", X["projT"][f0:f0 + 128, t0:t0 + nt])
                            elif f0 < C_K:
                                evac(pj, 128, nt, "bf16", X["qT"][f0 - C_Q:f0 - C_Q + 128, t0:t0 + nt])
                            else:
                                evac(pj, 128, nt, "bf16", X["kT"][f0 - C_K:f0 - C_K + 128, t0:t0 + nt])
                if tm:
                    for (t0, nt) in tok_tiles():
                        pj = cnt["ps"] % NPS
                        cnt["ps"] += 1
                        for c in range(16):
                            S.op("tensor", lambda e: e.matmul(ps[pj][:nt, :ncols], lhsT=hT[:, c, t0:t0 + nt],
                                                              rhs=wb[j][:, c, 0:ncols], start=(c == 0), stop=(c == 15)),
                                 reads=[b_wb[j], b_hT], writes=[b_ps[pj]], inc=(c == 15))
                        if c0 < C_V:
                            dp, dsm, off = O["k_p"], O["k_s"], c0 - C_K
                        else:
                            dp, dsm, off = O["v_p"], O["v_s"], c0 - C_V
                        if t0 < SEQ:
                            dst = dp[l, t0:t0 + nt, off:off + ncols]
                        else:
                            dst = dsm[l, 0:nt, off:off + ncols]
                        evac(pj, nt, ncols, "f32", dst)
            S.barrier()

    def fm_vec(self, ap1d, ntiles):
        return bass.AP(tensor=ap1d.tensor, offset=ap1d.offset, ap=[[1, 128], [128, ntiles]])

    def stage_lru(self, l):
        nc, S = self.nc, self.S
        I, O, X = self.I, self.O, self.X
        with contextlib.ExitStack() as es:
            sb = lambda n, s, d: es.enter_context(self.sbt(n, s, d))
            T = SEQ
            cw = sb("l_cw", [128, 4, 4], F32)
            cb = sb("l_cb", [128, 4], F32)
            ba = sb("l_ba", [128, 4], F32)
            bx = sb("l_bx", [128, 4], F32)
            lam = sb("l_lam", [128, 4], F32)
            sp = sb("l_sp", [128, 6, 4], F32)
            h0 = sb("l_h0", [128, 4], F32)
            wbd = sb("l_wbd", [128, 2, 4, 128], BF16)
            wk_names = ("xpad", "gb", "xc", "gr", "gi", "tmp", "hh", "xcb", "yb")
            wk = {}
            wkb = {}
            for nm_ in wk_names:
                shp_ = [128, T + 3] if nm_ == "xpad" else [128, T]
                dt_ = BF16 if nm_ in ("xcb", "yb") else F32
                wk[nm_] = [sb(f"l_{nm_}{jj_}", shp_, dt_) for jj_ in range(2)]
                wkb[nm_] = [Buf(), Buf()]
            lcnt = 0
            ps = [es.enter_context(self.pst(f"l_ps{j}", [128, 512], F32)) for j in range(2)]
            b_par = Buf(); b_wbd = Buf(); b_xpad = Buf(); b_gb = Buf(); b_xc = Buf(); b_gr = Buf(); b_gi = Buf()
            b_tmp = Buf(); b_hh = Buf(); b_xcb = Buf(); b_yb = Buf(); b_ps = [PBuf(), PBuf()]; b_h0 = Buf()
            nck = dict(allow_slow_non_contiguous=True)
            for tap in range(4):
                S.dma("sync", cw[:, tap, :], self.fm_vec(I["lru_conv_w"][l, tap, :], 4), writes=[b_par], **nck)
            S.dma("sync", cb[:, :], self.fm_vec(I["lru_conv_b"][l, :], 4), writes=[b_par], **nck)
            S.dma("sync", ba[:, :], self.fm_vec(I["lru_ba"][l, :], 4), writes=[b_par], **nck)
            S.dma("sync", bx[:, :], self.fm_vec(I["lru_bx"][l, :], 4), writes=[b_par], **nck)
            S.dma("sync", lam[:, :], self.fm_vec(I["lru_lambda"][l, :], 4), writes=[b_par], **nck)
            S.dma("sync", h0[:, :], self.fm_vec(I["st_lruh"][l, :], 4), writes=[b_h0], **nck)
            S.op("vector", lambda e: e.memset(wbd[:], 0.0), writes=[b_wbd])
            for gi_, wn in enumerate(("lru_wa", "lru_wx")):
                for n in range(8):
                    ct, hf = n // 2, n % 2
                    S.dma("gpsimd", wbd[hf * 64:(hf + 1) * 64, gi_, ct, hf * 64:(hf + 1) * 64], I[wn][l, n, :, :],
                          writes=[b_wbd])
            e_, ln_, ser, msk, res = sp[:, 0, :], sp[:, 1, :], sp[:, 2, :], sp[:, 3, :], sp[:, 4, :]
            S.op("scalar", lambda e: e.activation(out=e_, in_=lam[:, :], func=AF.Exp, scale=-1.0), reads=[b_par], writes=[b_par])
            S.op("scalar", lambda e: e.activation(out=ln_, in_=e_, func=AF.Ln, bias=1.0), reads=[b_par], writes=[b_par])
            S.op("vector", lambda e: e.tensor_scalar(out=ser, in0=e_, scalar1=-0.25, scalar2=1.0 / 3.0, op0=ALU.mult, op1=ALU.add), reads=[b_par], writes=[b_par])
            S.op("vector", lambda e: e.tensor_tensor(out=ser, in0=ser, in1=e_, op=ALU.mult), reads=[b_par], writes=[b_par])
            S.op("vector", lambda e: e.tensor_scalar(out=ser, in0=ser, scalar1=-1.0, scalar2=0.5, op0=ALU.mult, op1=ALU.add), reads=[b_par], writes=[b_par])
            S.op("vector", lambda e: e.tensor_tensor(out=ser, in0=ser, in1=e_, op=ALU.mult), reads=[b_par], writes=[b_par])
            S.op("vector", lambda e: e.tensor_scalar(out=ser, in0=ser, scalar1=-1.0, scalar2=1.0, op0=ALU.mult, op1=ALU.add), reads=[b_par], writes=[b_par])
            S.op("vector", lambda e: e.tensor_tensor(out=ser, in0=ser, in1=e_, op=ALU.mult), reads=[b_par], writes=[b_par])
            S.op("vector", lambda e: e.tensor_single_scalar(out=msk, in_=e_, scalar=0.05, op=ALU.is_lt), reads=[b_par], writes=[b_par])
            S.op("vector", lambda e: e.tensor_tensor(out=ser, in0=ser, in1=ln_, op=ALU.subtract), reads=[b_par], writes=[b_par])
            S.op("vector", lambda e: e.tensor_tensor(out=ser, in0=ser, in1=msk, op=ALU.mult), reads=[b_par], writes=[b_par])
            S.op("vector", lambda e: e.tensor_tensor(out=res, in0=ser, in1=ln_, op=ALU.add), reads=[b_par], writes=[b_par])
            S.op("vector", lambda e: e.tensor_scalar(out=res, in0=res, scalar1=-8.0, scalar2=None, op0=ALU.mult), reads=[b_par], writes=[b_par])
            m8sp = res

            items_ = [(col0, Tn, is_s, ct) for (col0, Tn, is_s) in ((0, SEQ, False), (SEQ, NS, True)) for ct in range(4)]

            def lru_loads(idx):
                col0, Tn, is_s, ct = items_[idx]
                jq = idx % 2
                xpad_, gb_ = wk["xpad"][jq], wk["gb"][jq]
                S.dma("sync", xpad_[:, 3:3 + Tn], X["projT"][C_LX + ct * 128:C_LX + (ct + 1) * 128, col0:col0 + Tn], writes=[wkb["xpad"][jq]])
                S.dma("sync", gb_[:, :Tn], X["projT"][C_LG + ct * 128:C_LG + (ct + 1) * 128, col0:col0 + Tn], writes=[wkb["gb"][jq]])
                if is_s:
                    src = bass.AP(tensor=I["st_conv"].tensor, offset=I["st_conv"][l, 0, ct * 128:(ct + 1) * 128].offset,
                                  ap=[[1, 128], [LW, 3]])
                    S.dma("sync", xpad_[:, 0:3], src, writes=[wkb["xpad"][jq]], **nck)
                else:
                    S.op("vector", lambda e: e.memset(xpad_[:, 0:3], 0.0), writes=[wkb["xpad"][jq]])

            lru_loads(0)
            for idx_, (col0, Tn, is_s, ct) in enumerate(items_):
                if True:
                    jb_ = idx_ % 2
                    xpad, gb, xc, gr, gi, tmp, hh, xcb, yb = (wk[nm_][jb_] for nm_ in wk_names)
                    b_xpad, b_gb, b_xc, b_gr, b_gi, b_tmp, b_hh, b_xcb, b_yb = (wkb[nm_][jb_] for nm_ in wk_names)
                    xa = xpad[:, 3:3 + Tn]
                    if idx_ + 1 < len(items_):
                        lru_loads(idx_ + 1)
                    S.op("vector", lambda e: e.tensor_scalar(out=xc[:, :Tn], in0=xa, scalar1=cw[:, 3, ct:ct + 1], scalar2=cb[:, ct:ct + 1],
                                                             op0=ALU.mult, op1=ALU.add), reads=[b_xpad, b_par], writes=[b_xc])
                    for tap in range(3):
                        S.op("vector", lambda e: e.scalar_tensor_tensor(out=xc[:, :Tn], in0=xpad[:, tap:tap + Tn], scalar=cw[:, tap, ct:ct + 1],
                                                                        in1=xc[:, :Tn], op0=ALU.mult, op1=ALU.add),
                             reads=[b_xpad, b_par, b_xc], writes=[b_xc])
                    S.op("scalar", lambda e: e.copy(out=xcb[:, :Tn], in_=xc[:, :Tn]), reads=[b_xc], writes=[b_xcb])
                    k = 0
                    for t0 in range(0, Tn, 512):
                        nt = min(512, Tn - t0)
                        for gsel, (dst, bd, bias) in enumerate(((gr, b_gr, ba), (gi, b_gi, bx))):
                            pj = k % 2
                            k += 1
                            S.op("tensor", lambda e: e.matmul(ps[pj][:, :nt], lhsT=wbd[:, gsel, ct, :], rhs=xcb[:, t0:t0 + nt], start=True, stop=True),
                                 reads=[b_wbd, b_xcb], writes=[b_ps[pj]])
                            S.op("scalar", lambda e: e.activation(out=dst[:, t0:t0 + nt], in_=ps[pj][:, :nt], func=AF.Sigmoid, bias=bias[:, ct:ct + 1]),
                                 reads=[b_ps[pj], b_par], writes=[bd])
                    S.op("vector", lambda e: e.tensor_scalar(out=gr[:, :Tn], in0=gr[:, :Tn], scalar1=m8sp[:, ct:ct + 1], scalar2=None, op0=ALU.mult),
                         reads=[b_gr, b_par], writes=[b_gr])
                    S.op("scalar", lambda e: e.activation(out=tmp[:, :Tn], in_=gr[:, :Tn], func=AF.Exp, scale=2.0), reads=[b_gr], writes=[b_tmp])
                    S.op("scalar", lambda e: e.activation(out=gr[:, :Tn], in_=gr[:, :Tn], func=AF.Exp), reads=[b_gr], writes=[b_gr])
                    S.op("vector", lambda e: e.tensor_scalar(out=tmp[:, :Tn], in0=tmp[:, :Tn], scalar1=-1.0, scalar2=1.0, op0=ALU.mult, op1=ALU.add),
                         reads=[b_tmp], writes=[b_tmp])
                    S.op("scalar", lambda e: e.activation(out=tmp[:, :Tn], in_=tmp[:, :Tn], func=AF.Sqrt), reads=[b_tmp], writes=[b_tmp])
                    S.op("vector", lambda e: e.tensor_tensor(out=gi[:, :Tn], in0=gi[:, :Tn], in1=xc[:, :Tn], op=ALU.mult), reads=[b_gi, b_xc], writes=[b_gi])
                    S.op("vector", lambda e: e.tensor_tensor(out=gi[:, :Tn], in0=gi[:, :Tn], in1=tmp[:, :Tn], op=ALU.mult), reads=[b_gi, b_tmp], writes=[b_gi])
                    init = h0[:, ct:ct + 1] if is_s else 0.0
                    S.op("vector", lambda e: e.tensor_tensor_scan(out=hh[:, :Tn], data0=gr[:, :Tn], data1=gi[:, :Tn], initial=init, op0=ALU.mult, op1=ALU.add),
                         reads=[b_gr, b_gi, b_h0], writes=[b_hh])
                    S.op("gpsimd", lambda e: e.tensor_tensor(out=tmp[:, :Tn], in0=gb[:, :Tn], in1=gb[:, :Tn], op=ALU.mult), reads=[b_gb], writes=[b_tmp])
                    S.op("gpsimd", lambda e: e.tensor_scalar(out=tmp[:, :Tn], in0=tmp[:, :Tn], scalar1=0.044715, scalar2=1.0, op0=ALU.mult, op1=ALU.add),
                         reads=[b_tmp], writes=[b_tmp])
                    S.op("gpsimd", lambda e: e.tensor_tensor(out=tmp[:, :Tn], in0=tmp[:, :Tn], in1=gb[:, :Tn], op=ALU.mult), reads=[b_tmp, b_gb], writes=[b_tmp])
                    S.op("scalar", lambda e: e.activation(out=tmp[:, :Tn], in_=tmp[:, :Tn], func=AF.Sigmoid, scale=1.5957691216057308), reads=[b_tmp], writes=[b_tmp])
                    S.op("gpsimd", lambda e: e.tensor_tensor(out=tmp[:, :Tn], in0=tmp[:, :Tn], in1=gb[:, :Tn], op=ALU.mult), reads=[b_tmp, b_gb], writes=[b_tmp])
                    S.op("vector", lambda e: e.tensor_tensor(out=yb[:, :Tn], in0=tmp[:, :Tn], in1=hh[:, :Tn], op=ALU.mult), reads=[b_tmp, b_hh], writes=[b_yb])
                    S.dma("sync", X["yT"][RW + ct * 128:RW + (ct + 1) * 128, col0:col0 + Tn], yb[:, :Tn], reads=[b_yb])
                    oh = O["lruh_s"] if is_s else O["lruh_p"]
                    oc = O["conv_s"] if is_s else O["conv_p"]
                    S.dma("sync", bass.AP(tensor=oh.tensor, offset=oh[l, ct * 128:(ct + 1) * 128].offset, ap=[[1, 128], [1, 1]]),
                          hh[:, Tn - 1:Tn], reads=[b_hh], **nck)
                    S.dma("sync", bass.AP(tensor=oc.tensor, offset=oc[l, 0, ct * 128:(ct + 1) * 128].offset, ap=[[1, 128], [LW, 3]]),
                          xpad[:, Tn:Tn + 3], reads=[b_xpad], **nck)
            S.barrier()

    def vv(self, out, in0, in1, op, R, W, eng="vector"):
        return self.S.op(eng, lambda e: e.tensor_tensor(out=out, in0=in0, in1=in1, op=op), reads=R, writes=W)

    def vs(self, out, in0, s1, s2, op0, op1, R, W, eng="vector"):
        if op1 is None:
            return self.S.op(eng, lambda e: e.tensor_scalar(out=out, in0=in0, scalar1=s1, scalar2=None, op0=op0), reads=R, writes=W)
        return self.S.op(eng, lambda e: e.tensor_scalar(out=out, in0=in0, scalar1=s1, scalar2=s2, op0=op0, op1=op1), reads=R, writes=W)

    def stt(self, out, in0, scalar, in1, op0, op1, R, W):
        return self.S.op("vector", lambda e: e.scalar_tensor_tensor(out=out, in0=in0, scalar=scalar, in1=in1, op0=op0, op1=op1), reads=R, writes=W)

    def act(self, out, in_, func, R, W, **kw):
        return self.S.op("scalar", lambda e: e.activation(out=out, in_=in_, func=func, **kw), reads=R, writes=W)

    def cp(self, out, in_, R, W, eng="vector"):
        if eng == "scalar":
            return self.S.op("scalar", lambda e: e.copy(out=out, in_=in_), reads=R, writes=W)
        return self.S.op(eng, lambda e: e.tensor_copy(out=out, in_=in_), reads=R, writes=W)

    def pe_fence(self):
        self.S.op("tensor", lambda e: e.matmul(self.ps_dummy[0:1, 0:1], lhsT=self.identb[:, 0:1], rhs=self.identb[:, 0:1], start=True, stop=True),
                  reads=[self.b_ident], writes=[], inc=True)

    def mm(self, out, lhsT, rhs, R, W, start=True, stop=True, inc=None, f32r=False):
        if inc is None:
            inc = stop
        guard = inc and (lhsT.dtype == F32) and not (f32r and self.stages.get('nofence_r', True))
        if f32r:
            lhsT = lhsT.bitcast(F32R)
            rhs = rhs.bitcast(F32R)
        t = self.S.op("tensor", lambda e: e.matmul(out, lhsT=lhsT, rhs=rhs, start=start, stop=stop), reads=R, writes=W, inc=(inc and not guard))
        if guard:
            self.pe_fence()
        return t

    def tr(self, out, in_, ident, R, W, inc=True):
        guard = inc and (in_.dtype == F32)
        t = self.S.op("tensor", lambda e: e.transpose(out=out, in_=in_, identity=ident), reads=list(R) + [self.b_ident], writes=W, inc=(inc and not guard))
        if guard:
            self.pe_fence()
        return t

    def getps(self):
        i = self.ps_next
        self.ps_next = (i + 1) % len(self.ps_pool)
        return self.ps_pool[i], self.ps_bufs[i]

    @staticmethod
    def bc_last(a, k):
        return bass.AP(tensor=a.tensor, offset=a.offset, ap=[list(x) for x in a.ap] + [[0, k]])

    def stage_rwkv(self, l):
        nc, S = self.nc, self.S
        I, O, X = self.I, self.O, self.X
        vv, vs, stt, act, cp, mm, tr = self.vv, self.vs, self.stt, self.act, self.cp, self.mm, self.tr
        nck = dict(allow_slow_non_contiguous=True)
        with contextlib.ExitStack() as es:
            sb = lambda n, s, d: es.enter_context(self.sbt(n, s, d))
            allps = [es.enter_context(self.pst(f"r_ps{j}", [128, 512], F32)) for j in range(7)]
            allpb = [PBuf() for _ in range(7)]
            own_ps, own_pb = allps[0:4], allpb[0:4]
            self.ps_pool = allps[4:7]
            self.ps_bufs = allpb[4:7]
            self.ps_next = 0
            self.ps_dummy = es.enter_context(self.pst("r_psd", [128, 512], F32))
            mu = sb("r_mu", [128, 14], F32)
            lup = sb("r_lup", [128, 512], BF16)
            gup = sb("r_gup", [128, 512], BF16)
            bpar = sb("r_bpar", [128, 7, 512], F32)
            tri = sb("r_tri", [128, 128], F32)
            stri = sb("r_stri", [128, 128], F32)
            ltri = sb("r_ltri", [128, 128], F32)
            ones = sb("r_ones", [128, 1], F32)
            ST = sb("r_ST", [64, 8, 64], F32)
            STd = sb("r_STd", [64, 8, 64], F32)
            STr = sb("r_STr", [64, 8, 64], F32); b_STr_all = [Buf() for _ in range(8)]
            b_par = Buf(); b_STall = [Buf() for _ in range(8)]; b_STd_all = [Buf() for _ in range(8)]
            S.dma("sync", mu[:, :], self.fm_vec(I["rwkv_mu"][l, :], 14), writes=[b_par], **nck)
            S.dma("gpsimd", lup[0:64, :], I["rwkv_w_up"][l, :, :], writes=[b_par])
            S.dma("gpsimd", lup[64:128, :], I["rwkv_a_up"][l, :, :], writes=[b_par])
            S.dma("gpsimd", gup[:, :], I["rwkv_g_up"][l, :, :], writes=[b_par])
            for i, nm in enumerate(("rwkv_w0", "rwkv_a0", "rwkv_k_k", "rwkv_k_a", "rwkv_r_k", "rwkv_lnx_g", "rwkv_lnx_b")):
                S.dma("sync", bpar[:, i, :], dram_bcast(I[nm][l:l + 1, :], 128, RW), writes=[b_par])
            S.dma("sync", tri[:, :], I["c_tri"][:, :], writes=[b_par])
            S.dma("sync", stri[:, :], I["c_stri"][:, :], writes=[b_par])
            S.dma("sync", ltri[:, :], I["c_ltri"][:, :], writes=[b_par])
            S.op("vector", lambda e: e.memset(ones[:], 1.0), writes=[b_par])
            W0, A0, KK_, KA_, RK_, LG_, LB_ = (bpar[:, i, :] for i in range(7))

            def T2(name, shape, dt=F32, single=False):
                if single:
                    t_ = sb(name, shape, dt); b_ = Buf()
                    return [t_, t_], [b_, b_]
                return [sb(f"{name}{j}", shape, dt) for j in range(2)], [Buf(), Buf()]

            pT, b_pT = T2("r_pT", [128, 14, 129])
            mT, b_mT = T2("r_mT", [128, 14, 128], single=True)
            dT, b_dT = T2("r_dT", [128, 14, 128], single=True)
            lin, b_lin = T2("r_lin", [128, 2, 128], BF16)
            names = ["r", "k", "v", "g", "a", "kk", "k2", "nlw", "epos", "eneg", "eprev", "ah", "bh", "kh", "rh", "t1", "t2", "y", "bon", "vr"]
            tm = {}; b_tm = {}
            for nm in names:
                tm[nm], b_tm[nm] = T2("r_tm_" + nm, [128, 512], single=(nm in ("kk", "k2", "epos", "eneg", "eprev", "t1", "t2", "vr")))
            sm, b_sm = T2("r_sm", [128, 8, 8])
            fmT, b_fmT = T2("r_fmT", [64, 8, 4, 128])
            pc, b_pc = T2("r_pc", [64, 8])
            yb, b_yb = T2("r_yb", [128, 512], BF16)
            ybT, b_ybT = T2("r_ybT", [128, 4, 128], BF16)
            NH = 4
            Am = [sb(f"r_Am{j}", [128, 4, 128], F32) for j in range(NH)]; b_Am = [Buf() for _ in range(NH)]
            Mx = [sb(f"r_Mx{j}", [128, 2, 128], F32) for j in range(NH)]; b_Mx = [Buf() for _ in range(NH)]
            Nx = [sb(f"r_Nx{j}", [128, 2, 128], F32) for j in range(NH)]; b_Nx = [Buf() for _ in range(NH)]
            Tt = [sb(f"r_Tt{j}", [128, 2, 128], F32) for j in range(NH)]; b_Tt = [Buf() for _ in range(NH)]
            akv = [sb(f"r_akv{j}", [128, 64], F32) for j in range(NH)]; b_akv = [Buf() for _ in range(NH)]
            wmT = [sb(f"r_wmT{j}", [64, 128], F32) for j in range(NH)]; b_wmT = [Buf() for _ in range(NH)]
            U = [sb(f"r_U{j}", [128, 64], F32) for j in range(NH)]; b_U = [Buf() for _ in range(NH)]
            sinit = sb("r_sinit", [64, 8, 64], F32); b_sinit = Buf()
            sout = sinit; b_sout = b_sinit
            hcnt = [0]

            for (col0, Tn, is_s) in ((0, SEQ, False), (SEQ, NS, True)):
                if is_s and not self.stages.get('rwkv_sample', True):
                    continue
                if is_s:
                    S.dma("sync", sinit[:, :, :], I["st_wkv"][l].rearrange("h v k -> v h k"), writes=[b_sinit])
                    for h in range(RH):
                        ps, bp = self.getps()
                        tr(ps[:64, :64], sinit[:, h, :], self.identf[:64, :64], [b_sinit], [bp])
                        cp(ST[:, h, :], ps[:64, :64], [bp], [b_STall[h]])
                else:
                    S.op("vector", lambda e: e.memset(ST[:], 0.0), writes=b_STall)
                cp((STr[:, :, :].bitcast(F32R) if self.stages.get('fp32r', True) else STr[:, :, :]), ST[:, :, :], b_STall, b_STr_all)
                nchunks = (Tn + 127) // 128
                if self.stages.get("rwkv_chunks"):
                    nchunks = min(nchunks, self.stages["rwkv_chunks"])
                for ci in range(nchunks):
                    j = ci % 2
                    t0 = col0 + ci * 128
                    n = min(128, Tn - ci * 128)
                    B = {k_: b_tm[k_][j] for k_ in names}
                    use_r = (n == 128) and self.stages.get('fp32r', True)
                    R_ = (lambda a_: a_.bitcast(F32R)) if self.stages.get('fp32r', True) else (lambda a_: a_)
                    Tm = {k_: tm[k_][j] for k_ in names}
                    def rw_load(ci, col0=col0, Tn=Tn, is_s=is_s):
                        j = ci % 2
                        t0 = col0 + ci * 128
                        n = min(128, Tn - ci * 128)
                        if ci == 0:
                            S.dma("sync", pT[j][:, :, 1:n + 1], X["projT"][0:RPROJ, t0:t0 + n].rearrange("(f p) t -> p f t", p=128), writes=[b_pT[j]])
                            if is_s:
                                S.dma("sync", pT[j][:, :, 0], self.fm_vec(I["st_shift"][l, :], 14), writes=[b_pT[j]], **nck)
                            else:
                                S.op("vector", lambda e: e.memset(pT[j][:, :, 0:1], 0.0), writes=[b_pT[j]])
                        else:
                            S.dma("sync", pT[j][:, :, 0:n + 1], X["projT"][0:RPROJ, t0 - 1:t0 + n].rearrange("(f p) t -> p f t", p=128), writes=[b_pT[j]])
                    if ci == 0:
                        rw_load(0)
                    if ci + 1 < nchunks:
                        rw_load(ci + 1)
                    if ci == nchunks - 1:
                        osh = O["shift_s"] if is_s else O["shift_p"]
                        S.dma("sync", self.fm_vec(osh[l, :], 14), pT[j][:, :, n], reads=[b_pT[j]], **nck)
                    if self.stages.get('rwkv_upto', 9) < 2:
                        continue
                    pcur = pT[j][:, :, 1:n + 1]
                    vv(dT[j][:, :, :n], pT[j][:, :, 0:n], pcur, ALU.subtract, [b_pT[j]], [b_dT[j]])
                    vv(dT[j][:, :, :n], dT[j][:, :, :n], self.bc_last(mu[:, :], n), ALU.mult, [b_dT[j], b_par], [b_dT[j]])
                    vv(mT[j][:, :, :n], dT[j][:, :, :n], pcur, ALU.add, [b_dT[j], b_pT[j]], [b_mT[j]])
                    if self.stages.get('rwkv_upto', 9) < 3:
                        continue
                    act(lin[j][0:64, 0, :n], mT[j][0:64, 12, :n], AF.Tanh, [b_mT[j]], [b_lin[j]])
                    cp(lin[j][64:128, 0, :n], mT[j][64:128, 12, :n], [b_mT[j]], [b_lin[j]], eng="gpsimd")
                    act(lin[j][:, 1, :n], mT[j][:, 13, :n], AF.Sigmoid, [b_mT[j]], [b_lin[j]])
                    ps_w, bp_w = self.getps()
                    mm(ps_w[:n, :], lin[j][0:64, 0, :n], lup[0:64, :], [b_lin[j], b_par], [bp_w])
                    ps_a, bp_a = self.getps()
                    mm(ps_a[:n, :], lin[j][64:128, 0, :n], lup[64:128, :], [b_lin[j], b_par], [bp_a])
                    ps_g, bp_g = self.getps()
                    mm(ps_g[:n, :], lin[j][:, 1, :n], gup[:, :], [b_lin[j], b_par], [bp_g])
                    vv(Tm["t1"][:n, :], ps_w[:n, :], W0[:n, :], ALU.add, [bp_w, b_par], [B["t1"]])
                    vv(Tm["a"][:n, :], ps_a[:n, :], A0[:n, :], ALU.add, [bp_a, b_par], [B["a"]])
                    cp(Tm["g"][:n, :], ps_g[:n, :], [bp_g], [B["g"]], eng="scalar")
                    if self.stages.get('rwkv_upto', 9) < 4:
                        continue
                    for ti_, nm in enumerate(("r", "k", "v")):
                        ps, bp = self.getps()
                        for q in range(4):
                            tr(ps[:n, q * 128:(q + 1) * 128], mT[j][:, ti_ * 4 + q, :n], self.identf[:, :], [b_mT[j]], [bp], inc=(q == 3))
                        cp(Tm[nm][:n, :], ps[:n, :], [bp], [B[nm]], eng=("scalar" if ti_ == 1 else "vector"))
                    cp(R_(Tm["vr"][:n, :]), Tm["v"][:n, :], [B["v"]], [B["vr"]], eng="scalar")
                    if self.stages.get('rwkv_upto', 9) < 5:
                        continue
                    act(Tm["t1"][:n, :], Tm["t1"][:n, :], AF.Exp, [B["t1"]], [B["t1"]], scale=-1.0)
                    act(Tm["t1"][:n, :], Tm["t1"][:n, :], AF.Ln, [B["t1"]], [B["t1"]], bias=1.0)
                    vs(Tm["t1"][:n, :], Tm["t1"][:n, :], -1.0, -0.5, ALU.mult, ALU.add, [B["t1"]], [B["t1"]])
                    act(Tm["nlw"][:n, :], Tm["t1"][:n, :], AF.Exp, [B["t1"]], [B["nlw"]])
                    act(Tm["a"][:n, :], Tm["a"][:n, :], AF.Sigmoid, [B["a"]], [B["a"]])
                    vv(Tm["kk"][:n, :], Tm["k"][:n, :], KK_[:n, :], ALU.mult, [B["k"], b_par], [B["kk"]])
                    vv(Tm["t2"][:n, :], Tm["kk"][:n, :], Tm["kk"][:n, :], ALU.mult, [B["kk"]], [B["t2"]], eng="gpsimd")
                    S.op("vector", lambda e: e.reduce_sum(out=sm[j][:n, 0, :], in_=Tm["t2"][:n, :].rearrange("p (h k) -> p h k", h=8), axis=AX.X),
                         reads=[B["t2"]], writes=[b_sm[j]])
                    act(sm[j][:n, 0, :], sm[j][:n, 0, :], AF.Sqrt, [b_sm[j]], [b_sm[j]])
                    vs(sm[j][:n, 0, :], sm[j][:n, 0, :], 1e-12, None, ALU.max, None, [b_sm[j]], [b_sm[j]])
                    S.op("vector", lambda e: e.reciprocal(out=sm[j][:n, 1, :], in_=sm[j][:n, 0, :]), reads=[b_sm[j]], writes=[b_sm[j]])
                    vv(Tm["kk"][:n, :].rearrange("p (h k) -> p h k", h=8), Tm["kk"][:n, :].rearrange("p (h k) -> p h k", h=8),
                       self.bc_last(sm[j][:n, 1, :], 64), ALU.mult, [B["kk"], b_sm[j]], [B["kk"]])
                    stt(Tm["t2"][:n, :], Tm["a"][:n, :], -1.0, KA_[:n, :], ALU.add, ALU.mult, [B["a"], b_par], [B["t2"]])
                    stt(Tm["k2"][:n, :], Tm["t2"][:n, :], 1.0, Tm["k"][:n, :], ALU.add, ALU.mult, [B["t2"], B["k"]], [B["k2"]])
                    ps_c, bp_c = self.getps()
                    mm(ps_c[:n, :], tri[:n, :n], Tm["nlw"][:n, :], [b_par, B["nlw"]], [bp_c])
                    act(Tm["epos"][:n, :], ps_c[:n, :], AF.Exp, [bp_c], [B["epos"]], scale=-1.0)
                    act(Tm["eneg"][:n, :], ps_c[:n, :], AF.Exp, [bp_c], [B["eneg"]])
                    vv(Tm["t1"][:n, :], ps_c[:n, :], Tm["nlw"][:n, :], ALU.subtract, [bp_c, B["nlw"]], [B["t1"]])
                    act(Tm["eprev"][:n, :], Tm["t1"][:n, :], AF.Exp, [B["t1"]], [B["eprev"]], scale=-1.0)
                    stt(Tm["ah"][:n, :], Tm["kk"][:n, :], -1.0, Tm["eprev"][:n, :], ALU.mult, ALU.mult, [B["kk"], B["eprev"]], [B["ah"]])
                    vv(Tm["bh"][:n, :], Tm["kk"][:n, :], Tm["a"][:n, :], ALU.mult, [B["kk"], B["a"]], [B["bh"]], eng="gpsimd")
                    vv(Tm["bh"][:n, :], Tm["bh"][:n, :], Tm["eneg"][:n, :], ALU.mult, [B["bh"], B["eneg"]], [B["bh"]], eng="gpsimd")
                    vv(Tm["kh"][:n, :], Tm["k2"][:n, :], Tm["eneg"][:n, :], ALU.mult, [B["k2"], B["eneg"]], [B["kh"]])
                    vv(Tm["rh"][:n, :], Tm["r"][:n, :], Tm["epos"][:n, :], ALU.mult, [B["r"], B["epos"]], [B["rh"]], eng="gpsimd")
                    vv(Tm["t2"][:n, :], Tm["r"][:n, :], Tm["k2"][:n, :], ALU.mult, [B["r"], B["k2"]], [B["t2"]], eng="gpsimd")
                    vv(Tm["t2"][:n, :], Tm["t2"][:n, :], RK_[:n, :], ALU.mult, [B["t2"], b_par], [B["t2"]], eng="gpsimd")
                    S.op("vector", lambda e: e.reduce_sum(out=sm[j][:n, 2, :], in_=Tm["t2"][:n, :].rearrange("p (h k) -> p h k", h=8), axis=AX.X),
                         reads=[B["t2"]], writes=[b_sm[j]])
                    vv(Tm["bon"][:n, :].rearrange("p (h k) -> p h k", h=8), Tm["v"][:n, :].rearrange("p (h k) -> p h k", h=8),
                       self.bc_last(sm[j][:n, 2, :], 64), ALU.mult, [B["v"], b_sm[j]], [B["bon"]])
                    if self.stages.get('rwkv_upto', 9) < 6:
                        continue
                    for h in range(RH):
                        ps, bp = self.getps()
                        for q, nm in enumerate(("ah", "rh", "bh", "kh")):
                            tr(ps[:64, q * 128:q * 128 + n], Tm[nm][:n, h * 64:(h + 1) * 64], self.identf[:n, :n], [B[nm]], [bp], inc=(q == 3))
                        cp(R_(fmT[j][:, h, :, :n]), ps[:64, :].rearrange("p (q t) -> p q t", q=4)[:, :, :n], [bp], [b_fmT[j]],
                           eng=("scalar" if h % 2 == 0 else "vector"))
                    ps_p, bp_p = self.getps()
                    for h in range(RH):
                        mm(ps_p[:64, h:h + 1], Tm["nlw"][:n, h * 64:(h + 1) * 64], ones[:n, :], [B["nlw"], b_par], [bp_p], inc=(h == RH - 1))
                    act(pc[j][:, :], ps_p[:64, 0:8], AF.Exp, [bp_p], [b_pc[j]], scale=-1.0)
                    if self.stages.get('rwkv_upto', 9) < 7:
                        continue
                    nsq = max(0, int(np.ceil(np.log2(max(n, 2)))) - 1)
                    if True:
                        def head_gen(h, hj):
                            aT = fmT[j][:, h, 0, :n]; rT = fmT[j][:, h, 1, :n]; bT = fmT[j][:, h, 2, :n]; kT = fmT[j][:, h, 3, :n]
                            arT = fmT[j][:, h, 0:2, :n]
                            v_h = Tm["v"][:n, h * 64:(h + 1) * 64]
                            vr_h = Tm["vr"][:n, h * 64:(h + 1) * 64]
                            ps, bp = (own_ps[h % 4], own_pb[h % 4])
                            o4 = ps[:n, :].rearrange("p (q t) -> p q t", q=4)
                            if not self.stages.get('dbg_nomm'):
                                mm(o4[:, 0, :n], bT, aT, [b_fmT[j]], [bp], inc=False, f32r=use_r)
                                mm(o4[:, 1, :n], bT, rT, [b_fmT[j]], [bp], inc=False, f32r=use_r)
                                mm(o4[:, 2, :n], kT, aT, [b_fmT[j]], [bp], inc=False, f32r=use_r)
                                mm(o4[:, 3, :n], kT, rT, [b_fmT[j]], [bp], f32r=use_r)
                            if self.stages.get('dbg_nomask'):
                                return
                            vv(R_(Am[hj][:n, 0, :n]), o4[:, 0, :n], stri[:n, :n], ALU.mult, [bp, b_par], [b_Am[hj]])
                            vv(R_(Am[hj][:n, 1, :n]), o4[:, 1, :n], tri[:n, :n], ALU.mult, [bp, b_par], [b_Am[hj]])
                            vv(R_(Am[hj][:n, 2, :n]), o4[:, 2, :n], stri[:n, :n], ALU.mult, [bp, b_par], [b_Am[hj]])
                            vv(R_(Am[hj][:n, 3, :n]), o4[:, 3, :n], tri[:n, :n], ALU.mult, [bp, b_par], [b_Am[hj]])
                            if self.stages.get('rwkv_sub', 9) < 1:
                                return
                            yield
                            ps2, bp2 = (own_ps[h % 4], own_pb[h % 4])
                            mm(ps2[:n, :n], aT, bT, [b_fmT[j]], [bp2], f32r=use_r)
                            vv(R_(Nx[hj][:n, 0, :n]), ps2[:n, :n], ltri[:n, :n], ALU.mult, [bp2, b_par], [b_Nx[hj]])
                            if self.stages.get('rwkv_sub', 9) < 2:
                                return
                            yield
                            vv(R_(Tt[hj][:n, 0, :n]), Am[hj][:n, 0, :n], self.identf[:n, :n], ALU.add, [b_Am[hj], self.b_ident], [b_Tt[hj]])
                            cp(R_(Mx[hj][:n, 0, :n]), Am[hj][:n, 0, :n], [b_Am[hj]], [b_Mx[hj]], eng="scalar")
                            cur = 0
                            for it in range(nsq):
                                nxt = 1 - cur
                                ps3, bp3 = (own_ps[h % 4], own_pb[h % 4])
                                mm(ps3[:n, 0:n], Nx[hj][:n, cur, :n], Mx[hj][:n, cur, :n], [b_Nx[hj], b_Mx[hj]], [bp3], inc=False, f32r=use_r)
                                mm(ps3[:n, 128:128 + n], Mx[hj][:n, cur, :n], Nx[hj][:n, cur, :n], [b_Nx[hj], b_Mx[hj]], [bp3], f32r=use_r)
                                yield
                                cp(R_(Mx[hj][:n, nxt, :n]), ps3[:n, 0:n], [bp3], [b_Mx[hj]], eng="scalar")
                                cp(R_(Nx[hj][:n, nxt, :n]), ps3[:n, 128:128 + n], [bp3], [b_Nx[hj]], eng="vector")
                                ps4, bp4 = (own_ps[h % 4], own_pb[h % 4])
                                mm(ps4[:n, :n], Nx[hj][:n, nxt, :n], Tt[hj][:n, cur, :n], [b_Nx[hj], b_Tt[hj]], [bp4], f32r=use_r)
                                yield
                                vv(R_(Tt[hj][:n, nxt, :n]), ps4[:n, :n], Tt[hj][:n, cur, :n], ALU.add, [bp4, b_Tt[hj]], [b_Tt[hj]])
                                yield
                                cur = nxt
                            if self.stages.get('rwkv_sub', 9) < 3:
                                return
                            TT_ = Tt[hj][:n, cur, :n]
                            ps5, bp5 = (own_ps[h % 4], own_pb[h % 4])
                            mm(ps5[:n, 0:64], Am[hj][:n, 2, :n], vr_h, [b_Am[hj], B["vr"]], [bp5], f32r=use_r)
                            yield
                            cp(R_(akv[hj][:n, :]), ps5[:n, 0:64], [bp5], [b_akv[hj]], eng="scalar")
                            ps6, bp6 = (own_ps[h % 4], own_pb[h % 4])
                            mm(ps6[:64, :n], Tm["ah"][:n, h * 64:(h + 1) * 64], TT_, [B["ah"], b_Tt[hj]], [bp6])
                            cp(R_(wmT[hj][:, :n]), ps6[:64, :n], [bp6], [b_wmT[hj]], eng="vector")
                            if self.stages.get('rwkv_sub', 9) < 4:
                                return
                            yield
                            ps7, bp7 = (own_ps[h % 4], own_pb[h % 4])
                            mm(ps7[:n, 0:64], TT_, akv[hj][:n, :], [b_Tt[hj], b_akv[hj]], [bp7], start=True, stop=False, inc=False, f32r=use_r)
                            mm(ps7[:n, 0:64], wmT[hj][:, :n], STr[:, h, :], [b_wmT[hj], b_STr_all[h]], [bp7], start=False, stop=True, f32r=use_r)
                            yield
                            cp(R_(U[hj][:n, :]), ps7[:n, 0:64], [bp7], [b_U[hj]], eng="scalar")
                            if self.stages.get('rwkv_sub', 9) < 5:
                                return
                            yield
                            ps8, bp8 = (own_ps[h % 4], own_pb[h % 4])
                            mm(ps8[:n, 0:64], rT, STr[:, h, :], [b_fmT[j], b_STr_all[h]], [bp8], start=True, stop=False, inc=False, f32r=use_r)
                            mm(ps8[:n, 0:64], Am[hj][:n, 1, :n], U[hj][:n, :], [b_Am[hj], b_U[hj]], [bp8], start=False, stop=False, inc=False, f32r=use_r)
                            mm(ps8[:n, 0:64], Am[hj][:n, 3, :n], vr_h, [b_Am[hj], B["vr"]], [bp8], start=False, stop=True, f32r=use_r)
                            yield
                            cp(Tm["y"][:n, h * 64:(h + 1) * 64], ps8[:n, 0:64], [bp8], [B["y"]], eng="vector")
                            if self.stages.get('rwkv_sub', 9) < 6:
                                return
                            vs(STd[:, h, :], ST[:, h, :], pc[j][:, h:h + 1], None, ALU.mult, None, [b_STall[h], b_pc[j]], [b_STd_all[h]], eng="gpsimd")
                            ps9, bp9 = (own_ps[h % 4], own_pb[h % 4])
                            mm(ps9[:64, 0:64], Tm["bh"][:n, h * 64:(h + 1) * 64], U[hj][:n, :], [B["bh"], b_U[hj]], [bp9], start=True, stop=False, inc=False)
                            mm(ps9[:64, 0:64], Tm["kh"][:n, h * 64:(h + 1) * 64], v_h, [B["kh"], B["v"]], [bp9], start=False, stop=True)
                            yield
                            stt(ST[:, h, :], ps9[:64, 0:64], pc[j][:, h:h + 1], STd[:, h, :], ALU.mult, ALU.add, [bp9, b_pc[j], b_STd_all[h]], [b_STall[h]])
                            cp(R_(STr[:, h, :]), ST[:, h, :], [b_STall[h]], [b_STr_all[h]], eng="scalar")
                            yield
                        for grp in range(2):
                            gens = [head_gen(h, h % NH) for h in range(grp * 4, grp * 4 + 4)]
                            if not self.stages.get("rwkv_interleave", True):
                                for g_ in gens:
                                    for _ in g_:
                                        pass
                                gens = []
                            while gens:
                                alive = []
                                for g_ in gens:
                                    try:
                                        next(g_)
                                        alive.append(g_)
                                    except StopIteration:
                                        pass
                                gens = alive
                    if self.stages.get('rwkv_upto', 9) < 8:
                        continue
                    y3 = Tm["y"][:n, :].rearrange("p (h k) -> p h k", h=8)
                    S.op("vector", lambda e: e.reduce_sum(out=sm[j][:n, 3, :], in_=y3, axis=AX.X), reads=[B["y"]], writes=[b_sm[j]])
                    vs(sm[j][:n, 3, :], sm[j][:n, 3, :], 1.0 / 64, None, ALU.mult, None, [b_sm[j]], [b_sm[j]])
                    vv(y3, y3, self.bc_last(sm[j][:n, 3, :], 64), ALU.subtract, [B["y"], b_sm[j]], [B["y"]])
                    vv(Tm["t1"][:n, :], Tm["y"][:n, :], Tm["y"][:n, :], ALU.mult, [B["y"]], [B["t1"]], eng="gpsimd")
                    S.op("vector", lambda e: e.reduce_sum(out=sm[j][:n, 4, :], in_=Tm["t1"][:n, :].rearrange("p (h k) -> p h k", h=8), axis=AX.X),
                         reads=[B["t1"]], writes=[b_sm[j]])
                    vs(sm[j][:n, 4, :], sm[j][:n, 4, :], 1.0 / 64, 64e-5, ALU.mult, ALU.add, [b_sm[j]], [b_sm[j]])
                    act(sm[j][:n, 4, :], sm[j][:n, 4, :], AF.Sqrt, [b_sm[j]], [b_sm[j]])
                    S.op("vector", lambda e: e.reciprocal(out=sm[j][:n, 5, :], in_=sm[j][:n, 4, :]), reads=[b_sm[j]], writes=[b_sm[j]])
                    vv(y3, y3, self.bc_last(sm[j][:n, 5, :], 64), ALU.mult, [B["y"], b_sm[j]], [B["y"]])
                    vv(Tm["y"][:n, :], Tm["y"][:n, :], LG_[:n, :], ALU.mult, [B["y"], b_par], [B["y"]])
                    vv(Tm["y"][:n, :], Tm["y"][:n, :], LB_[:n, :], ALU.add, [B["y"], b_par], [B["y"]])
                    vv(Tm["y"][:n, :], Tm["y"][:n, :], Tm["bon"][:n, :], ALU.add, [B["y"], B["bon"]], [B["y"]])
                    vv(yb[j][:n, :], Tm["y"][:n, :], Tm["g"][:n, :], ALU.mult, [B["y"], B["g"]], [b_yb[j]])
                    pst, bpt = self.getps()
                    pstb = pst[:, :].bitcast(BF16)
                    for q in range(4):
                        tr(pstb[:, q * 128:q * 128 + n], yb[j][:n, q * 128:(q + 1) * 128], self.identb[:n, :n], [b_yb[j]], [bpt], inc=(q == 3))
                    cp(ybT[j][:, :, :n], pstb[:, 0:512].rearrange("p (q t) -> p q t", q=4)[:, :, :n], [bpt], [b_ybT[j]], eng="scalar")
                    S.dma("sync", X["yT"][0:RW, t0:t0 + n].rearrange("(q p) t -> p q t", p=128), ybT[j][:, :, :n], reads=[b_ybT[j]])
                for h in range(RH):
                    ps, bp = self.getps()
                    tr(ps[:64, :64], ST[:, h, :], self.identf[:64, :64], [b_STall[h]], [bp])
                    cp(sout[:, h, :], ps[:64, :64], [bp], [b_sout])
                ow = O["wkv_s"] if is_s else O["wkv_p"]
                S.dma("sync", ow[l].rearrange("h v k -> v h k"), sout[:, :, :], reads=[b_sout])
            S.barrier()

    @staticmethod
    def strided(t3, h, start, step, count):
        base = t3[:, h, start:start + 1]
        return bass.AP(tensor=base.tensor, offset=base.offset, ap=[list(base.ap[0]), [step, count]])

    def stage_attn(self, l):
        nc, S = self.nc, self.S
        I, O, X = self.I, self.O, self.X
        vv, vs, stt, act, cp, mm, tr = self.vv, self.vs, self.stt, self.act, self.cp, self.mm, self.tr
        nck = dict(allow_slow_non_contiguous=True)
        SC = float(AD ** -0.5)
        DILS = (1, 4, 16)
        with contextlib.ExitStack() as es:
            sb = lambda n, s, d: es.enter_context(self.sbt(n, s, d))
            self.ps_pool = [es.enter_context(self.pst(f"a_ps{j}", [128, 512], F32)) for j in range(7)]
            self.ps_bufs = [PBuf() for _ in range(7)]
            self.ps_next = 0
            self.ps_dummy = es.enter_context(self.pst("a_psd", [128, 512], F32))
            relb = sb("a_relb", [32, 8], F32); E = sb("a_E", [32, 387], F32)
            Gt = sb("a_Gt", [8, 3, 383], F32)
            Mb = sb("a_Mb", [128, 3, 8, 256], F32)
            b_c = Buf(); b_Gt = Buf(); b_Gd = Buf(); b_Mb = Buf()
            Gd = X["Gd"]
            S.dma("sync", relb[:, :], I["rel_bias"][:, :], writes=[b_c])
            S.dma("sync", E[:, :], I["c_onehot"][:, :], writes=[b_c])
            S.op("vector", lambda e: e.memset(Gt[:], NEG), writes=[b_Gt])
            ps, bp = self.getps()
            mm(ps[:8, 0:387], relb[:, :], E[:, :], [b_c], [bp])
            cp(Gt[:, :, 127:256], ps[:8, 0:387].rearrange("p (g j) -> p g j", g=3), [bp], [b_Gt])
            S.dma("sync", Gd[:, :, :], Gt[:, :, :], reads=[b_Gt], writes=[b_Gd])
            with contextlib.ExitStack() as es3:
                Mr = es3.enter_context(self.sbt("a_Mr", [128, 3, 8, 256], F32)); b_Mr = Buf()
                Jm = es3.enter_context(self.sbt("a_J", [128, 128], F32))
                S.dma("sync", Jm[:, :], I["c_antiident"][:, :], writes=[b_c])
                for g in range(3):
                    src = bass.AP(tensor=Gd.tensor, offset=Gd[0, g, 0].offset, ap=[[1, 128], [3 * 383, 8], [1, 256]])
                    S.dma("sync", Mr[:, g, :, :], src, reads=[b_Gd], writes=[b_Mr])
                for g in range(3):
                    for hp in range(4):
                        ps, bp = self.getps()
                        mm(ps[:, :], Jm[:, :], Mr[:, g, 2 * hp:2 * hp + 2, :].rearrange("p h k -> p (h k)"), [b_c, b_Mr], [bp])
                        cp(Mb[:, g, 2 * hp:2 * hp + 2, :].rearrange("p h k -> p (h k)"), ps[:, :], [bp], [b_Mb], eng=("scalar" if hp % 2 else "vector"))
                S.barrier()

            es2 = contextlib.ExitStack()
            sbp = lambda n, s_, d: es2.enter_context(self.sbt(n, s_, d))
            QT = sbp("a_QT", [128, 8, SEQ], BF16)
            KT = sbp("a_KT", [128, 8, SEQ], BF16)
            Vg = sbp("a_Vg", [128, 16, 1024], BF16)
            b_QT = Buf(); b_KT = Buf(); b_Vg = Buf()
            S.dma("sync", QT[:, :, :], X["qT"][:, 0:SEQ].rearrange("(h d) t -> d h t", d=128), writes=[b_QT])
            S.dma("sync", KT[:, :, :], X["kT"][:, 0:SEQ].rearrange("(h d) t -> d h t", d=128), writes=[b_KT])
            NR = 3
            sc = [sbp(f"a_sc{j}", [128, 2, 256], F32) for j in range(NR)]; b_sc = [Buf() for _ in range(NR)]
            Pm = [sbp(f"a_P{j}", [128, 2, 256], BF16) for j in range(NR)]; b_P = [Buf() for _ in range(NR)]
            PT = [sbp(f"a_PT{j}", [128, 2, 256], BF16) for j in range(NR)]; b_PT = [Buf() for _ in range(NR)]
            Ou = [sbp(f"a_Ou{j}", [128, 1024], F32) for j in range(2)]; b_Ou = [Buf(), Buf()]
            st = [sbp(f"a_st{j}", [128, 5, 8], F32) for j in range(2)]; b_st = [Buf(), Buf()]; b_sth = [[Buf() for _ in range(8)] for _ in range(2)]
            b_Os = Buf(); b_Ls = Buf()
            u = 0
            k = 0
            pend = []
            def flush_pend():
                while pend:
                    it_ = pend.pop(0)
                    if it_[0] is not None:
                        it_[0]()
                    it_[1]()
                    if it_[2] is not None:
                        it_[2]()
            do_prompt = self.stages.get("attn_prompt", True)
            for g, dil in enumerate(DILS):
                if not do_prompt:
                    break
                nb = 16 // dil
                flush_pend()
                vsrc = bass.AP(tensor=O["v_p"].tensor, offset=O["v_p"][l, 0, 0].offset,
                               ap=[[dil * 1024, 128], [1024, dil], [dil * 128 * 1024, nb], [1, 1024]])
                S.dma("gpsimd", Vg[:, :, :].rearrange("p (c n) f -> p c n f", c=dil), vsrc, writes=[b_Vg])
                for c in range(dil):
                    for n_ in range(nb):
                        blk = c * nb + n_
                        q0 = c + dil * 128 * n_
                        nk = 256 if n_ > 0 else 128
                        k0 = q0 - dil * 128 if n_ > 0 else q0
                        koff = 0 if n_ > 0 else 128
                        uj = u % 2
                        u += 1
                        for hp in range(AH // 2):
                            h = hp
                            h0 = 2 * hp
                            kj = k % NR
                            k += 1

                            def fA(uj=uj, kj=kj, hp=hp, h0=h0, g=g, q0=q0, k0=k0, nk=nk, koff=koff, dil=dil):
                                ps_s, bp_s = self.getps()
                                for i in range(2):
                                    mm(ps_s[:, i * 256:i * 256 + nk], self.strided(QT, h0 + i, q0, dil, 128), self.strided(KT, h0 + i, k0, dil, nk), [b_QT, b_KT], [bp_s],
                                       inc=(i == 1))
                                ps3 = ps_s[:, :].rearrange("p (i k) -> p i k", i=2)[:, :, :nk]
                                stt(sc[kj][:, :, :nk], ps3, SC, Mb[:, g, h0:h0 + 2, koff:koff + nk], ALU.mult, ALU.add, [bp_s, b_Mb], [b_sc[kj]])
                                S.op("vector", lambda e: e.reduce_max(out=st[uj][:, 0, h0:h0 + 2], in_=sc[kj][:, :, :nk], axis=AX.X),
                                     reads=[b_sc[kj]], writes=[b_sth[uj][hp]])
                                vs(st[uj][:, 1, h0:h0 + 2], st[uj][:, 0, h0:h0 + 2], -1.0, None, ALU.mult, None, [b_sth[uj][hp]], [b_sth[uj][hp]])
                                for i in range(2):
                                    act(Pm[kj][:, i, :nk], sc[kj][:, i, :nk], AF.Exp, [b_sc[kj], b_sth[uj][hp]], [b_P[kj], b_sth[uj][hp]],
                                        bias=st[uj][:, 1, h0 + i:h0 + i + 1], accum_out=st[uj][:, 2, h0 + i:h0 + i + 1])

                            def fB(kj=kj, nk=nk):
                                pst, bpt = self.getps()
                                pstb = pst[:, :].bitcast(BF16)
                                nkc = nk // 128
                                for i in range(2):
                                    for kc in range(nkc):
                                        tr(pstb[:, i * 256 + kc * 128:i * 256 + (kc + 1) * 128], Pm[kj][:, i, kc * 128:(kc + 1) * 128], self.identb[:, :], [b_P[kj]], [bpt],
                                           inc=(i == 1 and kc == nkc - 1))
                                cp(PT[kj][:, :, :nk], pstb[:, 0:512].rearrange("p (i k) -> p i k", i=2)[:, :, :nk], [bpt], [b_PT[kj]], eng="scalar")

                            def fC(uj=uj, kj=kj, hp=hp, h0=h0, blk=blk, nk=nk):
                                nkc = nk // 128
                                ps_o, bp_o = self.getps()
                                for i in range(2):
                                    for kc in range(nkc):
                                        bk = blk - (nkc - 1 - kc)
                                        mm(ps_o[:, i * 128:(i + 1) * 128], PT[kj][:, i, kc * 128:(kc + 1) * 128], Vg[:, bk, (h0 + i) * 128:(h0 + i + 1) * 128], [b_PT[kj], b_Vg], [bp_o],
                                           start=(kc == 0), stop=(kc == nkc - 1), inc=(i == 1 and kc == nkc - 1))
                                S.op("vector", lambda e: e.reciprocal(out=st[uj][:, 3, h0:h0 + 2], in_=st[uj][:, 2, h0:h0 + 2]), reads=[b_sth[uj][hp]], writes=[b_sth[uj][hp]])
                                vv(Ou[uj][:, h0 * 128:(h0 + 2) * 128].rearrange("p (i d) -> p i d", i=2), ps_o[:, 0:256].rearrange("p (i d) -> p i d", i=2),
                                   self.bc_last(st[uj][:, 3, h0:h0 + 2], 128), ALU.mult, [bp_o, b_sth[uj][hp]], [b_Ou[uj]])

                            fE = None
                            if hp == AH // 2 - 1:
                                def fE(uj=uj, g=g, q0=q0, dil=dil):
                                    act(st[uj][:, 4, :], st[uj][:, 2, :], AF.Ln, b_sth[uj], b_sth[uj])
                                    vv(st[uj][:, 4, :], st[uj][:, 4, :], st[uj][:, 0, :], ALU.add, b_sth[uj], b_sth[uj])
                                    dO = bass.AP(tensor=X["Os"].tensor, offset=X["Os"][g, q0, 0].offset, ap=[[dil * 1024, 128], [1, 1024]])
                                    S.dma("sync", dO, Ou[uj][:, :], reads=[b_Ou[uj]], writes=[b_Os])
                                    dL = bass.AP(tensor=X["Ls"].tensor, offset=X["Ls"][g, q0, 0].offset, ap=[[dil * 8, 128], [1, 8]])
                                    S.dma("sync", dL, st[uj][:, 4, :], reads=b_sth[uj], writes=[b_Ls])
                            fA()
                            pend.append([fB, fC, fE])
                            if len(pend) >= 2 and pend[-2][0] is not None:
                                pend[-2][0](); pend[-2][0] = None
                            if len(pend) >= 3:
                                it_ = pend.pop(0)
                                it_[1]()
                                if it_[2] is not None:
                                    it_[2]()
            flush_pend()
            Om = [sbp(f"a_Om{j}", [128, 3, 1024], F32) for j in range(2)]; b_Om = [Buf(), Buf()]
            Lm = [sbp(f"a_Lm{j}", [128, 6, 8], F32) for j in range(2)]; b_Lm = [Buf(), Buf()]
            wg = [sbp(f"a_wg{j}", [128, 3, 8], F32) for j in range(2)]; b_wg = [Buf(), Buf()]
            ym = [sbp(f"a_ym{j}", [128, 1024], F32) for j in range(2)]; b_ym = [Buf(), Buf()]
            ymb = [sbp(f"a_ymb{j}", [128, 1024], BF16) for j in range(2)]; b_ymb = [Buf(), Buf()]
            yT_ = [sbp(f"a_yT{j}", [128, 8, 128], BF16) for j in range(2)]; b_yT = [Buf(), Buf()]
            def mg_loads(ti):
                j = ti % 2
                t0 = ti * 128
                for g in range(3):
                    S.dma("sync", Om[j][:, g, :], X["Os"][g, t0:t0 + 128, :], reads=[b_Os], writes=[b_Om[j]])
                    S.dma("sync", Lm[j][:, g, 0:8] if False else Lm[j][:, g, :], X["Ls"][g, t0:t0 + 128, :], reads=[b_Ls], writes=[b_Lm[j]])

            n_mt = SEQ // 128 if do_prompt else 0
            if n_mt:
                mg_loads(0)
            for ti in range(n_mt):
                j = ti % 2
                t0 = ti * 128
                if ti + 1 < n_mt:
                    mg_loads(ti + 1)
                L = Lm[j]
                vv(L[:, 3, :], L[:, 0, :], L[:, 1, :], ALU.max, [b_Lm[j]], [b_Lm[j]])
                vv(L[:, 3, :], L[:, 3, :], L[:, 2, :], ALU.max, [b_Lm[j]], [b_Lm[j]])
                for g in range(3):
                    vv(wg[j][:, g, :], L[:, g, :], L[:, 3, :], ALU.subtract, [b_Lm[j]], [b_wg[j]])
                act(wg[j][:, :, :], wg[j][:, :, :], AF.Exp, [b_wg[j]], [b_wg[j]])
                vv(L[:, 4, :], wg[j][:, 0, :], wg[j][:, 1, :], ALU.add, [b_wg[j]], [b_Lm[j]])
                vv(L[:, 4, :], L[:, 4, :], wg[j][:, 2, :], ALU.add, [b_wg[j], b_Lm[j]], [b_Lm[j]])
                S.op("vector", lambda e: e.reciprocal(out=L[:, 5, :], in_=L[:, 4, :]), reads=[b_Lm[j]], writes=[b_Lm[j]])
                for g in range(3):
                    vv(wg[j][:, g, :], wg[j][:, g, :], L[:, 5, :], ALU.mult, [b_wg[j], b_Lm[j]], [b_wg[j]])
                v3 = lambda t: t.rearrange("p (h d) -> p h d", h=8)
                vv(v3(ym[j][:, :]), v3(Om[j][:, 0, :]), self.bc_last(wg[j][:, 0, :], 128), ALU.mult, [b_Om[j], b_wg[j]], [b_ym[j]])
                vv(v3(Om[j][:, 1, :]), v3(Om[j][:, 1, :]), self.bc_last(wg[j][:, 1, :], 128), ALU.mult, [b_Om[j], b_wg[j]], [b_Om[j]], eng="gpsimd")
                vv(v3(Om[j][:, 2, :]), v3(Om[j][:, 2, :]), self.bc_last(wg[j][:, 2, :], 128), ALU.mult, [b_Om[j], b_wg[j]], [b_Om[j]], eng="gpsimd")
                vv(ym[j][:, :], ym[j][:, :], Om[j][:, 1, :], ALU.add, [b_ym[j], b_Om[j]], [b_ym[j]])
                vv(ymb[j][:, :], ym[j][:, :], Om[j][:, 2, :], ALU.add, [b_ym[j], b_Om[j]], [b_ymb[j]])
                pst, bpt = self.getps()
                pstb = pst[:, :].bitcast(BF16)
                for h in range(8):
                    tr(pstb[:, h * 128:(h + 1) * 128], ymb[j][:, h * 128:(h + 1) * 128], self.identb[:, :], [b_ymb[j]], [bpt], inc=(h == 7))
                cp(yT_[j][:, :, :], pstb[:, :].rearrange("p (h t) -> p h t", h=8), [bpt], [b_yT[j]], eng="scalar")
                S.dma("sync", X["yT"][2 * RW:D, t0:t0 + 128].rearrange("(q p) t -> p q t", p=128), yT_[j][:, :, :], reads=[b_yT[j]])

            S.barrier()
            es2.close()
            if self.stages.get("attn_sample", True):
                Kc = sb("s_Kc", [128, 9, 1024], BF16); Vc = sb("s_Vc", [128, 9, 1024], BF16)
                KTs = sb("s_KTs", [128, 9, 8, 128], BF16)
                QsT = sb("s_QsT", [128, 8, 4], BF16); KnT = sb("s_KnT", [128, 8, 4], BF16)
                Vn = sb("s_Vn", [4, 1024], BF16)
                Qm = sb("s_Qm", [128, 8, 4, 4], BF16)
                Sa = sb("s_Sa", [4, 3, 8, 132], F32); Ms = sb("s_Ms", [4, 3, 8, 132], F32)
                Pw = sb("s_Pw", [4, 3, 8, 132], BF16)
                d01 = sb("s_d01", [4, 4], F32); dneg = sb("s_dneg", [4, 4], F32); cmask = sb("s_cmask", [128, 4, 4], BF16)
                sst = sb("s_st", [4, 8, 24], F32)
                PTs = sb("s_PT", [128, 24, 4], BF16); PTm = sb("s_PTm", [128, 4, 24, 4], BF16)
                Pn = sb("s_Pn", [4, 8, 4], F32); Pnb = sb("s_Pnb", [4, 8, 4], BF16); PnT = sb("s_PnT", [4, 8, 4], BF16)
                ysb = sb("s_ysb", [4, 1024], BF16); ysT = sb("s_ysT", [128, 8, 4], BF16)
                b_Kc = Buf(); b_Vc = Buf(); b_KTs = Buf(); b_q = Buf(); b_Vn = Buf(); b_Qm = Buf(); b_Sa = Buf(); b_Ms = Buf()
                b_Pw = Buf(); b_k = Buf(); b_sst = Buf(); b_PTs = Buf(); b_PTm = Buf(); b_Pn = Buf(); b_PnT = Buf(); b_ysb = Buf(); b_ysT = Buf()
                rows = [(1920, 1)] + [(1536 + t, 4) for t in range(4)] + [(t, 16) for t in range(4)]
                for i, (r0, stp) in enumerate(rows):
                    for (dst, src_t, bb) in ((Kc, I["cache_k"], b_Kc), (Vc, I["cache_v"], b_Vc)):
                        src = bass.AP(tensor=src_t.tensor, offset=src_t[l, r0, 0].offset, ap=[[stp * 1024, 128], [1, 1024]])
                        S.dma("gpsimd", dst[:, i, :], src, writes=[bb])
                S.dma("sync", QsT[:, :, :], X["qT"][:, SEQ:SEQ + NS].rearrange("(h d) t -> d h t", d=128), writes=[b_q], **nck)
                S.dma("sync", KnT[:, :, :], X["kT"][:, SEQ:SEQ + NS].rearrange("(h d) t -> d h t", d=128), writes=[b_q], **nck)
                S.dma("gpsimd", Vn[:, :], O["v_s"][l, :, :], writes=[b_Vn])
                S.dma("sync", d01[:, :], I["c_d01"][:, :], writes=[b_k])
                S.dma("sync", dneg[:, :], I["c_dneg"][:, :], writes=[b_k])
                S.dma("gpsimd", cmask[:, :, :], I["c_colmask"][:, :, :], writes=[b_k])
                for i in range(9):
                    pst, bpt = self.getps()
                    pstb = pst[:, :].bitcast(BF16)
                    for h in range(8):
                        tr(pstb[:, h * 128:(h + 1) * 128], Kc[:, i, h * 128:(h + 1) * 128], self.identb[:, :], [b_Kc], [bpt], inc=(h == 7))
                    cp(KTs[:, i, :, :], pstb[:, :].rearrange("p (h t) -> p h t", h=8), [bpt], [b_KTs], eng=("scalar" if i % 2 else "vector"))
                S.op("vector", lambda e: e.memset(Qm[:], 0.0), writes=[b_Qm])
                for t in range(4):
                    cp(Qm[:, :, t, t], QsT[:, :, t], [b_q], [b_Qm])
                for t in range(4):
                    S.dma("sync", Ms[t:t + 1, 0, :, :], bass.AP(tensor=Gd.tensor, offset=Gd[0, 0, 127 - t].offset, ap=[[0, 1], [3 * 383, 8], [1, 132]]),
                          reads=[b_Gd], writes=[b_Ms])
                for g in (1, 2):
                    S.dma("sync", Ms[:, g, :, 0:128], bass.AP(tensor=Gd.tensor, offset=Gd[0, g, 127].offset, ap=[[0, 4], [3 * 383, 8], [1, 128]]),
                          reads=[b_Gd], writes=[b_Ms])
                    for tt in range(4):
                        S.dma("sync", Ms[:, g, :, 128 + tt], bass.AP(tensor=Gd.tensor, offset=Gd[0, g, 255].offset, ap=[[0, 4], [3 * 383, 8]]),
                              reads=[b_Gd], writes=[b_Ms], **nck)
                    blkv = Ms[:, g, :, 128:132]
                    d01b = bass.AP(tensor=d01[:, :].tensor, offset=d01[:, :].offset, ap=[list(d01[:, :].ap[0]), [0, 8], [1, 4]])
                    dngb = bass.AP(tensor=dneg[:, :].tensor, offset=dneg[:, :].offset, ap=[list(dneg[:, :].ap[0]), [0, 8], [1, 4]])
                    vv(blkv, blkv, d01b, ALU.mult, [b_Ms, b_k], [b_Ms])
                    vv(blkv, blkv, dngb, ALU.add, [b_Ms, b_k], [b_Ms])
                for g in range(3):
                    for h in range(8):
                        ps, bp = self.getps()
                        if g == 0:
                            mm(ps[:4, 0:128], QsT[:, h, :], KTs[:, 0, h, :], [b_q, b_KTs], [bp])
                        else:
                            for t in range(4):
                                mm(ps[:4, 0:128], Qm[:, h, t, :], KTs[:, 1 + 4 * (g - 1) + t, h, :], [b_Qm, b_KTs], [bp], start=(t == 0), stop=(t == 3))
                        mm(ps[:4, 128:132], QsT[:, h, :], KnT[:, h, :], [b_q], [bp])
                        stt(Sa[:, g, h, :], ps[:4, 0:132], SC, Ms[:, g, h, :], ALU.mult, ALU.add, [bp, b_Ms], [b_Sa])
                S3 = Sa[:, :, :, :].rearrange("p g h k -> p (g h) k")
                mx, den, lse, rden = sst[:, 0, :], sst[:, 1, :], sst[:, 2, :], sst[:, 3, :]
                S.op("vector", lambda e: e.reduce_max(out=mx, in_=S3, axis=AX.X), reads=[b_Sa], writes=[b_sst])
                vv(S3, S3, self.bc_last(mx, 132), ALU.subtract, [b_Sa, b_sst], [b_Sa])
                act(S3, S3, AF.Exp, [b_Sa], [b_Sa])
                S.op("vector", lambda e: e.reduce_sum(out=den, in_=S3, axis=AX.X), reads=[b_Sa], writes=[b_sst])
                act(lse, den, AF.Ln, [b_sst], [b_sst])
                vv(lse, lse, mx, ALU.add, [b_sst], [b_sst])
                mg, sg = sst[:, 4, 0:8], sst[:, 4, 8:16]
                eg = sst[:, 5, :]
                vv(mg, sst[:, 2, 0:8], sst[:, 2, 8:16], ALU.max, [b_sst], [b_sst])
                vv(mg, mg, sst[:, 2, 16:24], ALU.max, [b_sst], [b_sst])
                for g in range(3):
                    vv(sst[:, 5, g * 8:(g + 1) * 8], sst[:, 2, g * 8:(g + 1) * 8], mg, ALU.subtract, [b_sst], [b_sst])
                act(eg, eg, AF.Exp, [b_sst], [b_sst])
                vv(sg, sst[:, 5, 0:8], sst[:, 5, 8:16], ALU.add, [b_sst], [b_sst])
                vv(sg, sg, sst[:, 5, 16:24], ALU.add, [b_sst], [b_sst])
                S.op("vector", lambda e: e.reciprocal(out=sst[:, 6, 0:8], in_=sg), reads=[b_sst], writes=[b_sst])
                S.op("vector", lambda e: e.reciprocal(out=rden, in_=den), reads=[b_sst], writes=[b_sst])
                for g in range(3):
                    vv(sst[:, 5, g * 8:(g + 1) * 8], sst[:, 5, g * 8:(g + 1) * 8], sst[:, 6, 0:8], ALU.mult, [b_sst], [b_sst])
                vv(eg, eg, rden, ALU.mult, [b_sst], [b_sst])
                vv(Pw[:, :, :, :].rearrange("p g h k -> p (g h) k"), S3, self.bc_last(eg, 132), ALU.mult, [b_Sa, b_sst], [b_Pw])
                pst, bpt = self.getps()
                pstb = pst[:, :].bitcast(BF16)
                for gh in range(24):
                    tr(pstb[:, gh * 4:(gh + 1) * 4], Pw[:, gh // 8, gh % 8, 0:128], self.identb[:4, :4], [b_Pw], [bpt], inc=(gh == 23))
                cp(PTs[:, :, :], pstb[:, 0:96].rearrange("p (a t) -> p a t", t=4), [bpt], [b_PTs])
                for t in range(4):
                    cmb = bass.AP(tensor=cmask[:, t, :].tensor, offset=cmask[:, t, :].offset, ap=[list(cmask[:, t, :].ap[0]), [0, 24], [1, 4]])
                    vv(PTm[:, t, :, :], PTs[:, :, :], cmb, ALU.mult, [b_PTs, b_k], [b_PTm])
                vv(Pn[:, :, :], Pw[:, 0, :, 128:132], Pw[:, 1, :, 128:132], ALU.add, [b_Pw], [b_Pn])
                vv(Pnb[:, :, :], Pn[:, :, :], Pw[:, 2, :, 128:132], ALU.add, [b_Pw, b_Pn], [b_Pn])
                pst2, bpt2 = self.getps()
                pstb2 = pst2[:, :].bitcast(BF16)
                for h in range(8):
                    tr(pstb2[:4, h * 4:(h + 1) * 4], Pnb[:, h, :], self.identb[:4, :4], [b_Pn], [bpt2], inc=(h == 7))
                cp(PnT[:, :, :], pstb2[:4, 0:32].rearrange("p (h t) -> p h t", h=8), [bpt2], [b_PnT])
                for h in range(8):
                    ps, bp = self.getps()
                    hs = slice(h * 128, (h + 1) * 128)
                    mm(ps[:4, 0:128], PTs[:, h, :], Vc[:, 0, hs], [b_PTs, b_Vc], [bp], start=True, stop=False, inc=False)
                    for g in (1, 2):
                        for t in range(4):
                            mm(ps[:4, 0:128], PTm[:, t, g * 8 + h, :], Vc[:, 1 + 4 * (g - 1) + t, hs], [b_PTm, b_Vc], [bp], start=False, stop=False, inc=False)
                    mm(ps[:4, 0:128], PnT[:, h, :], Vn[:, hs], [b_PnT, b_Vn], [bp], start=False, stop=True)
                    cp(ysb[:, hs], ps[:4, 0:128], [bp], [b_ysb], eng=("scalar" if h % 2 else "vector"))
                pst3, bpt3 = self.getps()
                pstb3 = pst3[:, :].bitcast(BF16)
                for h in range(8):
                    tr(pstb3[:, h * 4:(h + 1) * 4], ysb[:, h * 128:(h + 1) * 128], self.identb[:4, :4], [b_ysb], [bpt3], inc=(h == 7))
                cp(ysT[:, :, :], pstb3[:, 0:32].rearrange("p (h t) -> p h t", h=8), [bpt3], [b_ysT])
                S.dma("sync", X["yT"][2 * RW:D, SEQ:SEQ + NS].rearrange("(q p) t -> p q t", p=128), ysT[:, :, :], reads=[b_ysT], **nck)
            S.barrier()

    def rstd_from_mean(self, st, n, R, W):
        self.vs(st[:n, 1:2], st[:n, 0:1], 1e-6, None, ALU.add, None, R, W)
        self.act(st[:n, 0:1], st[:n, 1:2], AF.Sqrt, W, W)
        self.S.op("vector", lambda e: e.reciprocal(out=st[:n, 1:2], in_=st[:n, 0:1]), reads=W, writes=W)

    def stage_wout(self, l, first):
        nc, S = self.nc, self.S
        I, O, X = self.I, self.O, self.X
        vv, vs, stt, act, cp, mm, tr = self.vv, self.vs, self.stt, self.act, self.cp, self.mm, self.tr
        with contextlib.ExitStack() as es:
            sb = lambda n, s, d: es.enter_context(self.sbt(n, s, d))
            self.ps_pool = [es.enter_context(self.pst(f"o_ps{j}", [128, 512], F32)) for j in range(8)]
            self.ps_bufs = [PBuf() for _ in range(8)]
            self.ps_next = 0
            wo = sb("o_wo", [128, 16, D], BF16); b_wo = Buf()
            g1 = sb("o_g1", [128, D], F32); g2 = sb("o_g2", [128, D], F32); b_g = Buf()
            yTt = [sb(f"o_yT{j}", [128, 16, 128], BF16) for j in range(2)]; b_yTt = [Buf(), Buf()]
            xt = [sb(f"o_xt{j}", [128, D], F32) for j in range(2)]; b_xt = [Buf(), Buf()]
            x1 = [sb(f"o_x1{j}", [128, D], F32) for j in range(2)]; b_x1 = [Buf(), Buf()]
            hb = [sb(f"o_hb{j}", [128, D], BF16) for j in range(2)]; b_hb = [Buf(), Buf()]
            hbT = [sb(f"o_hbT{j}", [128, 16, 128], BF16) for j in range(2)]; b_hbT = [Buf(), Buf()]
            junk = sb("o_junk", [128, D], BF16); b_junk = Buf()
            st = [sb(f"o_st{j}", [128, 8], F32) for j in range(2)]; b_st = [Buf(), Buf()]
            for q in range(4):
                S.dma("gpsimd", wo[:, q * 4:(q + 1) * 4, :], I["w_out"][l, q * 512:(q + 1) * 512, :].rearrange("(c p) n -> p c n", p=128), writes=[b_wo])
            S.dma("sync", g1[:, :], dram_bcast(I["norm_mix_post"][l:l + 1, :], 128, D), writes=[b_g])
            S.dma("sync", g2[:, :], dram_bcast(I["norm_ffn_pre"][l:l + 1, :], 128, D), writes=[b_g])
            tl_ = tok_tiles()

            def wo_loads(ti):
                t0, n = tl_[ti]
                j = ti % 2
                S.dma("sync", yTt[j][:, :, :n], X["yT"][:, t0:t0 + n].rearrange("(c p) t -> p c t", p=128), writes=[b_yTt[j]])
                if first:
                    src = I["xp"][t0:t0 + n, :] if t0 < SEQ else I["xs"][:, :]
                else:
                    src = X["xres"][t0:t0 + n, :]
                S.dma("sync", xt[j][:n, :], src, writes=[b_xt[j]])

            wo_loads(0)
            for ti, (t0, n) in enumerate(tl_):
                j = ti % 2
                if ti + 1 < len(tl_):
                    wo_loads(ti + 1)
                pss = []
                for db in range(4):
                    ps, bp = self.getps()
                    pss.append((ps, bp))
                    for c in range(16):
                        mm(ps[:n, :], yTt[j][:, c, :n], wo[:, c, db * 512:(db + 1) * 512], [b_yTt[j], b_wo], [bp], start=(c == 0), stop=(c == 15))
                    act(junk[:n, 0:512], ps[:n, :], AF.Square, [bp], [b_junk, b_st[j]], scale=float(D ** -0.5), accum_out=st[j][:n, 2 + db:3 + db])
                S.op("vector", lambda e: e.reduce_sum(out=st[j][:n, 0:1], in_=st[j][:n, 2:6], axis=AX.X), reads=[b_st[j]], writes=[b_st[j]])
                self.rstd_from_mean(st[j], n, [b_st[j]], [b_st[j]])
                for db in range(4):
                    ps, bp = pss[db]
                    sl = slice(db * 512, (db + 1) * 512)
                    stt(x1[j][:n, sl], ps[:n, :], st[j][:n, 1:2], g1[:n, sl], ALU.mult, ALU.mult, [bp, b_st[j], b_g], [b_x1[j]])
                vv(x1[j][:n, :], x1[j][:n, :], xt[j][:n, :], ALU.add, [b_x1[j], b_xt[j]], [b_x1[j]], eng="gpsimd")
                S.dma("sync", X["x1s"][t0:t0 + n, :], x1[j][:n, :], reads=[b_x1[j]])
                act(junk[:n, :], x1[j][:n, :], AF.Square, [b_x1[j]], [b_junk, b_st[j]], scale=float(D ** -0.5), accum_out=st[j][:n, 0:1])
                self.rstd_from_mean(st[j], n, [b_st[j]], [b_st[j]])
                stt(hb[j][:n, :], x1[j][:n, :], st[j][:n, 1:2], g2[:n, :], ALU.mult, ALU.mult, [b_x1[j], b_st[j], b_g], [b_hb[j]])
                for half in range(2):
                    pst, bpt = self.getps()
                    pstb = pst[:, :].bitcast(BF16)
                    for c in range(8):
                        cc = half * 8 + c
                        tr(pstb[:, c * 128:c * 128 + n], hb[j][:n, cc * 128:(cc + 1) * 128], self.identb[:n, :n], [b_hb[j]], [bpt], inc=(c == 7))
                    cp(hbT[j][:, half * 8:(half + 1) * 8, :n], pstb[:, :].rearrange("p (c t) -> p c t", c=8)[:, :, :n], [bpt], [b_hbT[j]],
                       eng=("scalar" if half == 0 else "vector"))
                S.dma("sync", X["hfT"][:, t0:t0 + n].rearrange("(c p) t -> p c t", p=128), hbT[j][:, :, :n], reads=[b_hbT[j]])
            S.barrier()

    def stage_ffn(self, l, last):
        nc, S = self.nc, self.S
        I, O, X = self.I, self.O, self.X
        vv, vs, stt, act, cp, mm, tr = self.vv, self.vs, self.stt, self.act, self.cp, self.mm, self.tr
        with contextlib.ExitStack() as es:
            sb = lambda n, s, d: es.enter_context(self.sbt(n, s, d))
            self.ps_pool = [es.enter_context(self.pst(f"f_ps{j}", [128, 512], F32)) for j in range(8)]
            self.ps_bufs = [PBuf() for _ in range(8)]
            self.ps_next = 0
            NSB = 1028
            hfs = sb("f_hfs", [128, 16, NSB], BF16); b_hfs = Buf()
            facc = sb("f_facc", [128, 9, D], F32); b_facc = [Buf() for _ in range(9)]
            w1 = [sb(f"f_w1{j}", [128, 16, 512], BF16) for j in range(2)]; b_w1 = [Buf(), Buf()]
            w2 = [sb(f"f_w2{j}", [128, 4, D], BF16) for j in range(2)]; b_w2 = [Buf(), Buf()]
            aT = [sb(f"f_aT{j}", [128, 4, NSB], BF16) for j in range(2)]; b_aT = [Buf(), Buf()]
            rl = [sb(f"f_rl{j}", [128, 512], F32) for j in range(2)]; b_rl = [Buf() for _ in range(2)]
            g3 = sb("f_g3", [128, D], F32); b_g = Buf()
            x1_ = sb("f_x1", [128, D], F32); x1 = [x1_, x1_]; _bx = Buf(); b_x1 = [_bx, _bx]
            junk = sb("f_junk", [128, 512], BF16); b_junk = Buf()
            st = [sb(f"f_st{j}", [128, 8], F32) for j in range(2)]; b_st = [Buf(), Buf()]
            S.dma("sync", g3[:, :], dram_bcast(I["norm_ffn_post"][l:l + 1, :], 128, D), writes=[b_g])
            sbs = [(0, 1024), (1024, TT - 1024)]
            nfb = self.stages.get("ffn_blocks", 16)
            wcnt = 0
            rcnt = 0

            def load_w(fb, j):
                if self.stages.get('dbg_noload') and fb >= 2:
                    return
                src1 = I["ffn_w1"][l, :, fb * 512:(fb + 1) * 512].rearrange("(c p) n -> p c n", p=128)
                S.dma("gpsimd", w1[j][:, 0:8, :], src1[:, 0:8, :], writes=[b_w1[j]])
                S.dma("gpsimd", w1[j][:, 8:16, :], src1[:, 8:16, :], writes=[b_w1[j]])
                src2 = I["ffn_w2"][l, fb * 512:(fb + 1) * 512, :].rearrange("(c p) n -> p c n", p=128)
                S.dma("gpsimd", w2[j][:, :, :], src2, writes=[b_w2[j]])


            def emit_w1(fb, j, tblocks):
                nonlocal rcnt
                for fc in range(4):
                    pss = []
                    for (b0, nb_) in tblocks:
                        pss.append(self.getps())
                    for c in range(16):
                        for bi_, (b0, nb_) in enumerate(tblocks):
                            ps, bp = pss[bi_]
                            mm(ps[:, :nb_], w1[j][:, c, fc * 128:(fc + 1) * 128], hfs[:, c, b0:b0 + nb_], [b_w1[j], b_hfs], [bp], start=(c == 0), stop=(c == 15))
                    for bi_, (b0, nb_) in enumerate(tblocks):
                        ps, bp = pss[bi_]
                        rj = rcnt % 2
                        rcnt += 1
                        act(rl[rj][:, :nb_], ps[:, :nb_], AF.Relu, [bp], [b_rl[rj]])
                        act(aT[j][:, fc, b0:b0 + nb_], rl[rj][:, :nb_], AF.Square, [b_rl[rj]], [b_aT[j]])

            def emit_w2(fb, j, tiles):
                for ti, (c0, n) in enumerate(tiles):
                    for db in range(4):
                        ps, bp = self.getps()
                        for fc in range(4):
                            mm(ps[:n, :], aT[j][:, fc, c0:c0 + n], w2[j][:, fc, db * 512:(db + 1) * 512], [b_aT[j], b_w2[j]], [bp], start=(fc == 0), stop=(fc == 3))
                        sl = slice(db * 512, (db + 1) * 512)
                        if fb == 0:
                            cp(facc[:n, ti, sl], ps[:n, :], [bp], [b_facc[ti]], eng="scalar")
                        else:
                            vv(facc[:n, ti, sl], facc[:n, ti, sl], ps[:n, :], ALU.add, [bp, b_facc[ti]], [b_facc[ti]])

            for (T0, NT) in sbs:
                S.dma("sync", hfs[:, :, :NT], X["hfT"][:, T0:T0 + NT].rearrange("(c p) t -> p c t", p=128), writes=[b_hfs])
                tiles = [(i * 128, min(128, NT - i * 128)) for i in range((NT + 127) // 128)]
                tblocks = [(i * 512, min(512, NT - i * 512)) for i in range((NT + 511) // 512)]
                load_w(0, wcnt % 2)
                jj = wcnt % 2
                wcnt += 1
                if nfb > 1:
                    load_w(1, wcnt % 2)
                emit_w1(0, jj, tblocks)
                for fb in range(nfb):
                    jn = wcnt % 2
                    if fb + 1 < nfb:
                        wcnt += 1
                        emit_w1(fb + 1, jn, tblocks)
                    emit_w2(fb, jj, tiles)
                    if fb + 2 < nfb:
                        load_w(fb + 2, jj)
                    jj = jn
                for ti, (c0, n) in enumerate(tiles):
                    j = ti % 2
                    t0 = T0 + c0
                    S.dma("sync", x1[j][:n, :], X["x1s"][t0:t0 + n, :], writes=[b_x1[j]])
                    for db in range(4):
                        act(junk[:n, :], facc[:n, ti, db * 512:(db + 1) * 512], AF.Square, [b_facc[ti]], [b_junk, b_st[j]], scale=float(D ** -0.5),
                            accum_out=st[j][:n, 2 + db:3 + db])
                    S.op("vector", lambda e: e.reduce_sum(out=st[j][:n, 0:1], in_=st[j][:n, 2:6], axis=AX.X), reads=[b_st[j]], writes=[b_st[j]])
                    self.rstd_from_mean(st[j], n, [b_st[j]], [b_st[j]])
                    stt(facc[:n, ti, :], facc[:n, ti, :], st[j][:n, 1:2], g3[:n, :], ALU.mult, ALU.mult, [b_facc[ti], b_st[j], b_g], [b_facc[ti]])
                    vv(x1[j][:n, :], x1[j][:n, :], facc[:n, ti, :], ALU.add, [b_x1[j], b_facc[ti]], [b_x1[j]])
                    if last:
                        dst = O["yp"][t0:t0 + n, :] if t0 < SEQ else O["ys"][:, :]
                    else:
                        dst = X["xres"][t0:t0 + n, :]
                    S.dma("sync", dst, x1[j][:n, :], reads=[b_x1[j]])
            S.barrier()

    def build(self):
        nc, S = self.nc, self.S
        self.declare()
        self.load_consts()
        for l in range(DEPTH):
            if l >= self.stages.get("layers", DEPTH):
                break
            with contextlib.ExitStack() as es:
                hT = es.enter_context(self.sbt("hT", [128, 16, TT], BF16))
                b_hT = Buf("hT")
                self.stage_norm_T(l, l == 0, hT, b_hT, "norm_mix_pre")
                self.stage_win(l, hT, b_hT)
                S.barrier()
            if self.stages.get('lru', True):
                self.stage_lru(l)
            if self.stages.get('rwkv', True):
                self.stage_rwkv(l)
            if self.stages.get('attn', True):
                self.stage_attn(l)
            if self.stages.get('ffn', True):
                self.stage_wout(l, l == 0)
                self.stage_ffn(l, l == DEPTH - 1)
        S.finish()
        self.es.close()
        return nc


def _t5_onehot():
    E = np.zeros((32, 3 * 129), np.float32)
    for g, dil in enumerate((1, 4, 16)):
        for j in range(129):
            dist = np.int32((128 - j) * dil)
            d = np.float32(max(int(dist), 1))
            large = 16 + int(np.float32(np.log(d / np.float32(16.0))) / np.float32(np.log(2048.0 / 16.0)) * np.float32(16.0))
            b = int(dist) if dist < 16 else min(large, 31)
            E[b, g * 129 + j] = 1.0
    return E


PROMPT_CORES = (0, 1, 4, 5)


def _per_core_inputs(inp, c):
    m = {}
    if c in PROMPT_CORES:
        m["xp"] = np.ascontiguousarray(inp["x_prompt"][PROMPT_CORES.index(c)])
    else:
        m["xp"] = np.zeros((SEQ, D), np.float32)
    m["xs"] = np.ascontiguousarray(inp["x_sample"][c])
    m["st_wkv"] = np.ascontiguousarray(inp["state_rwkv_wkv"][:, c])
    m["st_shift"] = np.ascontiguousarray(inp["state_rwkv_shift"][:, c])
    m["st_lruh"] = np.ascontiguousarray(inp["state_lru_h"][:, c])
    m["st_conv"] = np.ascontiguousarray(inp["state_lru_conv"][:, c])
    m["cache_k"] = np.ascontiguousarray(inp["cache_attn_k"][:, c]).reshape(DEPTH, SEQ, AW)
    m["cache_v"] = np.ascontiguousarray(inp["cache_attn_v"][:, c]).reshape(DEPTH, SEQ, AW)
    for n in ("rel_bias", "norm_mix_pre", "norm_mix_post", "norm_ffn_pre", "norm_ffn_post", "w_in", "w_out",
              "rwkv_mu", "rwkv_w0", "rwkv_w_up", "rwkv_a0", "rwkv_a_up", "rwkv_g_up", "rwkv_k_k", "rwkv_k_a",
              "rwkv_lnx_g", "rwkv_lnx_b", "lru_conv_w", "lru_conv_b", "lru_wa", "lru_ba", "lru_wx", "lru_bx",
              "lru_lambda", "ffn_w1", "ffn_w2"):
        m[n] = inp[n]
    m["rwkv_r_k"] = inp["rwkv_r_k"].reshape(DEPTH, RW)
    m["c_ident"] = np.eye(128, dtype=np.float32)
    m["c_tri"] = np.triu(np.ones((128, 128), np.float32))
    m["c_stri"] = np.triu(np.ones((128, 128), np.float32), 1)
    m["c_ltri"] = np.tril(np.ones((128, 128), np.float32), -1)
    m["c_onehot"] = _t5_onehot()
    m["c_d01"] = np.eye(4, dtype=np.float32)
    m["c_dneg"] = ((1.0 - np.eye(4)) * NEG).astype(np.float32)
    cm = np.zeros((128, 4, 4), np.float32)
    for t in range(4):
        cm[:, t, t] = 1.0
    m["c_colmask"] = cm
    m["c_antiident"] = np.ascontiguousarray(np.eye(128, dtype=np.float32)[::-1])
    return m


def kernel(stages=None, **inp):
    inp = {k: np.asarray(v) for k, v in inp.items()}
    prog = Prog(stages or {})
    nc = prog.build()
    ncores = prog.stages.get("ncores", 8)
    in_maps = [_per_core_inputs(inp, c) for c in range(ncores)]
    res = run_bass_kernel_spmd(nc, in_maps, core_ids=list(range(ncores)))
    R = list(res.results)
    while len(R) < 8:
        R.append(R[0])
    B = 4
    PC = PROMPT_CORES

    def stackp(name, shape_tail):
        return np.stack([R[b][name] for b in range(B)], axis=0)

    def stacks(name):
        return np.stack([R[c][name] for c in range(8)], axis=0)

    y_p = np.stack([R[PC[b]]["yp"] for b in range(B)], 0)
    y_s = stacks("ys")
    wkv_p = np.stack([R[PC[b]]["wkv_p"] for b in range(B)], 1)
    wkv_s = np.stack([R[c]["wkv_s"] for c in range(8)], 1)
    shift_p = np.stack([R[PC[b]]["shift_p"] for b in range(B)], 1)
    shift_s = np.stack([R[c]["shift_s"] for c in range(8)], 1)
    lruh_p = np.stack([R[PC[b]]["lruh_p"] for b in range(B)], 1)
    lruh_s = np.stack([R[c]["lruh_s"] for c in range(8)], 1)
    conv_p = np.stack([R[PC[b]]["conv_p"] for b in range(B)], 1)
    conv_s = np.stack([R[c]["conv_s"] for c in range(8)], 1)
    k_p = np.stack([R[PC[b]]["k_p"] for b in range(B)], 1).reshape(DEPTH, B, SEQ, AH, AD)
    k_s = np.stack([R[c]["k_s"] for c in range(8)], 1).reshape(DEPTH, 8, NS, AH, AD)
    v_p = np.stack([R[PC[b]]["v_p"] for b in range(B)], 1).reshape(DEPTH, B, SEQ, AH, AD)
    v_s = np.stack([R[c]["v_s"] for c in range(8)], 1).reshape(DEPTH, 8, NS, AH, AD)
    outs = (y_p, y_s, wkv_p, wkv_s, shift_p, shift_s, lruh_p, lruh_s, conv_p, conv_s, k_p, k_s, v_p, v_s)
    return tuple(np.ascontiguousarray(o, dtype=np.float32) for o in outs)
```
